# Optimizing a Trainium2 kernel written in Bass

```python
import jax, jax.numpy as jnp
from jax import lax
import numpy as np

D_MODEL = 1024
BATCH = 16
SEQ = 256
DEPTH = 4
DEC_BATCH = 4
DEC_SEQ = 4096
PAST_LEN = 256

GRID_W = 64
HEAD_DIM = 64
GLA_HEADS = 4
GLA_DK = 64
GLA_DV = 64
GLA_GATE_RANK = 16
GLA_TAU = 16.0
GLA_CHUNK = 64
SWA_Q_HEADS = 4
SWA_KV_HEADS = 2
SWA_REP = SWA_Q_HEADS // SWA_KV_HEADS
SWA_WINDOW = 128
SWA_BLOCK = 128
FNET_GROUPS = 4
FNET_GROUP_CH = 64
MLA_HEADS = 4
MLA_Q_RANK = 256
MLA_KV_RANK = 128
MLA_NOPE = 64
MLA_ROPE = 32
MLA_V = 64
D_FF = 2816
FFN_RES = 0.5
N_MOD = 9
ROPE_BASE = 10000.0
EPS = 1e-6
Q_BLOCK = 128
NEG_INF = -1e30

IN_SIZES = (
    GLA_HEADS * GLA_DK, GLA_HEADS * GLA_DK, GLA_HEADS * GLA_DV, GLA_HEADS * GLA_DV,
    GLA_GATE_RANK, GLA_GATE_RANK,
    SWA_Q_HEADS * HEAD_DIM, SWA_KV_HEADS * HEAD_DIM, SWA_KV_HEADS * HEAD_DIM,
    FNET_GROUPS * FNET_GROUP_CH,
    MLA_Q_RANK, MLA_KV_RANK, MLA_ROPE,
)
D_IN = sum(IN_SIZES)
MIX_WIDTH = GLA_HEADS * GLA_DV + SWA_Q_HEADS * HEAD_DIM + FNET_GROUPS * FNET_GROUP_CH + MLA_HEADS * MLA_V

kernel_name = "hybrid_flow_trunk_step"


def _rmsnorm(x, g):
    xf = x.astype(jnp.float32)
    y = xf * lax.rsqrt(jnp.mean(xf * xf, axis=-1, keepdims=True) + EPS)
    return (y * g.astype(jnp.float32)).astype(x.dtype)


def _split_cols(z):
    offs = np.cumsum(IN_SIZES)[:-1].tolist()
    return jnp.split(z, offs, axis=-1)


def _split_heads(z, n_heads):
    b, s, _ = z.shape
    return z.reshape(b, s, n_heads, -1).transpose(0, 2, 1, 3)


def _merge_heads(z):
    b, h, s, d = z.shape
    return z.transpose(0, 2, 1, 3).reshape(b, s, h * d)


def _rope_1d(x, pos):
    dim = x.shape[-1]
    inv = ROPE_BASE ** (-jnp.arange(0, dim, 2, dtype=jnp.float32) / dim)
    ang = pos[:, None] * inv[None, :]
    cos, sin = jnp.cos(ang), jnp.sin(ang)
    x1, x2 = jnp.split(x.astype(jnp.float32), 2, axis=-1)
    return jnp.concatenate([x1 * cos - x2 * sin, x2 * cos + x1 * sin], axis=-1).astype(x.dtype)


def _rope_axial(x):
    t = jnp.arange(x.shape[-2])
    rows = (t // GRID_W).astype(jnp.float32)
    cols = (t % GRID_W).astype(jnp.float32)
    half = x.shape[-1] // 2
    return jnp.concatenate([_rope_1d(x[..., :half], rows), _rope_1d(x[..., half:], cols)], axis=-1)


def _softmax_with_sink(s, sink):
    if sink is None:
        return jax.nn.softmax(s, axis=-1)
    sk = sink.astype(jnp.float32).reshape(sink.shape + (1,) * (s.ndim - 3))[None]
    m = jnp.maximum(jnp.max(s, axis=-1, keepdims=True), sk)
    e = jnp.exp(s - m)
    return e / (jnp.sum(e, axis=-1, keepdims=True) + jnp.exp(sk - m))


def _dense_attention(q, k, v, sink=None):
    b, g, r, sq, d = q.shape
    nq = sq // Q_BLOCK
    scale = d ** -0.5
    qb = jnp.moveaxis(q.reshape(b, g, r, nq, Q_BLOCK, d), 3, 0)

    def one_block(qblk):
        s = jnp.einsum("bgrqd,bgkd->bgrqk", qblk, k).astype(jnp.float32) * scale
        p = _softmax_with_sink(s, sink)
        return jnp.einsum("bgrqk,bgkd->bgrqd", p.astype(v.dtype), v)

    o = lax.map(one_block, qb)
    return jnp.moveaxis(o, 0, 3).reshape(b, g, r, sq, v.shape[-1])


def _window_attention(q, k, v, k_ctx, v_ctx, sink):
    b, g, r, s, d = q.shape
    nb = s // SWA_BLOCK
    kw_len = 3 * SWA_BLOCK

    def bands(t):
        tb = jnp.pad(t, ((0, 0), (0, 0), (SWA_BLOCK, SWA_BLOCK), (0, 0))).reshape(b, g, nb + 2, SWA_BLOCK, t.shape[-1])
        return jnp.concatenate([tb[:, :, :-2], tb[:, :, 1:-1], tb[:, :, 2:]], axis=3)

    kw, vw = bands(k), bands(v)
    qb = q.reshape(b, g, r, nb, SWA_BLOCK, d)
    scale = d ** -0.5
    s_loc = jnp.einsum("bgrnqd,bgnkd->bgrnqk", qb, kw).astype(jnp.float32) * scale
    s_ctx = jnp.einsum("bgrnqd,bgcd->bgrnqc", qb, k_ctx).astype(jnp.float32) * scale
    qpos = jnp.arange(nb)[:, None, None] * SWA_BLOCK + jnp.arange(SWA_BLOCK)[None, :, None]
    kpos = jnp.arange(nb)[:, None, None] * SWA_BLOCK - SWA_BLOCK + jnp.arange(kw_len)[None, None, :]
    valid = (jnp.abs(qpos - kpos) <= SWA_WINDOW) & (kpos >= 0) & (kpos < s)
    s_loc = jnp.where(valid, s_loc, NEG_INF)
    p = _softmax_with_sink(jnp.concatenate([s_loc, s_ctx], axis=-1), sink).astype(v.dtype)
    o = (jnp.einsum("bgrnqk,bgnkd->bgrnqd", p[..., :kw_len], vw)
         + jnp.einsum("bgrnqc,bgcd->bgrnqd", p[..., kw_len:], v_ctx))
    return o.reshape(b, g, r, s, d)


def _gla_scan(q, k, v, log_a, s0):
    b, h, s, _ = q.shape
    dv = v.shape[-1]
    n = s // GLA_CHUNK

    def chunks(t):
        return jnp.moveaxis(t.astype(jnp.float32).reshape(b, h, n, GLA_CHUNK, t.shape[-1]), 2, 0)

    causal = jnp.tril(jnp.ones((GLA_CHUNK, GLA_CHUNK), dtype=bool))

    def step(state, inp):
        qc, kc, vc, ac = inp
        cum = jnp.cumsum(ac, axis=-2)
        last = cum[..., -1:, :]
        q_dec = qc * jnp.exp(cum)
        k_inv = kc * jnp.exp(-cum)
        att = jnp.where(causal, jnp.einsum("bhid,bhjd->bhij", q_dec, k_inv), 0.0)
        o = jnp.einsum("bhij,bhjv->bhiv", att, vc) + jnp.einsum("bhid,bhdv->bhiv", q_dec, state)
        state = (jnp.exp(last)[..., 0, :, None] * state
                 + jnp.einsum("bhjd,bhjv->bhdv", kc * jnp.exp(last - cum), vc))
        return state, o

    state, o = lax.scan(step, s0.astype(jnp.float32), (chunks(q), chunks(k), chunks(v), chunks(log_a)))
    return jnp.moveaxis(o, 0, 2).reshape(b, h, s, dv), state


def _gla_mixer(q, k, v, r, a_f, a_b, lp, s0_f, s0_b):
    qh = _split_heads(q, GLA_HEADS) * (GLA_DK ** -0.5)
    kh = _split_heads(k, GLA_HEADS)
    vh = _split_heads(v, GLA_HEADS)

    def log_decay(a, i):
        logit = (a @ lp["gla_w_gate"][i] + lp["gla_b_gate"][i]).astype(jnp.float32)
        return _split_heads(jax.nn.log_sigmoid(logit) / GLA_TAU, GLA_HEADS)

    def flip(t):
        return jnp.flip(t, axis=2)

    o_f, st_f = _gla_scan(qh, kh, vh, log_decay(a_f, 0), s0_f)
    o_b, st_b = _gla_scan(flip(qh), flip(kh), flip(vh), flip(log_decay(a_b, 1)), s0_b)
    o = _rmsnorm(o_f + flip(o_b), lp["gla_g_out"]).astype(r.dtype)
    return _merge_heads(o) * jax.nn.silu(r), st_f, st_b


def _fourier_mix(z):
    b, s, _ = z.shape
    zg = z.astype(jnp.float32).reshape(b, s, FNET_GROUPS, FNET_GROUP_CH)
    y = jnp.fft.fft2(zg, axes=(1, 3), norm="ortho").real
    return y.reshape(b, s, FNET_GROUPS * FNET_GROUP_CH).astype(z.dtype)


def _mla_expand(c_kv, w_kv_b):
    b, s, _ = c_kv.shape
    kv = (c_kv @ w_kv_b).reshape(b, s, MLA_HEADS, MLA_NOPE + MLA_V).transpose(0, 2, 1, 3)
    return kv[..., :MLA_NOPE], kv[..., MLA_NOPE:]


def _token_mix(h, lp, ctx):
    b, s, _ = h.shape
    latent = ctx is not None
    (q_g, k_g, v_g, r_g, a_f, a_b, q_s, k_s, v_s, z_f, q_a, kv_a, k_r) = _split_cols(h @ lp["w_in"])

    if latent:
        s0_f, s0_b = ctx["gla"][:, 0], ctx["gla"][:, 1]
    else:
        s0_f = jnp.zeros((b, GLA_HEADS, GLA_DK, GLA_DV), jnp.float32)
        s0_b = s0_f
    o_gla, st_f, st_b = _gla_mixer(q_g, k_g, v_g, r_g, a_f, a_b, lp, s0_f, s0_b)

    qs = _split_heads(q_s, SWA_Q_HEADS)
    ks = _split_heads(k_s, SWA_KV_HEADS)
    vs = _split_heads(v_s, SWA_KV_HEADS)
    sink = lp["swa_sink"].reshape(SWA_KV_HEADS, SWA_REP)
    if latent:
        qs = _rope_axial(qs).reshape(b, SWA_KV_HEADS, SWA_REP, s, HEAD_DIM)
        o_swa = _window_attention(qs, _rope_axial(ks), vs, ctx["swa_k"], ctx["swa_v"], sink)
    else:
        o_swa = _dense_attention(qs.reshape(b, SWA_KV_HEADS, SWA_REP, s, HEAD_DIM), ks, vs, sink)
    o_swa = _merge_heads(o_swa.reshape(b, SWA_Q_HEADS, s, HEAD_DIM))

    o_fft = _fourier_mix(z_f)

    c_kv = _rmsnorm(kv_a, lp["mla_g_kv"])
    q_m = (_rmsnorm(q_a, lp["mla_g_q"]) @ lp["mla_w_q_b"]).reshape(b, s, MLA_HEADS, MLA_NOPE + MLA_ROPE).transpose(0, 2, 1, 3)
    k_nope, v_m = _mla_expand(c_kv, lp["mla_w_kv_b"])
    k_rope = k_r[:, None]
    if latent:
        q_m = jnp.concatenate([q_m[..., :MLA_NOPE], _rope_axial(q_m[..., MLA_NOPE:])], axis=-1)
        k_rope = _rope_axial(k_rope)
    k_m = jnp.concatenate([k_nope, jnp.broadcast_to(k_rope, (b, MLA_HEADS, s, MLA_ROPE))], axis=-1)
    if latent:
        c_len = ctx["mla_ckv"].shape[1]
        kc_nope, v_c = _mla_expand(ctx["mla_ckv"], lp["mla_w_kv_b"])
        k_c = jnp.concatenate([kc_nope, jnp.broadcast_to(ctx["mla_kr"][:, None], (b, MLA_HEADS, c_len, MLA_ROPE))], axis=-1)
        k_m = jnp.concatenate([k_c, k_m], axis=2)
        v_m = jnp.concatenate([v_c, v_m], axis=2)
    o_mla = _merge_heads(_dense_attention(q_m[:, :, None], k_m, v_m)[:, :, 0])

    out = jnp.concatenate([o_gla, o_swa, o_fft, o_mla], axis=-1) @ lp["w_out"]
    if latent:
        return out, None
    new_ctx = (jnp.stack([st_f, st_b], axis=1),
               k_s.reshape(b, s, SWA_KV_HEADS, HEAD_DIM),
               v_s.reshape(b, s, SWA_KV_HEADS, HEAD_DIM),
               c_kv, k_r)
    return out, new_ctx


def _swiglu(h, lp, i):
    return (jax.nn.silu(h @ lp["ffn_gate"][i]) * (h @ lp["ffn_up"][i])) @ lp["ffn_down"][i]


def _layer(x, mod, lp, ctx):
    sh1, sc1, gt1, sh2, sc2, gt2, sh3, sc3, gt3 = jnp.split(mod[:, None, :].astype(x.dtype), N_MOD, axis=-1)
    g = lp["g_norm"]
    h = _rmsnorm(x, g[0]) * (1 + sc1) + sh1
    x = x + FFN_RES * gt1 * _rmsnorm(_swiglu(h, lp, 0), g[1])
    h = _rmsnorm(x, g[2]) * (1 + sc2) + sh2
    o, new_ctx = _token_mix(h, lp, ctx)
    x = x + gt2 * _rmsnorm(o, g[3])
    h = _rmsnorm(x, g[4]) * (1 + sc3) + sh3
    x = x + FFN_RES * gt3 * _rmsnorm(_swiglu(h, lp, 1), g[5])
    return x, new_ctx


def setup_inputs(seed: int = 0) -> dict:
    key = jax.random.key(seed)
    ks = jax.random.split(key, 26)
    D = D_MODEL

    def nrm(k, shape, scale):
        return jax.random.normal(k, shape, jnp.float32) * scale

    return {
        "x_prompt": nrm(ks[0], (BATCH, SEQ, D), 1.0),
        "x_sample": nrm(ks[1], (DEC_BATCH, DEC_SEQ, D), 1.0),
        "c": nrm(ks[2], (DEC_BATCH, D), 1.0),
        "state_gla": nrm(ks[3], (DEC_BATCH, DEPTH, 2, GLA_HEADS, GLA_DK, GLA_DV), 1.0),
        "cache_swa_k": nrm(ks[4], (DEC_BATCH, DEPTH, PAST_LEN, SWA_KV_HEADS, HEAD_DIM), 1.0),
        "cache_swa_v": nrm(ks[5], (DEC_BATCH, DEPTH, PAST_LEN, SWA_KV_HEADS, HEAD_DIM), 1.0),
        "cache_mla_ckv": nrm(ks[6], (DEC_BATCH, DEPTH, PAST_LEN, MLA_KV_RANK), 1.0),
        "cache_mla_krope": nrm(ks[7], (DEC_BATCH, DEPTH, PAST_LEN, MLA_ROPE), 1.0),
        "c_ctx": nrm(ks[8], (D,), 1.0),
        "w_mod": nrm(ks[9], (DEPTH, D, N_MOD * D), 0.5 * D ** -0.5),
        "b_mod": nrm(ks[10], (DEPTH, N_MOD * D), 0.02),
        "g_norm": 1.0 + nrm(ks[11], (DEPTH, 6, D), 0.02),
        "w_ffn_gate": nrm(ks[12], (DEPTH, 2, D, D_FF), D ** -0.5),
        "w_ffn_up": nrm(ks[13], (DEPTH, 2, D, D_FF), D ** -0.5),
        "w_ffn_down": nrm(ks[14], (DEPTH, 2, D_FF, D), D_FF ** -0.5),
        "w_in": nrm(ks[15], (DEPTH, D, D_IN), D ** -0.5),
        "gla_w_gate": nrm(ks[16], (DEPTH, 2, GLA_GATE_RANK, GLA_HEADS * GLA_DK), GLA_GATE_RANK ** -0.5),
        "gla_b_gate": nrm(ks[17], (DEPTH, 2, GLA_HEADS * GLA_DK), 0.1),
        "gla_g_out": 1.0 + nrm(ks[18], (DEPTH, GLA_DV), 0.02),
        "swa_sink": nrm(ks[19], (DEPTH, SWA_Q_HEADS), 0.1),
        "mla_g_q": 1.0 + nrm(ks[20], (DEPTH, MLA_Q_RANK), 0.02),
        "mla_g_kv": 1.0 + nrm(ks[21], (DEPTH, MLA_KV_RANK), 0.02),
        "mla_w_q_b": nrm(ks[22], (DEPTH, MLA_Q_RANK, MLA_HEADS * (MLA_NOPE + MLA_ROPE)), MLA_Q_RANK ** -0.5),
        "mla_w_kv_b": nrm(ks[23], (DEPTH, MLA_KV_RANK, MLA_HEADS * (MLA_NOPE + MLA_V)), MLA_KV_RANK ** -0.5),
        "w_out": nrm(ks[24], (DEPTH, MIX_WIDTH, D), MIX_WIDTH ** -0.5),
    }


def reference(x_prompt, x_sample, c, state_gla, cache_swa_k, cache_swa_v, cache_mla_ckv, cache_mla_krope,
              c_ctx, w_mod, b_mod, g_norm, w_ffn_gate, w_ffn_up, w_ffn_down, w_in, gla_w_gate, gla_b_gate,
              gla_g_out, swa_sink, mla_g_q, mla_g_kv, mla_w_q_b, mla_w_kv_b, w_out):
    y_prompt, y_sample = x_prompt, x_sample
    st_gla, st_k, st_v, st_ckv, st_kr = [], [], [], [], []
    for l in range(DEPTH):
        lp = {
            "w_in": w_in[l], "w_out": w_out[l], "g_norm": g_norm[l],
            "ffn_gate": w_ffn_gate[l], "ffn_up": w_ffn_up[l], "ffn_down": w_ffn_down[l],
            "gla_w_gate": gla_w_gate[l], "gla_b_gate": gla_b_gate[l], "gla_g_out": gla_g_out[l],
            "swa_sink": swa_sink[l], "mla_g_q": mla_g_q[l], "mla_g_kv": mla_g_kv[l],
            "mla_w_q_b": mla_w_q_b[l], "mla_w_kv_b": mla_w_kv_b[l],
        }
        mod_ctx = jax.nn.silu(c_ctx)[None, :] @ w_mod[l] + b_mod[l]
        mod_lat = jax.nn.silu(c) @ w_mod[l] + b_mod[l]

        y_prompt, (g_st, k_c, v_c, ckv_c, kr_c) = _layer(y_prompt, mod_ctx, lp, None)
        st_gla.append(g_st)
        st_k.append(k_c)
        st_v.append(v_c)
        st_ckv.append(ckv_c)
        st_kr.append(kr_c)

        ctx = {
            "gla": state_gla[:, l],
            "swa_k": cache_swa_k[:, l].transpose(0, 2, 1, 3),
            "swa_v": cache_swa_v[:, l].transpose(0, 2, 1, 3),
            "mla_ckv": cache_mla_ckv[:, l],
            "mla_kr": cache_mla_krope[:, l],
        }
        y_sample, _ = _layer(y_sample, mod_lat, lp, ctx)

    return (y_prompt, y_sample, jnp.stack(st_gla, axis=1), jnp.stack(st_k, axis=1), jnp.stack(st_v, axis=1),
            jnp.stack(st_ckv, axis=1), jnp.stack(st_kr, axis=1))
```

```python
import os
import numpy as np
import ml_dtypes
DBG = os.environ.get('KDBG', '')
from contextlib import ExitStack
import concourse.bass as bass
import concourse.mybir as mybir
from concourse.bass_utils import run_bass_kernel_spmd

F32 = mybir.dt.float32
BF16 = mybir.dt.bfloat16
AF = mybir.ActivationFunctionType
ALU = mybir.AluOpType
AX = mybir.AxisListType

D = 1024
DFF = 2816
NJ = 22
NL = 4
SS = 4096
SP = 256
NTOK = 4608
EPS = 1e-6
GROUPS = [(0, 1024, 0), (1024, 1024, 0), (2048, 1024, 0), (3072, 1024, 0), (4096, 512, 1)]
FM = [("qg01", 128), ("qg23", 128), ("kg01", 128), ("kg23", 128), ("af", 16), ("ab", 16),
      ("qs0", 64), ("qs1", 64), ("qs2", 64), ("qs3", 64), ("qsp0", 64), ("qsp1", 64), ("qsp2", 64), ("qsp3", 64),
      ("ks0", 64), ("ks1", 64), ("ksp0", 64), ("ksp1", 64), ("zf01", 128), ("zf23", 128), ("kr", 32), ("krp", 32)]
FMI = {n: i for i, (n, _) in enumerate(FM)}
FMOFF = np.cumsum([0] + [s for _, s in FM]).tolist()
NFM = FMOFF[-1]
NTM = 1184
TMCH = [(0, 256), (256, 256), (512, 256), (768, 256), (1024, 160)]
O_QG, O_KG, O_VG, O_RG, O_AF, O_AB, O_QS, O_KS, O_VS, O_ZF, O_QA, O_KVA, O_KR = (
    0, 256, 512, 768, 1024, 1040, 1056, 1312, 1440, 1568, 1824, 2080, 2208)


def _partner(dim):
    half = dim // 2
    q = half // 2
    idx = np.arange(dim)
    blk = idx // half
    w = idx % half
    partner = blk * half + (w + q) % half
    sign = np.where(w < q, -1.0, 1.0)
    return partner, sign


def _rope_tables(dim, S=SS):
    half = dim // 2
    q = half // 2
    t = np.arange(S)
    rows = (t // 64).astype(np.float64)
    cols = (t % 64).astype(np.float64)
    inv = 10000.0 ** (-np.arange(0, half, 2, dtype=np.float64) / half)
    idx = np.arange(dim)
    blk = idx // half
    w = idx % half
    i = w % q
    pos = np.where(blk[:, None] == 0, rows[None, :], cols[None, :])
    ang = pos * inv[i][:, None]
    _, sign = _partner(dim)
    return np.cos(ang).astype(np.float32), (np.sin(ang) * sign[:, None]).astype(np.float32)


class Res:
    __slots__ = ("w", "r")

    def __init__(self):
        self.w = []
        self.r = []


class Q:
    def __init__(self, name, handle, sem):
        self.name = name
        self.h = handle
        self.sem = sem
        self.count = 0
        self.known = {}
        self.dma_pool = []
        self.dma_next = 0


class FW:
    def __init__(self, nc, n_dma_sems=24):
        self.nc = nc
        self.sems = {}
        self._stack = []

        def mk(name):
            cm = nc.semaphore(name)
            self.sems[name] = cm.__enter__()
            self._stack.append(cm)
            return name

        self.pe = Q("pe", nc.tensor, mk("s_pe"))
        self.act = Q("act", nc.scalar, mk("s_act"))
        self.dve = Q("dve", nc.vector, mk("s_dve"))
        self.pool = Q("pool", nc.gpsimd, mk("s_pool"))
        self.sp = Q("sp", nc.sync, mk("s_sp"))
        self.queues = [self.pe, self.act, self.dve, self.pool, self.sp]
        for q in (self.sp, self.pool):
            for i in range(n_dma_sems):
                q.dma_pool.append([mk(f"d_{q.name}{i}"), 0])
        self.n_instr = 0

    def close(self):
        for cm in reversed(self._stack):
            cm.__exit__(None, None, None)

    def _wait(self, q, tok):
        if tok is None:
            return
        key, val = tok
        if q.known.get(key, 0) >= val:
            return
        q.h.wait_ge(self.sems[key], val)
        self.n_instr += 1
        q.known[key] = val

    def deps(self, q, reads, writes, join=False):
        for r in reads:
            for t in r.w:
                self._wait(q, t)
        for w in writes:
            if not join:
                for t in w.w:
                    self._wait(q, t)
            for t in w.r:
                self._wait(q, t)

    def _commit(self, tok, reads, writes, join=False):
        for r in reads:
            r.r.append(tok)
            if len(r.r) > 48:
                best = {}
                for k, v in r.r:
                    if best.get(k, 0) < v:
                        best[k] = v
                r.r = list(best.items())
        for w in writes:
            if join:
                w.w.append(tok)
            else:
                w.w = [tok]
            w.r = []

    def op(self, q, fn, reads=(), writes=()):
        self.deps(q, reads, writes)
        ins = fn()
        q.count += 1
        ins.then_inc(self.sems[q.sem], 1)
        tok = (q.sem, q.count)
        self._commit(tok, reads, writes)
        self.n_instr += 1
        return tok

    def group(self, q, fns, reads=(), writes=()):
        self.deps(q, reads, writes)
        ins = None
        for fn in fns:
            ins = fn()
            self.n_instr += 1
        q.count += 1
        ins.then_inc(self.sems[q.sem], 1)
        if q.name == "pe":
            q.known[q.sem] = q.count
        tok = (q.sem, q.count)
        self._commit(tok, reads, writes)
        return tok

    def dma(self, q, out, in_, reads=(), writes=(), join=False, **kw):
        self.deps(q, reads, writes, join)
        slot = q.dma_pool[q.dma_next]
        q.dma_next = (q.dma_next + 1) % len(q.dma_pool)
        key, val = slot
        if val > 0:
            self._wait(q, (key, val))
        ins = q.h.dma_start(out=out, in_=in_, **kw)
        val += 16
        slot[1] = val
        ins.then_inc(self.sems[key], 16)
        tok = (key, val)
        self._commit(tok, reads, writes, join)
        self.n_instr += 1
        return tok

    def all_tokens(self):
        toks = []
        for q in self.queues:
            if q.count > 0:
                toks.append((q.sem, q.count))
            for key, val in q.dma_pool:
                if val > 0:
                    toks.append((key, val))
        return toks

    def barrier(self):
        toks = self.all_tokens()
        for q in self.queues:
            for t in toks:
                self._wait(q, t)

    def final_wait(self, q):
        for t in self.all_tokens():
            self._wait(q, t)


def build_program(n_layers=NL, do_mix=True, stage=99):
    nc = bass.Bass("TRN2", target_bir_lowering=False)
    fw = FW(nc)
    PE, ACT, DVE, POOL, SPQ = fw.pe, fw.act, fw.dve, fw.pool, fw.sp

    def din(name, shape, dt=F32):
        return nc.dram_tensor(name, list(shape), dt, kind="ExternalInput").ap()

    def dout(name, shape, dt=F32):
        return nc.dram_tensor(name, list(shape), dt, kind="ExternalOutput").ap()

    def dscr(name, shape, dt):
        return nc.dram_tensor(name, list(shape), dt, kind="Internal").ap()

    xin = din("xin", [NTOK, D])
    condT = din("condT", [128, 8, 2])
    st_gla = din("st_gla", [NL, 2, 4, 64, 64])
    c_swa_k = din("c_swa_k", [NL, 256, 128])
    c_swa_v = din("c_swa_v", [NL, 256, 128])
    c_ckv = din("c_ckv", [NL, 256, 128])
    c_kr = din("c_kr", [NL, 256, 32])
    w_mod = din("w_mod", [NL, D, 9 * D])
    b_mod = din("b_mod", [NL, 9 * D])
    bmod_r = din("bmod_r", [128, NL, 9, 8])
    g_norm = din("g_norm", [NL, 6, D])
    gnorm_r = din("gnorm_r", [128, NL, 6, 8])
    w_gate = din("w_gate", [NL, 2, NJ, 128, 8 * 128])
    w_up = din("w_up", [NL, 2, NJ, 128, 8 * 128])
    w_down = din("w_down", [NL, 2, DFF, D])
    w_fm = din("w_fm", [NL, 128, 8 * NFM])
    w_tm = din("w_tm", [NL, D, NTM])
    w_out = din("w_out", [NL, D, D])
    gla_wg = din("gla_wg", [NL, 2, 16, 256])
    gla_bg_r = din("gla_bg_r", [128, NL, 2, 2])
    gla_gout = din("gla_gout", [NL, 64])
    swa_sink = din("swa_sink", [NL, 4])
    mla_gq = din("mla_gq", [NL, 256])
    mla_gkv = din("mla_gkv", [NL, 128])
    mla_wqb = din("mla_wqb", [NL, 256, 4, 96])
    mla_wqbp = din("mla_wqbp", [NL, 256, 4, 96])
    mla_wkvb = din("mla_wkvb", [NL, 128, 512])
    ident_in = din("ident", [128, 128])
    mask_f_in = din("mask_f", [128, 128])
    mask_b_in = din("mask_b", [128, 128])
    cos64 = din("cos64", [64, SS])
    sin64 = din("sin64", [64, SS])
    cos96 = din("cos96", [96, SS])
    sin96 = din("sin96", [96, SS])
    dftc = din("dftc", [32, 128, 32 * 128], BF16)
    dfts = din("dfts", [32, 128, 32 * 128], BF16)
    dftc_p = din("dftc_p", [2, 128, 2 * 128], BF16)
    dfts_p = din("dfts_p", [2, 128, 2 * 128], BF16)
    chc = din("chc", [128, 128])
    chs = din("chs", [128, 128])
    y_out = dout("y_out", [NTOK, D])
    o_gla = dout("o_gla", [2, NL, 2, 4, 64, 64])
    o_k = dout("o_k", [2, NL, 256, 128])
    o_v = dout("o_v", [2, NL, 256, 128])
    o_ckv = dout("o_ckv", [2, NL, 256, 128])
    o_kr = dout("o_kr", [2, NL, 256, 32])
    dbg_mix = dout("dbg_mix", [NTOK, D]) if DBG else None
    xs = dscr("xs", [NTOK, D], F32)
    zT = dscr("zT", [len(FM), 128, NTOK], BF16)
    ztm = dscr("ztm", [NTOK, 1024], BF16)
    mixo = dscr("mixo", [NTOK, D], BF16)
    gates = dscr("gates", [NL, 3, 2, D], F32)
    R_xs = [Res() for _ in GROUPS]
    R_zT = Res()
    R_ztm = Res()
    R_mixo = Res()
    R_gates = Res()

    glob = ExitStack()

    def sb(stack, name, shape, dt):
        return stack.enter_context(nc.sbuf_tensor("t_" + name, list(shape), dt))

    def pm(stack, name, shape, dt):
        return stack.enter_context(nc.psum_tensor("q_" + name, list(shape), dt))

    PT = [pm(glob, f"pt{i}", [128, 1024], BF16) for i in range(2)]
    R_PT = [Res() for _ in range(2)]
    PB = [pm(glob, f"pb{i}", [128, 512], F32) for i in range(6)]
    R_PB = [Res() for _ in range(6)]

    ident = sb(glob, "ident", [128, 128], BF16)
    R_const = Res()
    maskf = sb(glob, "maskf", [128, 128], BF16)
    maskb = sb(glob, "maskb", [128, 128], BF16)
    epsb = sb(glob, "epsb", [128, 1], F32)
    gsT = sb(glob, "gsT", [128, NL, 3, 2, 8], F32)
    shT = sb(glob, "shT", [128, NL, 3, 2, 8], F32)
    R_mod = Res()
    fw.dma(POOL, ident[:], ident_in[:, :], writes=[R_const])
    fw.dma(POOL, maskf[:], mask_f_in[:, :], writes=[R_const], join=True)
    fw.dma(POOL, maskb[:], mask_b_in[:, :], writes=[R_const], join=True)
    R_eps = Res()
    fw.op(DVE, lambda: nc.vector.memset(epsb[:], EPS), writes=[R_eps])

    with ExitStack() as st:
        silc32 = sb(st, "silc32", [128, 8, 2], F32)
        silc = sb(st, "silc", [128, 8, 2], BF16)
        silbc = sb(st, "silbc", [128, 2, 8, 128], BF16)
        bmr = sb(st, "bmr", [128, NL, 9, 8], F32)
        gnr = sb(st, "gnr", [128, NL, 6, 8], F32)
        modraw = sb(st, "modraw", [128, 9, 8, 2], F32)
        wm = [sb(st, f"wm{i}", [128, 8, 1024], BF16) for i in range(2)]
        R_wm = [Res(), Res()]
        brow = [sb(st, f"brow{i}", [128, 1024], F32) for i in range(2)]
        grow = [sb(st, f"grow{i}", [128, 1024], F32) for i in range(2)]
        gt_sb = [sb(st, f"gtsb{i}", [128, 1024], F32) for i in range(2)]
        R_row = [Res(), Res()]
        R_gt = [Res(), Res()]
        R_m = Res()
        R_raw = Res()
        fw.dma(SPQ, silc32[:], condT[:, :, :], writes=[R_m])
        fw.dma(SPQ, bmr[:], bmod_r[:, :, :, :], writes=[R_m], join=True)
        fw.dma(SPQ, gnr[:], gnorm_r[:, :, :, :], writes=[R_m], join=True)
        fw.op(ACT, lambda: nc.scalar.activation(out=silc[:], in_=silc32[:], func=AF.Silu), reads=[R_m], writes=[R_m])
        for c in range(2):
            fw.op(DVE, lambda c=c: nc.vector.tensor_copy(
                silbc[:, c, :, :], silc[:, :, c:c + 1].to_broadcast([128, 8, 128])), reads=[R_m], writes=[R_m])
        cnt = 0
        gcnt = 0
        for l in range(n_layers):
            for v in range(9):
                w_i = cnt % 2
                cnt += 1
                fw.dma(POOL, wm[w_i][:], w_mod[l, :, v * D:(v + 1) * D].rearrange("(k p) c -> p k c", p=128),
                       writes=[R_wm[w_i]])
                i_sub, j_kind = v // 3, v % 3
                if j_kind < 2:
                    ps = PB[0]
                    fns = []
                    for m in range(8):
                        for k in range(8):
                            fns.append(lambda m=m, k=k, w_i=w_i: nc.tensor.matmul(
                                ps[:, m * 2:(m + 1) * 2], lhsT=wm[w_i][:, k, m * 128:(m + 1) * 128],
                                rhs=silc[:, k, :], start=(k == 0), stop=(k == 7)))
                    fw.group(PE, fns, reads=[R_wm[w_i], R_m], writes=[R_PB[0]])
                    dst = shT if j_kind == 0 else gsT
                    psv = ps[:, 0:16].rearrange("p (m c) -> p c m", c=2)
                    fw.op(DVE, lambda dst=dst, psv=psv, l=l, v=v, i_sub=i_sub: nc.vector.tensor_tensor(
                        out=dst[:, l, i_sub, :, :], in0=psv,
                        in1=bmr[:, l, v, :].unsqueeze(1).to_broadcast([128, 2, 8]), op=ALU.add),
                        reads=[R_PB[0], R_m], writes=[R_mod])
                    if j_kind == 1:
                        fw.op(DVE, lambda l=l, i_sub=i_sub: nc.vector.scalar_tensor_tensor(
                            out=gsT[:, l, i_sub, :, :], in0=gsT[:, l, i_sub, :, :], scalar=1.0,
                            in1=gnr[:, l, 2 * i_sub, :].unsqueeze(1).to_broadcast([128, 2, 8]),
                            op0=ALU.add, op1=ALU.mult), reads=[R_mod, R_m], writes=[R_mod])
                else:
                    r_i = gcnt % 2
                    gcnt += 1
                    fw.dma(SPQ, brow[r_i][:], b_mod[l, v * D:(v + 1) * D].partition_broadcast(128), writes=[R_row[r_i]])
                    fw.dma(SPQ, grow[r_i][:], g_norm[l, 2 * i_sub + 1, :].partition_broadcast(128), writes=[R_row[r_i]], join=True)
                    fac = 1.0 if i_sub == 1 else 0.5
                    for c in range(2):
                        for n in range(2):
                            pb = 1 + n
                            fns = [lambda k=k, c=c, n=n, pb=pb, w_i=w_i: nc.tensor.matmul(
                                PB[pb][:, :], lhsT=silbc[:, c, k, :], rhs=wm[w_i][:, k, n * 512:(n + 1) * 512],
                                start=(k == 0), stop=(k == 7)) for k in range(8)]
                            fw.group(PE, fns, reads=[R_wm[w_i], R_m], writes=[R_PB[pb]])
                            fw.op(DVE, lambda c=c, n=n, pb=pb, r_i=r_i: nc.vector.tensor_tensor(
                                out=gt_sb[c][:, n * 512:(n + 1) * 512], in0=PB[pb][:, :],
                                in1=brow[r_i][:, n * 512:(n + 1) * 512], op=ALU.add),
                                reads=[R_PB[pb], R_row[r_i]], writes=[R_gt[c]])
                        fw.op(DVE, lambda c=c, r_i=r_i, fac=fac: nc.vector.scalar_tensor_tensor(
                            out=gt_sb[c][:], in0=gt_sb[c][:], scalar=fac, in1=grow[r_i][:],
                            op0=ALU.mult, op1=ALU.mult), reads=[R_gt[c], R_row[r_i]], writes=[R_gt[c]])
                        fw.dma(SPQ, gates[l, i_sub, c:c + 1, :], gt_sb[c][0:1, :], reads=[R_gt[c]], writes=[R_gates])
        fw.barrier()

    if stage == 0:
        dbg = sb(glob, "dbg", [128, 1024], F32)
        fw.dma(SPQ, dbg[:], gates[0, 0, 0, :].partition_broadcast(128), reads=[R_gates], writes=[R_mod])
        fw.dma(SPQ, y_out[0:128, :], dbg[:], reads=[R_mod])
        fw.dma(SPQ, y_out[128:256, 0:192], gsT[:].rearrange("p a b c d -> p (a b c d)"), reads=[R_mod])
        fw.dma(SPQ, y_out[256:384, 0:192], shT[:].rearrange("p a b c d -> p (a b c d)"), reads=[R_mod])
        fw.final_wait(SPQ)
        glob.close()
        fw.close()
        return nc, fw
    tl = None
    xg = xn = hT = actT = wd = junk = ss = rstd = ss2 = rs2 = None
    wgu = gtg = sg = tmp = zst = zo32 = None
    tlc = [0]

    def alloc_tl():
        nonlocal tl, xg, xn, hT, actT, wd, wgu, gtg, junk, sg, ss, rstd, ss2, rs2, tmp, zst, zo32
        tl = ExitStack()
        tlc[0] += 1
        u = f"_{tlc[0]}"
        xg = sb(tl, "xg" + u, [128, 8, D], F32)
        xn = sb(tl, "xn" + u, [128, 8, D], BF16)
        hT = sb(tl, "hT" + u, [128, 8, 1024], BF16)
        actT = sb(tl, "actT" + u, [128, NJ, 1024], BF16)
        wd = sb(tl, "wd" + u, [128, NJ, D], BF16)
        wgu = [sb(tl, f"wgu{i}" + u, [128, 2, 8 * 128], BF16) for i in range(3)]
        gtg = [sb(tl, f"gtg{i}" + u, [128, D], F32) for i in range(2)]
        junk = sb(tl, "junk" + u, [128, D], BF16)
        sg = [sb(tl, f"sg{i}" + u, [128, 512], F32) for i in range(2)]
        ss = sb(tl, "ss" + u, [128, 8], F32)
        rstd = sb(tl, "rstd" + u, [128, 8], F32)
        ss2 = sb(tl, "ss2" + u, [128, 2, 2], F32)
        rs2 = sb(tl, "rs2" + u, [128, 2], F32)
        tmp = [sb(tl, f"tmp{i}" + u, [128, 512], F32) for i in range(2)]
        zst = [sb(tl, f"zst{i}" + u, [128, 1024], BF16) for i in range(2)]
        zo32 = [sb(tl, f"zo32{i}" + u, [128, 288], F32) for i in range(4)]

    def free_tl():
        fw.barrier()
        tl.close()

    alloc_tl()
    R_xg, R_xn, R_hT, R_actT, R_wd, R_wtm = Res(), Res(), Res(), Res(), Res(), Res()
    R_wgu = [Res() for _ in range(3)]
    R_gtg = [Res(), Res()]
    R_xns = [Res() for _ in range(8)]
    R_hTs = [Res() for _ in range(8)]
    R_sg = [Res(), Res()]
    R_ss, R_rstd = Res(), Res()
    R_ss2 = [Res(), Res()]
    R_tmp = [Res(), Res()]
    R_zst = [Res(), Res()]
    R_zo32 = [Res() for _ in range(4)]
    state = {"wgu": 0, "gtg": 0, "pd": 0, "zst": 0, "ev": 0}

    def evac(out, in_, reads, writes, scale=None):
        state["ev"] += 1
        if False:
            if scale is None:
                fw.op(ACT, lambda: nc.scalar.copy(out, in_), reads=reads, writes=writes)
            else:
                fw.op(ACT, lambda: nc.scalar.mul(out, in_, scale), reads=reads, writes=writes)
        else:
            if scale is None:
                fw.op(DVE, lambda: nc.vector.tensor_copy(out, in_), reads=reads, writes=writes)
            else:
                fw.op(DVE, lambda: nc.vector.tensor_scalar_mul(out, in_, scale), reads=reads, writes=writes)

    def norm_to_hT(T, l, i_sub, c):
        nonlocal transpose_to_hT
        nst = T // 128
        fw.op(DVE, lambda: nc.vector.memset(ss[:], 0.0), writes=[R_ss])
        for s in range(nst):
            fw.op(ACT, lambda s=s: nc.scalar.activation(out=junk[:], in_=xg[:, s, :], func=AF.Square,
                                                        accum_out=ss[:, s:s + 1]), reads=[R_xg], writes=[R_ss])
        fw.op(ACT, lambda: nc.scalar.activation(out=rstd[:, 0:nst], in_=ss[:, 0:nst], func=AF.Sqrt,
                                                scale=1.0 / D, bias=epsb[:, 0:1]), reads=[R_ss, R_eps], writes=[R_rstd])
        fw.op(DVE, lambda: nc.vector.reciprocal(rstd[:, 0:nst], rstd[:, 0:nst]), reads=[R_rstd], writes=[R_rstd])
        for s in range(nst):
            fw.op(DVE, lambda s=s: nc.vector.tensor_scalar_mul(xn[:, s, :], xg[:, s, :], rstd[:, s:s + 1]),
                  reads=[R_xg, R_rstd], writes=[R_xn])
        transpose_to_hT(T, scale_bias=(l, i_sub, c))

    def norm_elem(s, l, i_sub, c):
        fw.op(DVE, lambda: nc.vector.memset(ss[:, s:s + 1], 0.0), writes=[R_ss])
        fw.op(ACT, lambda: nc.scalar.activation(out=junk[:], in_=xg[:, s, :], func=AF.Square, accum_out=ss[:, s:s + 1]),
              reads=[R_xg], writes=[R_ss])
        fw.op(ACT, lambda: nc.scalar.activation(out=rstd[:, s:s + 1], in_=ss[:, s:s + 1], func=AF.Sqrt, scale=1.0 / D,
                                                bias=epsb[:, 0:1]), reads=[R_ss, R_eps], writes=[R_rstd])
        fw.op(DVE, lambda: nc.vector.reciprocal(rstd[:, s:s + 1], rstd[:, s:s + 1]), reads=[R_rstd], writes=[R_rstd])
        fw.op(DVE, lambda: nc.vector.tensor_scalar_mul(xn[:, s, :], xg[:, s, :], rstd[:, s:s + 1]),
              reads=[R_xg, R_rstd], writes=[R_xns[s], R_xn])

    def norm_tr(s, l, i_sub, c):
        p = s % 2
        fns = [lambda k=k: nc.tensor.transpose(PT[p][:, k * 128:(k + 1) * 128], xn[:, s, k * 128:(k + 1) * 128], ident[:])
               for k in range(8)]
        fw.group(PE, fns, reads=[R_xns[s], R_const], writes=[R_PT[p]])
        for k in range(8):
            fw.op(DVE, lambda k=k: nc.vector.tensor_scalar(
                out=hT[:, k, s * 128:(s + 1) * 128], in0=PT[p][:, k * 128:(k + 1) * 128], scalar1=gsT[:, l, i_sub, c, k:k + 1],
                scalar2=shT[:, l, i_sub, c, k:k + 1], op0=ALU.mult, op1=ALU.add),
                reads=[R_PT[p], R_mod], writes=[R_hTs[s]])

    def transpose_to_hT(T, scale_bias=None):
        nst = T // 128
        for k in range(8):
            p = k % 2
            fns = [lambda s=s, k=k, p=p: nc.tensor.transpose(PT[p][:, s * 128:(s + 1) * 128],
                                                           xn[:, s, k * 128:(k + 1) * 128], ident[:])
                   for s in range(nst)]
            fw.group(PE, fns, reads=[R_xn, R_const], writes=[R_PT[p]])
            if scale_bias is not None:
                l, i_sub, c = scale_bias
                fw.op(DVE, lambda k=k, p=p: nc.vector.tensor_scalar(
                    out=hT[:, k, 0:T], in0=PT[p][:, 0:T], scalar1=gsT[:, l, i_sub, c, k:k + 1],
                    scalar2=shT[:, l, i_sub, c, k:k + 1], op0=ALU.mult, op1=ALU.add),
                    reads=[R_PT[p], R_mod], writes=R_hTs[0:nst])
            else:
                fw.op(ACT, lambda k=k, p=p: nc.scalar.copy(hT[:, k, 0:T], PT[p][:, 0:T]),
                      reads=[R_PT[p]], writes=R_hTs[0:nst])

    def load_gtg(l, i_sub, c):
        gi = state["gtg"] % 2
        state["gtg"] += 1
        fw.dma(SPQ, gtg[gi][:], gates[l, i_sub, c, :].partition_broadcast(128), reads=[R_gates], writes=[R_gtg[gi]])
        return gi

    def post_norm_residual(s, pa, pb, gi):
        q = s % 2
        fw.op(DVE, lambda: nc.vector.memset(ss2[:, q, :], 0.0), writes=[R_ss2[q]])
        for n, bank in enumerate((pa, pb)):
            fw.op(ACT, lambda n=n, bank=bank: nc.scalar.activation(
                out=junk[:, 0:512], in_=PB[bank][:, :], func=AF.Square, accum_out=ss2[:, q, n:n + 1]),
                reads=[R_PB[bank]], writes=[R_ss2[q]])
        fw.op(DVE, lambda: nc.vector.tensor_tensor(out=rs2[:, q:q + 1], in0=ss2[:, q, 0:1], in1=ss2[:, q, 1:2],
                                                   op=ALU.add), reads=[R_ss2[q]], writes=[R_ss2[q]])
        fw.op(ACT, lambda: nc.scalar.activation(out=rs2[:, q:q + 1], in_=rs2[:, q:q + 1], func=AF.Sqrt,
                                                scale=1.0 / D, bias=epsb[:, 0:1]), reads=[R_ss2[q], R_eps],
              writes=[R_ss2[q]])
        fw.op(DVE, lambda: nc.vector.reciprocal(rs2[:, q:q + 1], rs2[:, q:q + 1]), reads=[R_ss2[q]], writes=[R_ss2[q]])
        for n, bank in enumerate((pa, pb)):
            fw.op(DVE, lambda n=n, bank=bank: nc.vector.scalar_tensor_tensor(
                out=tmp[n][:], in0=PB[bank][:, :], scalar=rs2[:, q:q + 1], in1=gtg[gi][:, n * 512:(n + 1) * 512],
                op0=ALU.mult, op1=ALU.mult), reads=[R_PB[bank], R_ss2[q], R_gtg[gi]], writes=[R_tmp[n]])
            fw.op(DVE, lambda n=n: nc.vector.tensor_tensor(
                out=xg[:, s, n * 512:(n + 1) * 512], in0=xg[:, s, n * 512:(n + 1) * 512], in1=tmp[n][:], op=ALU.add),
                reads=[R_tmp[n], R_xg], writes=[R_xg])

    def load_wgu(l, i, j):
        wi = state["wgu"] % 3
        state["wgu"] += 1
        fw.dma(POOL, wgu[wi][:, 0, :], w_gate[l, i, j, :, :], writes=[R_wgu[wi]])
        fw.dma(POOL, wgu[wi][:, 1, :], w_up[l, i, j, :, :], writes=[R_wgu[wi]], join=True)
        return wi

    def ffn(T, l, i, c, pre_normed=False, next_norm=None):
        i_sub = 0 if i == 0 else 2
        nst, nh = T // 128, T // 512
        gi = load_gtg(l, i_sub, c)
        pend = [load_wgu(l, i, 0), load_wgu(l, i, 1)]
        if not pre_normed:
            norm_to_hT(T, l, i_sub, c)
        for j in range(NJ):
            wi = pend.pop(0)
            if j + 2 < NJ:
                pend.append(load_wgu(l, i, j + 2))
            fw.dma(POOL, wd[:, j, :], w_down[l, i, j * 128:(j + 1) * 128, :], writes=[R_wd], join=(j > 0))
            for h in range(nh):
                pg, pu = (0, 1) if (j * nh + h) % 2 == 0 else (2, 3)
                wv = wgu[wi][:].rearrange("p g (k c) -> p g k c", k=8)
                fns = [lambda k=k, wv=wv, h=h, pg=pg: nc.tensor.matmul(
                    PB[pg][:, :], lhsT=wv[:, 0, k, :], rhs=hT[:, k, h * 512:(h + 1) * 512],
                    start=(k == 0), stop=(k == 7)) for k in range(8)]
                fw.group(PE, fns, reads=[R_wgu[wi]] + R_hTs[4 * h:4 * h + 4], writes=[R_PB[pg]])
                fns = [lambda k=k, wv=wv, h=h, pu=pu: nc.tensor.matmul(
                    PB[pu][:, :], lhsT=wv[:, 1, k, :], rhs=hT[:, k, h * 512:(h + 1) * 512],
                    start=(k == 0), stop=(k == 7)) for k in range(8)]
                fw.group(PE, fns, reads=[R_wgu[wi]] + R_hTs[4 * h:4 * h + 4], writes=[R_PB[pu]])
                si = (j * nh + h) % 2
                fw.op(ACT, lambda si=si, pg=pg: nc.scalar.activation(out=sg[si][:], in_=PB[pg][:, :], func=AF.Silu),
                      reads=[R_PB[pg]], writes=[R_sg[si]])
                fw.op(DVE, lambda si=si, pu=pu, j=j, h=h: nc.vector.tensor_tensor(
                    out=actT[:, j, h * 512:(h + 1) * 512], in0=sg[si][:], in1=PB[pu][:, :], op=ALU.mult),
                    reads=[R_sg[si], R_PB[pu]], writes=[R_actT])
        for s in range(nst):
            pa, pb = (0, 1) if s % 2 == 0 else (2, 3)
            for n, bank in enumerate((pa, pb)):
                fns = [lambda j=j, s=s, n=n, bank=bank: nc.tensor.matmul(
                    PB[bank][:, :], lhsT=actT[:, j, s * 128:(s + 1) * 128], rhs=wd[:, j, n * 512:(n + 1) * 512],
                    start=(j == 0), stop=(j == NJ - 1)) for j in range(NJ)]
                fw.group(PE, fns, reads=[R_actT, R_wd], writes=[R_PB[bank]])
            post_norm_residual(s, pa, pb, gi)
            if next_norm is not None:
                norm_elem(s, *next_norm)
                if s >= 1:
                    norm_tr(s - 1, *next_norm)
        if next_norm is not None:
            norm_tr(nst - 1, *next_norm)

    def win_phase(T, l, c, t0, gidx, pre_normed=False):
        nst, nh = T // 128, T // 512
        if not pre_normed:
            norm_to_hT(T, l, 1, c)
        for ci, (nm, m) in enumerate(FM):
            if stage == 51 or (stage == 53 and m < 64):
                continue
            wi = state["wgu"] % 3
            state["wgu"] += 1
            off = FMOFF[ci] * 8
            fw.dma(POOL, wgu[wi][:, 0, 0:8 * m], w_fm[l, :, off:off + 8 * m], writes=[R_wgu[wi]])
            wv = wgu[wi][:, 0, 0:8 * m].rearrange("p (k c) -> p k c", k=8)
            zi = state["zst"] % 2
            state["zst"] += 1
            for h in range(nh):
                bank = 4 + (h % 2)
                fns = [lambda k=k, wv=wv, h=h, bank=bank, m=m: nc.tensor.matmul(
                    PB[bank][0:m, :], lhsT=wv[:, k, :], rhs=hT[:, k, h * 512:(h + 1) * 512],
                    start=(k == 0), stop=(k == 7)) for k in range(8)]
                fw.group(PE, fns, reads=[R_wgu[wi]] + R_hTs[4 * h:4 * h + 4], writes=[R_PB[bank]])
                evac(zst[zi][0:m, h * 512:(h + 1) * 512], PB[bank][0:m, :], [R_PB[bank]], [R_zst[zi]],
                     scale=(0.125 if ci < 2 else None))
            fw.dma(SPQ, zT[ci, 0:m, t0:t0 + T], zst[zi][0:m, 0:T], reads=[R_zst[zi]], writes=[R_zT])
        if stage in (52, 53):
            return
        for ni, (o, w) in enumerate(TMCH):
            wi = state["wgu"] % 3
            state["wgu"] += 1
            wflat = wgu[wi][:].rearrange("p g c -> p (g c)")
            wv = wflat.rearrange("p (k c) -> p k c", k=8)
            fw.dma(POOL, wv[:, :, 0:w], w_tm[l, :, o:o + w].rearrange("(k p) c -> p k c", p=128), writes=[R_wgu[wi]])
            for s in range(nst):
                bank = 4 + ((ni * nst + s) % 2)
                fns = [lambda k=k, s=s, w=w, bank=bank, wv=wv: nc.tensor.matmul(
                    PB[bank][:, 0:w], lhsT=hT[:, k, s * 128:(s + 1) * 128], rhs=wv[:, k, 0:w],
                    start=(k == 0), stop=(k == 7)) for k in range(8)]
                fw.group(PE, fns, reads=[R_wgu[wi], R_hTs[s]], writes=[R_PB[bank]])
                if o < 1024:
                    evac(xn[:, s, o:o + w], PB[bank][:, 0:w], [R_PB[bank]], [R_xn])
                if c == 1 and o == 512:
                    fw.op(DVE, lambda bank=bank, s=s: nc.vector.tensor_copy(zo32[s][:, 0:128], PB[bank][:, 0:128]),
                          reads=[R_PB[bank]], writes=[R_zo32[s]])
                if c == 1 and o == 1024:
                    fw.op(DVE, lambda bank=bank, s=s: nc.vector.tensor_copy(zo32[s][:, 128:288], PB[bank][:, 0:160]),
                          reads=[R_PB[bank]], writes=[R_zo32[s]])
        for s in range(nst):
            r0 = t0 + s * 128
            fw.dma(SPQ, ztm[r0:r0 + 128, :], xn[:, s, :], reads=[R_xn], writes=[R_ztm], join=(s > 0))
            if c == 1:
                pi, pr = (s * 128) // SP, (s * 128) % SP
                fw.dma(SPQ, o_v[pi, l, pr:pr + 128, :], zo32[s][:, 0:128], reads=[R_zo32[s]])
                fw.dma(SPQ, o_k[pi, l, pr:pr + 128, :], zo32[s][:, 128:256], reads=[R_zo32[s]])
                fw.dma(SPQ, o_kr[pi, l, pr:pr + 128, :], zo32[s][:, 256:288], reads=[R_zo32[s]])

    def wout_phase(T, l, c, t0, next_norm=None):
        nst = T // 128
        gi = load_gtg(l, 1, c)
        for s in range(nst):
            pa, pb = (0, 1) if s % 2 == 0 else (2, 3)
            for n, bank in enumerate((pa, pb)):
                fns = [lambda k=k, s=s, n=n, bank=bank: nc.tensor.matmul(
                    PB[bank][:, :], lhsT=hT[:, k, s * 128:(s + 1) * 128], rhs=wd[:, k, n * 512:(n + 1) * 512],
                    start=(k == 0), stop=(k == 7)) for k in range(8)]
                fw.group(PE, fns, reads=[R_hTs[s], R_wd], writes=[R_PB[bank]])
            post_norm_residual(s, pa, pb, gi)
            if next_norm is not None:
                norm_elem(s, *next_norm)
                if s >= 1:
                    norm_tr(s - 1, *next_norm)
        if next_norm is not None:
            norm_tr(nst - 1, *next_norm)

    def wout_prefetch(T, l, t0):
        nst = T // 128
        fw.dma(POOL, wd[:, 0:8, :], w_out[l, :, :].rearrange("(k p) c -> p k c", p=128), writes=[R_wd])
        fw.dma(SPQ, xn[:, 0:nst, :], mixo[t0:t0 + T, :].rearrange("(s p) c -> p s c", p=128),
               reads=[R_mixo], writes=[R_xn] + R_xns[0:nst])
        transpose_to_hT(T, None)

    def load_x(src, t0, T, res=None):
        fw.dma(SPQ, xg[:, 0:T // 128, :], src[t0:t0 + T, :].rearrange("(s p) c -> p s c", p=128),
               reads=[res] if res is not None else [], writes=[R_xg])

    def store_x(dst, t0, T, res=None):
        fw.dma(SPQ, dst[t0:t0 + T, :].rearrange("(s p) c -> p s c", p=128), xg[:, 0:T // 128, :],
               reads=[R_xg], writes=[res] if res is not None else [])

    from_mix = {}

    def mix_phase(l):
        if not do_mix:
            return
        _mixers(nc, fw, l, dict(
            zT=zT, ztm=ztm, mixo=mixo, R_zT=R_zT, R_ztm=R_ztm, R_mixo=R_mixo, PB=PB, R_PB=R_PB, PT=PT, R_PT=R_PT,
            ident=ident, maskf=maskf, maskb=maskb, epsb=epsb, R_const=R_const, st_gla=st_gla, c_swa_k=c_swa_k,
            c_swa_v=c_swa_v, c_ckv=c_ckv, c_kr=c_kr, gla_wg=gla_wg, gla_bg_r=gla_bg_r, gla_gout=gla_gout,
            swa_sink=swa_sink, mla_gq=mla_gq, mla_gkv=mla_gkv, mla_wqb=mla_wqb, mla_wqbp=mla_wqbp, mla_wkvb=mla_wkvb,
            cos64=cos64, sin64=sin64, cos96=cos96, sin96=sin96, dftc=dftc, dfts=dfts, dftc_p=dftc_p, dfts_p=dfts_p,
            chc=chc, chs=chs, o_gla=o_gla, o_ckv=o_ckv, sb=sb, dbg_mix=dbg_mix))

    if stage in (1, 2, 3, 4, 5, 6, 20, 21, 51, 52, 53):
        t0, T, c = GROUPS[4] if stage != 4 else GROUPS[0]
        load_x(xin, t0, T)
        if stage == 2:
            norm_to_hT(T, 0, 0, c)
        if stage == 20:
            _saved = transpose_to_hT
            transpose_to_hT = lambda *a, **k: None
            norm_to_hT(T, 0, 0, c)
        if stage == 21:
            fw.op(DVE, lambda: nc.vector.tensor_copy(xn[:, 0:4, :], xg[:, 0:4, :]), reads=[R_xg], writes=[R_xn])
            transpose_to_hT(T, None)
        if stage in (3, 4):
            ffn(T, 0, 0, c)
        if stage in (5, 6, 51, 52, 53):
            win_phase(T, 0, c, t0, 4)
        if stage == 6:
            store_x(xs, t0, T, R_xs[4])
            fw.barrier()
            load_x(xs, t0, T, R_xs[4])
        store_x(y_out, t0, T)
        fw.final_wait(SPQ)
        tl.close()
        glob.close()
        fw.close()
        return nc, fw
    for gidx, (t0, T, c) in enumerate(GROUPS):
        load_x(xin, t0, T)
        ffn(T, 0, 0, c, next_norm=(0, 1, c))
        win_phase(T, 0, c, t0, gidx, pre_normed=True)
        store_x(xs, t0, T, R_xs[gidx])
    for l in range(n_layers):
        free_tl()
        mix_phase(l)
        fw.barrier()
        alloc_tl()
        for gidx, (t0, T, c) in enumerate(GROUPS):
            if do_mix:
                wout_prefetch(T, l, t0)
            load_x(xs, t0, T, R_xs[gidx])
            if do_mix:
                wout_phase(T, l, c, t0)
            if l + 1 < n_layers:
                ffn(T, l, 1, c, next_norm=(l + 1, 0, c))
                ffn(T, l + 1, 0, c, pre_normed=True, next_norm=(l + 1, 1, c))
                win_phase(T, l + 1, c, t0, gidx, pre_normed=True)
                store_x(xs, t0, T, R_xs[gidx])
            else:
                ffn(T, l, 1, c)
                store_x(y_out, t0, T)
    fw.final_wait(SPQ)
    fw.final_wait(ACT)
    tl.close()
    glob.close()
    fw.close()
    return nc, fw


def _mixers(nc, fw, l, E):
    PE, ACT, DVE, POOL, SPQ = fw.pe, fw.act, fw.dve, fw.pool, fw.sp
    sb = E["sb"]
    PB, R_PB, PT, R_PT = E["PB"], E["R_PB"], E["PT"], E["R_PT"]
    zT, ztm, mixo = E["zT"], E["ztm"], E["mixo"]
    R_zT, R_ztm, R_mixo = E["R_zT"], E["R_ztm"], E["R_mixo"]
    ident, maskf, maskb, epsb, R_const = E["ident"], E["maskf"], E["maskb"], E["epsb"], E["R_const"]
    dbg_mix = E["dbg_mix"]
    which = [ch for ch in "fmsg" if ch in DBG] or list("fmsg")
    seqs = [(0, SS, True, 1), (SS, 2 * SP, False, 2)]
    uid = [0]
    cnt = {"s": 0, "o": 0, "pt": 0, "tp": 0}

    def U(n):
        uid[0] += 1
        return f"{n}_L{l}_{uid[0]}"

    def emit_out(ap, R_ap, row0, col0, rows=128, w=256):
        fw.dma(SPQ, mixo[row0:row0 + rows, col0:col0 + w], ap, reads=[R_ap], writes=[R_mixo], join=True)
        if dbg_mix is not None and l == 0:
            fw.dma(POOL, dbg_mix[row0:row0 + rows, col0:col0 + w], ap, reads=[R_ap])

    def load_rows(dst3, src2, R_src, R_dst, nt):
        for i, s0 in enumerate(range(0, nt, 8)):
            n_ = min(8, nt - s0)
            fw.dma(SPQ, dst3[:, s0:s0 + n_, :], src2[s0 * 128:(s0 + n_) * 128, :].rearrange("(s p) c -> p s c", p=128),
                   reads=[R_src], writes=[R_dst], join=(i > 0))

    def transpose_many(dst_fn, src_fn, n, rows_out, reads, R_dst):
        for i0 in range(0, n, 8):
            c_ = min(8, n - i0)
            p = cnt["tp"] % 2
            cnt["tp"] += 1
            fns = [lambda i=i, p=p, i0=i0: nc.tensor.transpose(PT[p][0:rows_out, (i - i0) * 128:(i - i0 + 1) * 128],
                                                                src_fn(i), ident[:]) for i in range(i0, i0 + c_)]
            fw.group(PE, fns, reads=list(reads) + [R_const], writes=[R_PT[p]])
            fw.op(ACT, lambda p=p, i0=i0, c_=c_: nc.scalar.copy(dst_fn(i0, c_), PT[p][0:rows_out, 0:c_ * 128]),
                  reads=[R_PT[p]], writes=[R_dst])

    class AttStream:
        def __init__(self, st, LA=3):
            self.LA = LA
            self.pt = [sb(st, U("pt"), [128, 512], BF16) for _ in range(4)]
            self.R_pt = [Res() for _ in range(4)]
            self.den = sb(st, U("den"), [128, 4], F32)
            self.R_den = [Res() for _ in range(4)]
            self.sbank = [0, 1, 4, 5]
            self.items = []
            self.nblk = 0

        def add_block(self, qT, keys, scale, den_extra, out_ap, R_out, reads, done_cb=None):
            blk = dict(ob=2 + self.nblk % 2, di=self.nblk % 4, den_extra=den_extra, out_ap=out_ap, R_out=R_out,
                       reads=reads, done_cb=done_cb, nb=len(keys), scale=scale, qT=qT)
            self.nblk += 1
            for b0 in range(0, len(keys), 4):
                self.items.append((blk, b0, keys[b0:b0 + 4]))

        def _qk(self, i):
            blk, b0, batch = self.items[i]
            slot = i % 4
            sbank = self.sbank[slot]
            qT, scale = blk["qT"], blk["scale"]
            fns = [lambda j=j, kT=kT: nc.tensor.matmul(PB[sbank][:, j * 128:(j + 1) * 128], lhsT=kT, rhs=qT,
                                                       start=True, stop=True) for j, (kT, _, _) in enumerate(batch)]
            fw.group(PE, fns, reads=blk["reads"], writes=[R_PB[sbank]])
            wv_ = len(batch) * 128
            fw.op(ACT, lambda: nc.scalar.activation(out=self.pt[slot][:, 0:wv_], in_=PB[sbank][:, 0:wv_], func=AF.Exp,
                                                    scale=scale), reads=[R_PB[sbank]], writes=[self.R_pt[slot]])
            for j, (_, _, mk) in enumerate(batch):
                if mk is not None:
                    fw.op(POOL, lambda j=j, mk=mk: nc.gpsimd.tensor_tensor(
                        out=self.pt[slot][:, j * 128:(j + 1) * 128], in0=self.pt[slot][:, j * 128:(j + 1) * 128], in1=mk[:],
                        op=ALU.mult), reads=[self.R_pt[slot], R_const], writes=[self.R_pt[slot]])

        def _pv(self, i):
            blk, b0, batch = self.items[i]
            slot = i % 4
            ob, nb, di = blk["ob"], blk["nb"], blk["di"]
            fns = [lambda j=j, vx=vx: nc.tensor.matmul(PB[ob][:, 0:65], lhsT=self.pt[slot][:, j * 128:(j + 1) * 128], rhs=vx,
                                                       start=(b0 + j == 0), stop=(b0 + j == nb - 1))
                   for j, (_, vx, _) in enumerate(batch)]
            fw.group(PE, fns, reads=list(blk["reads"]) + [self.R_pt[slot]], writes=[R_PB[ob]])
            if b0 + len(batch) == nb:
                den, R_den = self.den, self.R_den
                if blk["den_extra"] is not None:
                    fw.op(DVE, lambda: nc.vector.tensor_tensor(out=den[:, di:di + 1], in0=PB[ob][:, 64:65], in1=blk["den_extra"],
                                                               op=ALU.add), reads=list(blk["reads"]) + [R_PB[ob]], writes=[R_den[di]])
                else:
                    fw.op(DVE, lambda: nc.vector.tensor_copy(den[:, di:di + 1], PB[ob][:, 64:65]), reads=[R_PB[ob]],
                          writes=[R_den[di]])
                fw.op(DVE, lambda: nc.vector.reciprocal(den[:, di:di + 1], den[:, di:di + 1]), reads=[R_den[di]], writes=[R_den[di]])
                fw.op(DVE, lambda: nc.vector.tensor_scalar_mul(blk["out_ap"], PB[ob][:, 0:64], den[:, di:di + 1]),
                      reads=[R_PB[ob], R_den[di]], writes=[blk["R_out"]])
                if blk["done_cb"] is not None:
                    blk["done_cb"]()

        def run(self):
            n = len(self.items)
            for i in range(n + self.LA):
                if i < n:
                    self._qk(i)
                if i - self.LA >= 0:
                    self._pv(i - self.LA)
            self.items = []

    def rope_rows(st, dst_fn, R_dst, rows, na, npm, cosT, sinT, row_off, t0, S):
        CH = 1024
        ra = [sb(st, U("ra"), [rows, CH], BF16) for _ in range(2)]
        rp = [sb(st, U("rp"), [rows, CH], BF16) for _ in range(2)]
        tc_ = [sb(st, U("tc"), [rows, CH], F32) for _ in range(2)]
        ts_ = [sb(st, U("ts"), [rows, CH], F32) for _ in range(2)]
        R_in = [Res(), Res()]
        R_t = [Res(), Res()]
        for i, c0 in enumerate(range(0, S, CH)):
            b = i % 2
            fw.dma(SPQ, ra[b][:], zT[FMI[na], 0:rows, t0 + c0:t0 + c0 + CH], reads=[R_zT], writes=[R_in[b]])
            fw.dma(SPQ, rp[b][:], zT[FMI[npm], 0:rows, t0 + c0:t0 + c0 + CH], reads=[R_zT], writes=[R_in[b]], join=True)
            fw.dma(SPQ, tc_[b][:], cosT[row_off:row_off + rows, c0:c0 + CH], writes=[R_t[b]])
            fw.dma(SPQ, ts_[b][:], sinT[row_off:row_off + rows, c0:c0 + CH], writes=[R_t[b]], join=True)
            fw.op(DVE, lambda b=b: nc.vector.tensor_tensor(out=tc_[b][:], in0=tc_[b][:], in1=ra[b][:], op=ALU.mult),
                  reads=[R_in[b], R_t[b]], writes=[R_t[b]])
            fw.op(POOL, lambda b=b: nc.gpsimd.tensor_tensor(out=ts_[b][:], in0=ts_[b][:], in1=rp[b][:], op=ALU.mult),
                  reads=[R_in[b], R_t[b]], writes=[R_t[b]])
            fw.op(DVE, lambda b=b, c0=c0: nc.vector.tensor_tensor(out=dst_fn(c0, CH), in0=tc_[b][:], in1=ts_[b][:], op=ALU.add),
                  reads=[R_t[b]], writes=[R_dst])

    def rope_multi(st, items, rows, cosT, sinT, row_off, t0, S):
        CH = 1024
        NB = 3
        tc_ = [sb(st, U("mtc"), [rows, CH], F32) for _ in range(2)]
        ts_ = [sb(st, U("mts"), [rows, CH], F32) for _ in range(2)]
        ra = [sb(st, U("mra"), [rows, CH], BF16) for _ in range(NB)]
        rp = [sb(st, U("mrp"), [rows, CH], BF16) for _ in range(NB)]
        t1 = [sb(st, U("mt1"), [rows, CH], F32) for _ in range(NB)]
        t2 = [sb(st, U("mt2"), [rows, CH], F32) for _ in range(NB)]
        R_tab = [Res(), Res()]
        R_in = [Res() for _ in range(NB)]
        R_t1 = [Res() for _ in range(NB)]
        R_t2 = [Res() for _ in range(NB)]
        n = 0
        for ci, c0 in enumerate(range(0, S, CH)):
            tb = ci % 2
            fw.dma(SPQ, tc_[tb][:], cosT[row_off:row_off + rows, c0:c0 + CH], writes=[R_tab[tb]])
            fw.dma(SPQ, ts_[tb][:], sinT[row_off:row_off + rows, c0:c0 + CH], writes=[R_tab[tb]], join=True)
            for (na, npm, dst_fn, R_d) in items:
                b = n % NB
                n += 1
                fw.dma(SPQ, ra[b][:], zT[FMI[na], 0:rows, t0 + c0:t0 + c0 + CH], reads=[R_zT], writes=[R_in[b]])
                fw.dma(SPQ, rp[b][:], zT[FMI[npm], 0:rows, t0 + c0:t0 + c0 + CH], reads=[R_zT], writes=[R_in[b]], join=True)
                fw.op(DVE, lambda b=b, tb=tb: nc.vector.tensor_tensor(out=t1[b][:], in0=tc_[tb][:], in1=ra[b][:], op=ALU.mult),
                      reads=[R_in[b], R_tab[tb]], writes=[R_t1[b]])
                fw.op(POOL, lambda b=b, tb=tb: nc.gpsimd.tensor_tensor(out=t2[b][:], in0=ts_[tb][:], in1=rp[b][:], op=ALU.mult),
                      reads=[R_in[b], R_tab[tb]], writes=[R_t2[b]])
                fw.op(DVE, lambda b=b, c0=c0, dst_fn=dst_fn: nc.vector.tensor_tensor(
                    out=dst_fn(c0, CH), in0=t1[b][:], in1=t2[b][:], op=ALU.add), reads=[R_t1[b], R_t2[b]], writes=[R_d])

    def fnet(t0, S, sample, nseq):
        nt = S // 128
        ntq = nt // nseq
        with ExitStack() as st:
            zf = sb(st, U("zf"), [128, 2, S], BF16)
            R_zf = Res()
            for q, nm in enumerate(("zf01", "zf23")):
                fw.dma(SPQ, zf[:, q, :], zT[FMI[nm], :, t0:t0 + S], reads=[R_zT], writes=[R_zf], join=(q > 0))
            ch = sb(st, U("ch"), [128, 2, 128], BF16)
            R_ch = Res()
            fw.dma(POOL, ch[:, 0, :], E["chc"][:, :], writes=[R_ch])
            fw.dma(POOL, ch[:, 1, :], E["chs"][:, :], writes=[R_ch], join=True)
            zcs = sb(st, U("zcs"), [128, nt, 512], BF16)
            R_zcs = Res()
            for t in range(nt):
                bank = 4 + t % 2
                fns = []
                for cs in range(2):
                    for q in range(2):
                        fns.append(lambda cs=cs, q=q, t=t, bank=bank: nc.tensor.matmul(
                            PB[bank][:, cs * 256 + q * 128: cs * 256 + (q + 1) * 128], lhsT=zf[:, q, t * 128:(t + 1) * 128],
                            rhs=ch[:, cs, :], start=True, stop=True))
                fw.group(PE, fns, reads=[R_zf, R_ch], writes=[R_PB[bank]])
                fw.op(DVE, lambda t=t, bank=bank: nc.vector.tensor_copy(zcs[:, t, :], PB[bank][:, :]),
                      reads=[R_PB[bank]], writes=[R_zcs])
            tabc, tabs = (E["dftc"], E["dfts"]) if sample else (E["dftc_p"], E["dfts_p"])
            NTB = 5 if sample else 2
            tb = [sb(st, U("tb"), [128, 2, ntq * 128], BF16) for _ in range(NTB)]
            R_tb = [Res() for _ in range(NTB)]
            ost = [sb(st, U("fo"), [128, 256], BF16) for _ in range(2)]
            R_ost = [Res(), Res()]
            def ld_tab(m):
                tbi = m % NTB
                fw.dma(SPQ, tb[tbi][:, 0, :], tabc[m % ntq, :, :], writes=[R_tb[tbi]])
                fw.dma(SPQ, tb[tbi][:, 1, :], tabs[m % ntq, :, :], writes=[R_tb[tbi]], join=True)
            for m in range(min(NTB - 1, nt)):
                ld_tab(m)
            for m in range(nt):
                if m + NTB - 1 < nt:
                    ld_tab(m + NTB - 1)
                b = m % 2
                tbi = m % NTB
                bank = 4 + m % 2
                fns = []
                n_mm = 2 * ntq
                kb = (m // ntq) * ntq
                for cs in range(2):
                    for k in range(ntq):
                        idx = cs * ntq + k
                        fns.append(lambda cs=cs, k=k, tbi=tbi, bank=bank, idx=idx, kb=kb: nc.tensor.matmul(
                            PB[bank][:, 0:256], lhsT=tb[tbi][:, cs, k * 128:(k + 1) * 128],
                            rhs=zcs[:, kb + k, cs * 256:(cs + 1) * 256], start=(idx == 0), stop=(idx == n_mm - 1)))
                fw.group(PE, fns, reads=[R_tb[tbi], R_zcs], writes=[R_PB[bank]])
                fw.op(DVE, lambda b=b, bank=bank: nc.vector.tensor_copy(ost[b][:], PB[bank][:, 0:256]),
                      reads=[R_PB[bank]], writes=[R_ost[b]])
                emit_out(ost[b][:], R_ost[b], t0 + m * 128, 512)
            fw.barrier()

    def swa(t0, S, sample, nseq):
        nt = S // 128
        ntq = nt // nseq
        with ExitStack() as st:
            q_sb = sb(st, U("sq"), [64, 4, S], BF16)
            k_sb = sb(st, U("sk"), [64, 2, S], BF16)
            R_q, R_k = Res(), Res()
            if sample:
                with ExitStack() as st2:
                    items = [(f"qs{h}", f"qsp{h}", (lambda c0, n_, h=h: q_sb[:, h, c0:c0 + n_]), R_q) for h in range(4)]
                    items += [(f"ks{h}", f"ksp{h}", (lambda c0, n_, h=h: k_sb[:, h, c0:c0 + n_]), R_k) for h in range(2)]
                    rope_multi(st2, items, 64, E["cos64"], E["sin64"], 0, t0, S)
                    fw.barrier()
            else:
                for h in range(4):
                    fw.dma(SPQ, q_sb[:, h, :], zT[FMI[f"qs{h}"], 0:64, t0:t0 + S], reads=[R_zT], writes=[R_q], join=(h > 0))
                for h in range(2):
                    fw.dma(SPQ, k_sb[:, h, :], zT[FMI[f"ks{h}"], 0:64, t0:t0 + S], reads=[R_zT], writes=[R_k], join=(h > 0))
            vraw = sb(st, U("vr"), [128, nt, 128], BF16)
            vx = sb(st, U("vx"), [128, nt, 2, 65], BF16)
            R_vr, R_vx = Res(), Res()
            load_rows(vraw, ztm[t0:t0 + S, 512:640], R_ztm, R_vr, nt)
            fw.op(DVE, lambda: nc.vector.memset(vx[:], 1.0), writes=[R_vx])
            fw.op(DVE, lambda: nc.vector.tensor_copy(vx[:, :, :, 0:64], vraw[:].rearrange("p s (h d) -> p s h d", h=2)),
                  reads=[R_vr], writes=[R_vx])
            rd = [R_q, R_k, R_vx]
            if sample:
                kc_tm = sb(st, U("kctm"), [128, 2, 128], BF16)
                vc_tm = sb(st, U("vctm"), [128, 2, 128], BF16)
                kcT = sb(st, U("kcT"), [64, 2, 256], BF16)
                vcx = sb(st, U("vcx"), [128, 2, 2, 65], BF16)
                R_kc, R_vc, R_kcT, R_vcx = Res(), Res(), Res(), Res()
                fw.dma(POOL, kc_tm[:], E["c_swa_k"][l, :, :].rearrange("(s p) c -> p s c", p=128), writes=[R_kc])
                fw.dma(POOL, vc_tm[:], E["c_swa_v"][l, :, :].rearrange("(s p) c -> p s c", p=128), writes=[R_vc])
                for kvh in range(2):
                    transpose_many(lambda i0, c_, kvh=kvh: kcT[:, kvh, i0 * 128:(i0 + c_) * 128],
                                   lambda i, kvh=kvh: kc_tm[:, i, kvh * 64:(kvh + 1) * 64], 2, 64, [R_kc], R_kcT)
                fw.op(DVE, lambda: nc.vector.memset(vcx[:], 1.0), writes=[R_vcx])
                fw.op(DVE, lambda: nc.vector.tensor_copy(vcx[:, :, :, 0:64], vc_tm[:].rearrange("p s (h d) -> p s h d", h=2)),
                      reads=[R_vc], writes=[R_vcx])
                rd += [R_kcT, R_vcx]
            snk = sb(st, U("snk"), [128, 4], F32)
            R_snk = Res()
            fw.dma(SPQ, snk[:], E["swa_sink"][l, :].partition_broadcast(128), writes=[R_snk])
            fw.op(ACT, lambda: nc.scalar.activation(out=snk[:], in_=snk[:], func=AF.Exp), reads=[R_snk], writes=[R_snk])
            rd.append(R_snk)
            stream = AttStream(st)
            ost = [sb(st, U("so"), [128, 256], BF16) for _ in range(2)]
            R_ost = [Res(), Res()]
            for n in range(nt):
                b = n % 2
                for h in range(4):
                    kvh = h // 2
                    keys = []
                    if sample:
                        keys += [(kcT[:, kvh, 0:128], vcx[:, 0, kvh, :], None), (kcT[:, kvh, 128:256], vcx[:, 1, kvh, :], None)]
                        for dn, mk in ((-1, maskb), (0, None), (1, maskf)):
                            nb_ = n + dn
                            if 0 <= nb_ < nt:
                                keys.append((k_sb[:, kvh, nb_ * 128:(nb_ + 1) * 128], vx[:, nb_, kvh, :], mk))
                    else:
                        keys = [(k_sb[:, kvh, j * 128:(j + 1) * 128], vx[:, j, kvh, :], None)
                                for j in range((n // ntq) * ntq, (n // ntq + 1) * ntq)]
                    cb = (lambda b=b, n=n: emit_out(ost[b][:], R_ost[b], t0 + n * 128, 256)) if h == 3 else None
                    stream.add_block(q_sb[:, h, n * 128:(n + 1) * 128], keys, 0.125, snk[:, h:h + 1],
                                     ost[b][:, h * 64:(h + 1) * 64], R_ost[b], rd, cb)
            stream.run()
            fw.barrier()

    def mla(t0, S, sample, nseq):
        nt = S // 128
        ntq = nt // nseq
        nctx = 256 if sample else 0
        NK = nctx + S
        nkt = NK // 128
        nct = nctx // 128
        with ExitStack() as st:
            KT = sb(st, U("KT"), [96, 4, NK], BF16)
            QT = sb(st, U("QT"), [96, 4, S], BF16)
            vmx = sb(st, U("vmx"), [128, nkt, 4, 65], BF16)
            R_KT, R_QT, R_vmx = Res(), Res(), Res()
            with ExitStack() as st2:
                gkv = sb(st2, U("gkv"), [128, 128], F32)
                R_g = Res()
                fw.dma(SPQ, gkv[:], E["mla_gkv"][l, :].partition_broadcast(128), writes=[R_g])
                kva = sb(st2, U("kva"), [128, nt, 128], BF16)
                R_kva = Res()
                load_rows(kva, ztm[t0:t0 + S, 896:1024], R_ztm, R_kva, nt)
                sq = sb(st2, U("sqk"), [128, nt, 128], F32)
                ssk = sb(st2, U("ssk"), [128, nt], F32)
                R_sq, R_ss = Res(), Res()
                fw.op(POOL, lambda: nc.gpsimd.tensor_tensor(out=sq[:], in0=kva[:], in1=kva[:], op=ALU.mult), reads=[R_kva], writes=[R_sq])
                fw.op(DVE, lambda: nc.vector.reduce_sum(out=ssk[:], in_=sq[:], axis=AX.X), reads=[R_sq], writes=[R_ss])
                fw.op(ACT, lambda: nc.scalar.activation(out=ssk[:], in_=ssk[:], func=AF.Sqrt, scale=1.0 / 128, bias=epsb[:, 0:1]),
                      reads=[R_ss], writes=[R_ss])
                fw.op(DVE, lambda: nc.vector.reciprocal(ssk[:], ssk[:]), reads=[R_ss], writes=[R_ss])
                fw.op(POOL, lambda: nc.gpsimd.tensor_tensor(out=sq[:], in0=kva[:], in1=ssk[:].unsqueeze(2).to_broadcast([128, nt, 128]),
                                                            op=ALU.mult), reads=[R_kva, R_ss, R_sq], writes=[R_sq])
                fw.op(DVE, lambda: nc.vector.tensor_tensor(out=sq[:], in0=sq[:], in1=gkv[:].unsqueeze(1).to_broadcast([128, nt, 128]),
                                                           op=ALU.mult), reads=[R_sq, R_g], writes=[R_sq])
                if not sample:
                    for pi in range(nseq):
                        fw.dma(SPQ, E["o_ckv"][pi, l, :, :].rearrange("(s p) c -> p s c", p=128), sq[:, pi * ntq:(pi + 1) * ntq, :],
                               reads=[R_sq])
                ckvb = sb(st2, U("ckvb"), [128, nkt, 128], BF16)
                R_cb = Res()
                if sample:
                    fw.dma(POOL, ckvb[:, 0:2, :], E["c_ckv"][l, :, :].rearrange("(s p) c -> p s c", p=128), writes=[R_cb])
                fw.op(DVE, lambda: nc.vector.tensor_copy(ckvb[:, nct:nkt, :], sq[:]), reads=[R_sq], writes=[R_cb])
                ckvT = sb(st2, U("ckvT"), [128, NK], BF16)
                R_cT = Res()
                transpose_many(lambda i0, c_: ckvT[:, i0 * 128:(i0 + c_) * 128], lambda i: ckvb[:, i, :], nkt, 128, [R_cb], R_cT)
                wkv = sb(st2, U("wkv"), [128, 512], BF16)
                R_wkv = Res()
                fw.dma(POOL, wkv[:], E["mla_wkvb"][l, :, :], writes=[R_wkv])
                ci = 0
                for h in range(4):
                    for c0 in range(0, NK, 512):
                        n_ = min(512, NK - c0)
                        bank = 4 + ci % 2
                        ci += 1
                        fw.group(PE, [lambda h=h, c0=c0, n_=n_, bank=bank: nc.tensor.matmul(
                            PB[bank][0:64, 0:n_], lhsT=wkv[:, h * 128:h * 128 + 64], rhs=ckvT[:, c0:c0 + n_], start=True, stop=True)],
                            reads=[R_wkv, R_cT], writes=[R_PB[bank]])
                        fw.op(DVE, lambda h=h, c0=c0, n_=n_, bank=bank: nc.vector.tensor_copy(KT[0:64, h, c0:c0 + n_], PB[bank][0:64, 0:n_]),
                              reads=[R_PB[bank]], writes=[R_KT])
                fw.op(DVE, lambda: nc.vector.memset(vmx[:], 1.0), writes=[R_vmx])
                for kt in range(nkt):
                    bank = 4 + ci % 2
                    ci += 1
                    fw.group(PE, [lambda kt=kt, bank=bank: nc.tensor.matmul(
                        PB[bank][:, :], lhsT=ckvT[:, kt * 128:(kt + 1) * 128], rhs=wkv[:, :], start=True, stop=True)],
                        reads=[R_wkv, R_cT], writes=[R_PB[bank]])
                    fw.op(DVE, lambda kt=kt, bank=bank: nc.vector.tensor_copy(
                        vmx[:, kt, :, 0:64], PB[bank][:, :].rearrange("p (h c) -> p h c", h=4)[:, :, 64:128]),
                        reads=[R_PB[bank]], writes=[R_vmx])
                krs = sb(st2, U("krs"), [32, NK], BF16)
                R_krs = Res()
                if sample:
                    kr_tm = sb(st2, U("krtm"), [128, 2, 32], BF16)
                    R_krt = Res()
                    fw.dma(POOL, kr_tm[:], E["c_kr"][l, :, :].rearrange("(s p) c -> p s c", p=128), writes=[R_krt])
                    transpose_many(lambda i0, c_: krs[:, i0 * 128:(i0 + c_) * 128], lambda i: kr_tm[:, i, :], 2, 32, [R_krt], R_krs)
                    rope_rows(st2, lambda c0, n_: krs[:, nctx + c0:nctx + c0 + n_], R_krs, 32, "kr", "krp",
                              E["cos96"], E["sin96"], 64, t0, S)
                else:
                    fw.dma(SPQ, krs[:], zT[FMI["kr"], 0:32, t0:t0 + S], reads=[R_zT], writes=[R_krs])
                for h in range(4):
                    fw.dma(SPQ, KT[64:96, h, :], krs[:], reads=[R_krs], writes=[R_KT], join=True)
                fw.barrier()
            with ExitStack() as st2:
                gq = sb(st2, U("gq"), [128, 256], F32)
                R_g = Res()
                fw.dma(SPQ, gq[:], E["mla_gq"][l, :].partition_broadcast(128), writes=[R_g])
                qa = sb(st2, U("qa"), [128, nt, 256], BF16)
                R_qa = Res()
                load_rows(qa, ztm[t0:t0 + S, 640:896], R_ztm, R_qa, nt)
                sq = sb(st2, U("sqq"), [128, nt, 256], F32)
                ssq = sb(st2, U("ssq"), [128, nt], F32)
                R_sq, R_ss = Res(), Res()
                fw.op(POOL, lambda: nc.gpsimd.tensor_tensor(out=sq[:], in0=qa[:], in1=qa[:], op=ALU.mult), reads=[R_qa], writes=[R_sq])
                fw.op(DVE, lambda: nc.vector.reduce_sum(out=ssq[:], in_=sq[:], axis=AX.X), reads=[R_sq], writes=[R_ss])
                fw.op(ACT, lambda: nc.scalar.activation(out=ssq[:], in_=ssq[:], func=AF.Sqrt, scale=1.0 / 256, bias=epsb[:, 0:1]),
                      reads=[R_ss], writes=[R_ss])
                fw.op(DVE, lambda: nc.vector.reciprocal(ssq[:], ssq[:]), reads=[R_ss], writes=[R_ss])
                fw.op(POOL, lambda: nc.gpsimd.tensor_tensor(out=sq[:], in0=qa[:], in1=ssq[:].unsqueeze(2).to_broadcast([128, nt, 256]),
                                                            op=ALU.mult), reads=[R_qa, R_ss, R_sq], writes=[R_sq])
                fw.op(DVE, lambda: nc.vector.tensor_tensor(out=qa[:], in0=sq[:], in1=gq[:].unsqueeze(1).to_broadcast([128, nt, 256]),
                                                           op=ALU.mult), reads=[R_sq, R_g], writes=[R_qa])
                qnT = sb(st2, U("qnT"), [128, 2, S], BF16)
                R_qnT = Res()
                for k in range(2):
                    transpose_many(lambda i0, c_, k=k: qnT[:, k, i0 * 128:(i0 + c_) * 128],
                                   lambda i, k=k: qa[:, i, k * 128:(k + 1) * 128], nt, 128, [R_qa], R_qnT)
                wq = sb(st2, U("wq"), [128, 2, 4, 96], BF16)
                wqp = sb(st2, U("wqp"), [128, 2, 4, 96], BF16)
                R_wq = Res()
                fw.dma(POOL, wq[:], E["mla_wqb"][l, :, :, :].rearrange("(k p) h c -> p k h c", p=128), writes=[R_wq])
                fw.dma(POOL, wqp[:], E["mla_wqbp"][l, :, :, :].rearrange("(k p) h c -> p k h c", p=128), writes=[R_wq], join=True)
                tcs = [sb(st2, U("tcq"), [96, 512], F32) for _ in range(2)]
                tsn = [sb(st2, U("tsq"), [96, 512], F32) for _ in range(2)]
                t1q = [sb(st2, U("t1q"), [96, 512], F32) for _ in range(2)]
                t2q = [sb(st2, U("t2q"), [96, 512], F32) for _ in range(2)]
                R_t = [Res(), Res()]
                R_t1 = [Res(), Res()]
                R_t2 = [Res(), Res()]
                ci = 0
                for cix, c0 in enumerate(range(0, S, 512)):
                    n_ = min(512, S - c0)
                    tb = cix % 2
                    if sample:
                        fw.dma(SPQ, tcs[tb][:, 0:n_], E["cos96"][:, c0:c0 + n_], writes=[R_t[tb]])
                        fw.dma(SPQ, tsn[tb][:, 0:n_], E["sin96"][:, c0:c0 + n_], writes=[R_t[tb]], join=True)
                    for h in range(4):
                        b = ci % 2
                        ci += 1
                        fw.group(PE, [lambda k=k, h=h, c0=c0, n_=n_: nc.tensor.matmul(
                            PB[4][0:96, 0:n_], lhsT=wq[:, k, h, :], rhs=qnT[:, k, c0:c0 + n_], start=(k == 0), stop=(k == 1))
                            for k in range(2)], reads=[R_wq, R_qnT], writes=[R_PB[4]])
                        if sample:
                            fw.group(PE, [lambda k=k, h=h, c0=c0, n_=n_: nc.tensor.matmul(
                                PB[5][0:96, 0:n_], lhsT=wqp[:, k, h, :], rhs=qnT[:, k, c0:c0 + n_], start=(k == 0), stop=(k == 1))
                                for k in range(2)], reads=[R_wq, R_qnT], writes=[R_PB[5]])
                            fw.op(DVE, lambda b=b, n_=n_, tb=tb: nc.vector.tensor_tensor(
                                out=t1q[b][:, 0:n_], in0=tcs[tb][:, 0:n_], in1=PB[4][0:96, 0:n_], op=ALU.mult),
                                reads=[R_t[tb], R_PB[4]], writes=[R_t1[b]])
                            fw.op(DVE, lambda b=b, n_=n_, tb=tb: nc.vector.tensor_tensor(
                                out=t2q[b][:, 0:n_], in0=tsn[tb][:, 0:n_], in1=PB[5][0:96, 0:n_], op=ALU.mult),
                                reads=[R_t[tb], R_PB[5]], writes=[R_t2[b]])
                            fw.op(POOL, lambda b=b, n_=n_, h=h, c0=c0: nc.gpsimd.tensor_tensor(
                                out=QT[:, h, c0:c0 + n_], in0=t1q[b][:, 0:n_], in1=t2q[b][:, 0:n_], op=ALU.add),
                                reads=[R_t1[b], R_t2[b]], writes=[R_QT])
                        else:
                            fw.op(DVE, lambda h=h, c0=c0, n_=n_: nc.vector.tensor_copy(QT[:, h, c0:c0 + n_], PB[4][0:96, 0:n_]),
                                  reads=[R_PB[4]], writes=[R_QT])
                fw.barrier()
            stream = AttStream(st)
            ost = [sb(st, U("mo"), [128, 256], BF16) for _ in range(2)]
            R_ost = [Res(), Res()]
            rd = [R_KT, R_QT, R_vmx]
            sc = 96.0 ** -0.5
            for n in range(nt):
                b = n % 2
                for h in range(4):
                    krange = range(nkt) if sample else range((n // ntq) * ntq, (n // ntq + 1) * ntq)
                    keys = [(KT[:, h, kt * 128:(kt + 1) * 128], vmx[:, kt, h, :], None) for kt in krange]
                    cb = (lambda b=b, n=n: emit_out(ost[b][:], R_ost[b], t0 + n * 128, 768)) if h == 3 else None
                    stream.add_block(QT[:, h, n * 128:(n + 1) * 128], keys, sc, None, ost[b][:, h * 64:(h + 1) * 64],
                                     R_ost[b], rd, cb)
            stream.run()
            fw.barrier()

    def gla(t0, S, sample, nseq):
        nt = S // 128
        ntq = nt // nseq
        with ExitStack() as st:
            oacc = sb(st, U("oacc"), [128, nt, 256], F32)
            R_oacc = Res()
            with ExitStack() as st2:
                rmask = sb(st2, U("rmask"), [128, S], F32)
                R_rm = Res()
                fw.op(DVE, lambda: nc.vector.memset(rmask[:], 1.0), writes=[R_rm])
                fw.op(DVE, lambda: nc.vector.memset(rmask[:].rearrange("p (c t) -> p c t", t=128)[:, :, 0:1], 0.0), writes=[R_rm])
                bg = sb(st2, U("bg"), [128, 2, 2], F32)
                R_bg = Res()
                fw.dma(SPQ, bg[:], E["gla_bg_r"][:, l, :, :], writes=[R_bg])
                qT = sb(st2, U("gq"), [128, S], BF16)
                kT = sb(st2, U("gk"), [128, S], BF16)
                v = sb(st2, U("gv"), [128, nt, 256], BF16)
                aT = sb(st2, U("ga"), [16, S], BF16)
                wg = sb(st2, U("gwg"), [16, 256], BF16)
                L = sb(st2, U("gL"), [128, S], F32)
                cum = sb(st2, U("gcum"), [128, S], F32)
                Ee = sb(st2, U("gE"), [128, S], F32)
                qd = [sb(st2, U("gqd"), [128, S], BF16) for _ in range(2)]
                ki = [sb(st2, U("gki"), [128, S], BF16) for _ in range(2)]
                kie_tm = [sb(st2, U("gkt"), [128, S], BF16) for _ in range(2)]
                elast = sb(st2, U("gel"), [128, 2, nt], F32)
                S32 = sb(st2, U("gS"), [128, 2, 128], F32)
                Sbf = [sb(st2, U("gSb"), [128, 2, 128], BF16) for _ in range(2)]
                att_sb = [sb(st2, U("gat"), [128, 2, 128], BF16) for _ in range(4)]
                R_q, R_k, R_v, R_a, R_wg, R_L, R_cum, R_E, R_el, R_S = [Res() for _ in range(10)]
                R_qd, R_ki, R_kt = [Res(), Res()], [Res(), Res()], [Res(), Res()]
                R_Sb = [Res(), Res()]
                R_att = [Res() for _ in range(4)]
                R_ab = [Res() for _ in range(4)]
                R_o = [Res(), Res()]
                R_p = [Res(), Res()]
                abank = [0, 1, 4, 5]
                load_rows(v, ztm[t0:t0 + S, 0:256], R_ztm, R_v, nt)
                for dr in range(2):
                    fw.dma(SPQ, aT[:], zT[FMI["af" if dr == 0 else "ab"], 0:16, t0:t0 + S], reads=[R_zT], writes=[R_a])
                    fw.dma(POOL, wg[:], E["gla_wg"][l, dr, :, :], writes=[R_wg])
                    for hp in range(2):
                        fw.dma(SPQ, qT[:], zT[FMI["qg01" if hp == 0 else "qg23"], :, t0:t0 + S], reads=[R_zT], writes=[R_q])
                        fw.dma(SPQ, kT[:], zT[FMI["kg01" if hp == 0 else "kg23"], :, t0:t0 + S], reads=[R_zT], writes=[R_k])
                        BW = 1024 if S >= 1024 else S
                        blocks = [(c0, BW) for c0 in range(0, S, BW)]
                        R_Lb = [Res() for _ in blocks]
                        R_cmb = [Res() for _ in blocks]
                        R_Eb = [Res() for _ in blocks]
                        cum3 = cum[:].rearrange("p (c t) -> p c t", t=128)
                        L3 = L[:].rearrange("p (c t) -> p c t", t=128)
                        mi = 0
                        for bi, (c0, bw) in enumerate(blocks):
                            for cc in range(c0, c0 + bw, 512):
                                n_ = min(512, c0 + bw - cc)
                                bank = 4 + mi % 2
                                mi += 1
                                fw.group(PE, [lambda cc=cc, n_=n_, bank=bank, hp=hp: nc.tensor.matmul(
                                    PB[bank][:, 0:n_], lhsT=wg[:, hp * 128:(hp + 1) * 128], rhs=aT[:, cc:cc + n_], start=True, stop=True)],
                                    reads=[R_wg, R_a], writes=[R_PB[bank]])
                                fw.op(DVE, lambda cc=cc, n_=n_, bank=bank, dr=dr, hp=hp: nc.vector.tensor_scalar(
                                    out=L[:, cc:cc + n_], in0=PB[bank][:, 0:n_], scalar1=bg[:, dr, hp:hp + 1], scalar2=-1.0,
                                    op0=ALU.add, op1=ALU.mult), reads=[R_PB[bank], R_bg], writes=[R_Lb[bi]])
                        for bi, (c0, bw) in enumerate(blocks):
                            fw.op(ACT, lambda c0=c0, bw=bw: nc.scalar.activation(out=L[:, c0:c0 + bw], in_=L[:, c0:c0 + bw], func=AF.Exp),
                                  reads=[R_Lb[bi]], writes=[R_Lb[bi]])
                        for bi, (c0, bw) in enumerate(blocks):
                            fw.op(DVE, lambda c0=c0, bw=bw: nc.vector.tensor_scalar_add(L[:, c0:c0 + bw], L[:, c0:c0 + bw], 1.0),
                                  reads=[R_Lb[bi]], writes=[R_Lb[bi]])
                        for bi, (c0, bw) in enumerate(blocks):
                            fw.op(ACT, lambda c0=c0, bw=bw: nc.scalar.activation(out=L[:, c0:c0 + bw], in_=L[:, c0:c0 + bw], func=AF.Ln),
                                  reads=[R_Lb[bi]], writes=[R_Lb[bi]])
                        for bi, (c0, bw) in enumerate(blocks):
                            fw.op(DVE, lambda c0=c0, bw=bw: nc.vector.tensor_tensor_scan(
                                out=cum[:, c0:c0 + bw], data0=rmask[:, c0:c0 + bw], data1=L[:, c0:c0 + bw], initial=0.0,
                                op0=ALU.mult, op1=ALU.add), reads=[R_Lb[bi], R_rm], writes=[R_cmb[bi]])
                        for bi, (c0, bw) in enumerate(blocks):
                            k0, k1 = c0 // 128, (c0 + bw) // 128
                            fw.op(ACT, lambda k0=k0, k1=k1, hp=hp: nc.scalar.activation(
                                out=elast[:, hp, k0:k1], in_=cum3[:, k0:k1, 127], func=AF.Exp, scale=-1.0 / 16),
                                reads=[R_cmb[bi]], writes=[R_el])
                        if dr == 0:
                            csrc, R_cs = cum, R_cmb
                        else:
                            for bi, (c0, bw) in enumerate(blocks):
                                k0, k1 = c0 // 128, (c0 + bw) // 128
                                fw.op(DVE, lambda c0=c0, bw=bw: nc.vector.tensor_tensor(
                                    out=L[:, c0:c0 + bw], in0=L[:, c0:c0 + bw], in1=cum[:, c0:c0 + bw], op=ALU.subtract),
                                    reads=[R_Lb[bi], R_cmb[bi]], writes=[R_Lb[bi]])
                                fw.op(DVE, lambda k0=k0, k1=k1: nc.vector.tensor_tensor(
                                    out=L3[:, k0:k1, :], in0=L3[:, k0:k1, :],
                                    in1=cum3[:, k0:k1, 127:128].to_broadcast([128, k1 - k0, 128]), op=ALU.add),
                                    reads=[R_Lb[bi], R_cmb[bi]], writes=[R_Lb[bi]])
                            csrc, R_cs = L, R_Lb
                        for bi, (c0, bw) in enumerate(blocks):
                            fw.op(ACT, lambda c0=c0, bw=bw, csrc=csrc: nc.scalar.activation(
                                out=Ee[:, c0:c0 + bw], in_=csrc[:, c0:c0 + bw], func=AF.Exp, scale=-1.0 / 16),
                                reads=[R_cs[bi]], writes=[R_Eb[bi]])
                        for bi, (c0, bw) in enumerate(blocks):
                            fw.op(POOL, lambda c0=c0, bw=bw, hp=hp: nc.gpsimd.tensor_tensor(
                                out=qd[hp][:, c0:c0 + bw], in0=qT[:, c0:c0 + bw], in1=Ee[:, c0:c0 + bw], op=ALU.mult),
                                reads=[R_q, R_Eb[bi]], writes=[R_qd[hp]])
                        for bi, (c0, bw) in enumerate(blocks):
                            fw.op(ACT, lambda c0=c0, bw=bw, csrc=csrc: nc.scalar.activation(
                                out=Ee[:, c0:c0 + bw], in_=csrc[:, c0:c0 + bw], func=AF.Exp, scale=1.0 / 16),
                                reads=[R_cs[bi]], writes=[R_Eb[bi]])
                        for bi, (c0, bw) in enumerate(blocks):
                            fw.op(POOL, lambda c0=c0, bw=bw, hp=hp: nc.gpsimd.tensor_tensor(
                                out=ki[hp][:, c0:c0 + bw], in0=kT[:, c0:c0 + bw], in1=Ee[:, c0:c0 + bw], op=ALU.mult),
                                reads=[R_k, R_Eb[bi]], writes=[R_ki[hp]])
                        kT3 = kT[:].rearrange("p (c t) -> p c t", t=128)
                        ki3 = ki[hp][:].rearrange("p (c t) -> p c t", t=128)
                        for bi, (c0, bw) in enumerate(blocks):
                            k0, k1 = c0 // 128, (c0 + bw) // 128
                            fw.op(POOL, lambda k0=k0, k1=k1, hp=hp, kT3=kT3, ki3=ki3: nc.gpsimd.tensor_tensor(
                                out=kT3[:, k0:k1, :], in0=ki3[:, k0:k1, :],
                                in1=elast[:, hp, k0:k1].unsqueeze(2).to_broadcast([128, k1 - k0, 128]), op=ALU.mult),
                                reads=[R_ki[hp], R_el], writes=[R_k])
                        transpose_many(lambda i0, c_, hp=hp: kie_tm[hp][:, i0 * 128:(i0 + c_) * 128],
                                       lambda i: kT[:, i * 128:(i + 1) * 128], nt, 128, [R_k], R_kt[hp])
                    fw.op(DVE, lambda: nc.vector.memset(S32[:], 0.0), writes=[R_S])
                    if sample:
                        for hp in range(2):
                            for hh in range(2):
                                fw.dma(SPQ, S32[hh * 64:(hh + 1) * 64, hp, hh * 64:(hh + 1) * 64],
                                       E["st_gla"][l, dr, 2 * hp + hh, :, :], writes=[R_S], join=(hp + hh > 0))
                    fw.op(DVE, lambda: nc.vector.tensor_copy(Sbf[0][:], S32[:]), reads=[R_S], writes=[R_Sb[0]])
                    mk = maskf if dr == 0 else maskb
                    order = list(range(nt)) if dr == 0 else list(range(nt - 1, -1, -1))

                    obank = [2, 4]
                    pbank = [3, 5]

                    def front(step):
                        c = order[step]
                        cols = slice(c * 128, (c + 1) * 128)
                        ph = step % 2
                        for hh in range(2):
                            fns = [lambda hh=hh, hp=hp: nc.tensor.matmul(
                                PB[hh][:, hp * 128:(hp + 1) * 128], lhsT=ki[hp][hh * 64:(hh + 1) * 64, cols],
                                rhs=qd[hp][hh * 64:(hh + 1) * 64, cols], start=True, stop=True) for hp in range(2)]
                            fw.group(PE, fns, reads=[R_ki[0], R_ki[1], R_qd[0], R_qd[1]], writes=[R_PB[hh]])
                        for hh in range(2):
                            fw.op(DVE, lambda hh=hh: nc.vector.tensor_tensor(
                                out=att_sb[ph * 2 + hh][:], in0=PB[hh][:, 0:256].rearrange("p (h t) -> p h t", h=2),
                                in1=mk[:].unsqueeze(1).to_broadcast([128, 2, 128]), op=ALU.mult),
                                reads=[R_PB[hh], R_const], writes=[R_att[ph * 2 + hh]])
                        pb_ = pbank[ph]
                        fns = [lambda hp=hp: nc.tensor.matmul(
                            PB[pb_][:, hp * 128:(hp + 1) * 128], lhsT=kie_tm[hp][:, cols],
                            rhs=v[:, c, hp * 128:(hp + 1) * 128], start=True, stop=True) for hp in range(2)]
                        fw.group(PE, fns, reads=[R_kt[0], R_kt[1], R_v], writes=[R_PB[pb_]])

                    def back(step):
                        c = order[step]
                        cols = slice(c * 128, (c + 1) * 128)
                        ph = step % 2
                        ob_, pb_ = obank[ph], pbank[ph]
                        fns = []
                        for hp in range(2):
                            for hh in range(2):
                                hb = hh * 64
                                oc = hp * 128 + hh * 64
                                fns.append(lambda hp=hp, hh=hh, oc=oc: nc.tensor.matmul(
                                    PB[ob_][:, oc:oc + 64], lhsT=att_sb[ph * 2 + hh][:, hp, :],
                                    rhs=v[:, c, hp * 128 + hh * 64: hp * 128 + (hh + 1) * 64], start=True, stop=False))
                                fns.append(lambda hp=hp, hh=hh, hb=hb, oc=oc: nc.tensor.matmul(
                                    PB[ob_][:, oc:oc + 64], lhsT=qd[hp][hb:hb + 64, cols], rhs=Sbf[ph][hb:hb + 64, hp, hh * 64:(hh + 1) * 64],
                                    start=False, stop=True))
                        fw.group(PE, fns, reads=[R_att[ph * 2], R_att[ph * 2 + 1], R_v, R_qd[0], R_qd[1], R_Sb[ph]], writes=[R_PB[ob_]])
                        for hp in range(2):
                            fw.op(DVE, lambda hp=hp: nc.vector.scalar_tensor_tensor(
                                out=S32[:, hp, :], in0=S32[:, hp, :], scalar=elast[:, hp, c:c + 1],
                                in1=PB[pb_][:, hp * 128:(hp + 1) * 128], op0=ALU.mult, op1=ALU.add),
                                reads=[R_S, R_el, R_PB[pb_]], writes=[R_S])
                        fw.op(POOL, lambda: nc.gpsimd.tensor_copy(Sbf[1 - ph][:], S32[:]), reads=[R_S], writes=[R_Sb[1 - ph]])
                        if dr == 0:
                            fw.op(DVE, lambda: nc.vector.tensor_copy(oacc[:, c, :], PB[ob_][:, 0:256]),
                                  reads=[R_PB[ob_]], writes=[R_oacc])
                        else:
                            fw.op(DVE, lambda: nc.vector.tensor_tensor(out=oacc[:, c, :], in0=oacc[:, c, :],
                                                                       in1=PB[ob_][:, 0:256], op=ALU.add),
                                  reads=[R_PB[ob_], R_oacc], writes=[R_oacc])

                    front(0)
                    for step in range(nt):
                        if step + 1 < nt:
                            front(step + 1)
                        back(step)
                        c = order[step]
                        boundary = (step == nt - 1) or (order[step + 1] // ntq != c // ntq)
                        if boundary and not sample:
                            pi = c // ntq
                            for hp in range(2):
                                for hh in range(2):
                                    fw.dma(SPQ, E["o_gla"][pi, l, dr, 2 * hp + hh, :, :],
                                           S32[hh * 64:(hh + 1) * 64, hp, hh * 64:(hh + 1) * 64], reads=[R_S])
                            if step < nt - 1:
                                fw.op(DVE, lambda: nc.vector.memset(S32[:], 0.0), writes=[R_S])
                                nph = (step + 1) % 2
                                fw.op(DVE, lambda nph=nph: nc.vector.memset(Sbf[nph][:], 0.0), writes=[R_Sb[nph]])
                fw.barrier()
            with ExitStack() as st2:
                r = sb(st2, U("gr"), [128, nt, 256], BF16)
                sqo = sb(st2, U("gsq"), [128, nt, 256], F32)
                ssg = sb(st2, U("gss"), [128, nt * 4], F32)
                gout = sb(st2, U("ggo"), [128, 64], F32)
                y = sb(st2, U("gy"), [128, nt, 256], BF16)
                R_r, R_sq, R_ss, R_go, R_y = Res(), Res(), Res(), Res(), Res()
                load_rows(r, ztm[t0:t0 + S, 256:512], R_ztm, R_r, nt)
                fw.dma(SPQ, gout[:], E["gla_gout"][l, :].partition_broadcast(128), writes=[R_go])
                fw.op(DVE, lambda: nc.vector.tensor_tensor(out=sqo[:], in0=oacc[:], in1=oacc[:], op=ALU.mult), reads=[R_oacc], writes=[R_sq])
                fw.op(DVE, lambda: nc.vector.reduce_sum(out=ssg[:], in_=sqo[:].rearrange("p s (h d) -> p (s h) d", h=4), axis=AX.X),
                      reads=[R_sq], writes=[R_ss])
                fw.op(ACT, lambda: nc.scalar.activation(out=ssg[:], in_=ssg[:], func=AF.Sqrt, scale=1.0 / 64, bias=epsb[:, 0:1]),
                      reads=[R_ss], writes=[R_ss])
                fw.op(DVE, lambda: nc.vector.reciprocal(ssg[:], ssg[:]), reads=[R_ss], writes=[R_ss])
                o4 = oacc[:].rearrange("p s (h d) -> p (s h) d", h=4)
                fw.op(DVE, lambda: nc.vector.tensor_tensor(out=o4, in0=o4, in1=ssg[:].unsqueeze(2).to_broadcast([128, nt * 4, 64]),
                                                           op=ALU.mult), reads=[R_oacc, R_ss], writes=[R_oacc])
                fw.op(DVE, lambda: nc.vector.tensor_tensor(out=o4, in0=o4, in1=gout[:].unsqueeze(1).to_broadcast([128, nt * 4, 64]),
                                                           op=ALU.mult), reads=[R_oacc, R_go], writes=[R_oacc])
                fw.op(ACT, lambda: nc.scalar.activation(out=sqo[:], in_=r[:], func=AF.Silu), reads=[R_r, R_ss], writes=[R_sq])
                fw.op(DVE, lambda: nc.vector.tensor_tensor(out=y[:], in0=oacc[:], in1=sqo[:], op=ALU.mult),
                      reads=[R_oacc, R_sq], writes=[R_y])
                for s0 in range(0, nt, 8):
                    n_ = min(8, nt - s0)
                    r0 = t0 + s0 * 128
                    fw.dma(SPQ, mixo[r0:r0 + n_ * 128, 0:256].rearrange("(s p) c -> p s c", p=128), y[:, s0:s0 + n_, :],
                           reads=[R_y], writes=[R_mixo], join=True)
                    if dbg_mix is not None and l == 0:
                        fw.dma(POOL, dbg_mix[r0:r0 + n_ * 128, 0:256].rearrange("(s p) c -> p s c", p=128), y[:, s0:s0 + n_, :],
                               reads=[R_y])
                fw.barrier()

    for (t0, S, sample, nseq) in seqs:
        if "f" in which:
            fnet(t0, S, sample, nseq)
        if "s" in which:
            swa(t0, S, sample, nseq)
        if "m" in which:
            mla(t0, S, sample, nseq)
        if "g" in which:
            gla(t0, S, sample, nseq)


_CACHE = {}


def _consts():
    if "c" in _CACHE:
        return _CACHE["c"]
    c = {}
    c["ident"] = np.eye(128, dtype=np.float32)
    j = np.arange(128)[:, None]
    i = np.arange(128)[None, :]
    c["mask_f"] = (j <= i).astype(np.float32)
    c["mask_b"] = (j >= i).astype(np.float32)
    c64, s64 = _rope_tables(64)
    c["cos64"], c["sin64"] = c64, s64
    c32, s32 = _rope_tables(32)
    c["cos96"] = np.concatenate([np.ones((64, SS), np.float32), c32], 0)
    c["sin96"] = np.concatenate([np.zeros((64, SS), np.float32), s32], 0)

    def dft(S, scale):
        s = np.arange(S, dtype=np.int64)
        prod = (s[:, None] * s[None, :]) % S
        ang = 2.0 * np.pi * prod.astype(np.float64) / S
        nb = S // 128
        C = (np.cos(ang) * scale).astype(np.float32)
        Sn = (np.sin(ang) * scale).astype(np.float32)

        def lay(M):
            return np.ascontiguousarray(M.reshape(nb, 128, nb, 128).transpose(2, 1, 0, 3)).reshape(
                nb, 128, nb * 128).astype(ml_dtypes.bfloat16)
        return lay(C), lay(Sn)
    c["dftc"], c["dfts"] = dft(SS, 1.0 / 64)
    c["dftc_p"], c["dfts_p"] = dft(SP, 1.0 / 16)
    cc = np.arange(64)
    angc = 2.0 * np.pi * ((cc[:, None] * cc[None, :]) % 64) / 64
    Cc = (np.cos(angc) / 8).astype(np.float32)
    Sc = (-np.sin(angc) / 8).astype(np.float32)
    z = np.zeros((64, 64), np.float32)
    c["chc"] = np.block([[Cc, z], [z, Cc]])
    c["chs"] = np.block([[Sc, z], [z, Sc]])
    _CACHE["c"] = c
    return c


def _prep_shared(inp):
    f = lambda a: np.ascontiguousarray(np.asarray(a, dtype=np.float32))
    sh = {}
    sh["w_mod"] = f(inp["w_mod"])
    sh["b_mod"] = f(inp["b_mod"])
    sh["bmod_r"] = f(np.asarray(inp["b_mod"]).reshape(NL, 9, 8, 128).transpose(3, 0, 1, 2))
    sh["g_norm"] = f(inp["g_norm"])
    sh["gnorm_r"] = f(np.asarray(inp["g_norm"]).reshape(NL, 6, 8, 128).transpose(3, 0, 1, 2))
    for nm, key in (("w_gate", "w_ffn_gate"), ("w_up", "w_ffn_up")):
        w = np.asarray(inp[key]).reshape(NL, 2, 8, 128, NJ, 128)
        sh[nm] = f(w.transpose(0, 1, 4, 3, 2, 5).reshape(NL, 2, NJ, 128, 8 * 128))
    sh["w_down"] = f(inp["w_ffn_down"])
    w_in = np.asarray(inp["w_in"])
    p64, _ = _partner(64)
    p32, _ = _partner(32)
    cols = {}
    cols["qg01"] = np.arange(O_QG, O_QG + 128); cols["qg23"] = np.arange(O_QG + 128, O_QG + 256)
    cols["kg01"] = np.arange(O_KG, O_KG + 128); cols["kg23"] = np.arange(O_KG + 128, O_KG + 256)
    cols["af"] = np.arange(O_AF, O_AF + 16); cols["ab"] = np.arange(O_AB, O_AB + 16)
    for h in range(4):
        cols[f"qs{h}"] = O_QS + h * 64 + np.arange(64)
        cols[f"qsp{h}"] = O_QS + h * 64 + p64
    for h in range(2):
        cols[f"ks{h}"] = O_KS + h * 64 + np.arange(64)
        cols[f"ksp{h}"] = O_KS + h * 64 + p64
    cols["zf01"] = np.arange(O_ZF, O_ZF + 128); cols["zf23"] = np.arange(O_ZF + 128, O_ZF + 256)
    cols["kr"] = O_KR + np.arange(32); cols["krp"] = O_KR + p32
    blocks = []
    for nm, m in FM:
        blk = w_in[:, :, cols[nm]].reshape(NL, 8, 128, m).transpose(0, 2, 1, 3).reshape(NL, 128, 8 * m)
        blocks.append(blk)
    sh["w_fm"] = f(np.concatenate(blocks, axis=2))
    tmcols = np.concatenate([np.arange(O_VG, O_VG + 256), np.arange(O_RG, O_RG + 256), np.arange(O_VS, O_VS + 128),
                             np.arange(O_QA, O_QA + 256), np.arange(O_KVA, O_KVA + 128),
                             np.arange(O_KS, O_KS + 128), np.arange(O_KR, O_KR + 32)])
    sh["w_tm"] = f(w_in[:, :, tmcols])
    sh["w_out"] = f(inp["w_out"])
    sh["gla_wg"] = f(inp["gla_w_gate"])
    sh["gla_bg_r"] = f(np.asarray(inp["gla_b_gate"]).reshape(NL, 2, 2, 128).transpose(3, 0, 1, 2))
    sh["gla_gout"] = f(inp["gla_g_out"])
    sh["swa_sink"] = f(inp["swa_sink"])
    sh["mla_gq"] = f(inp["mla_g_q"])
    sh["mla_gkv"] = f(inp["mla_g_kv"])
    wq = np.asarray(inp["mla_w_q_b"]).reshape(NL, 256, 4, 96)
    sh["mla_wqb"] = f(wq)
    permq = np.concatenate([np.arange(64), 64 + p32])
    sh["mla_wqbp"] = f(wq[:, :, :, permq])
    sh["mla_wkvb"] = f(inp["mla_w_kv_b"])
    sh.update(_consts())
    return sh


def _make_in_maps(inp, n=8):
    shared = _prep_shared(inp)
    xs_ = np.asarray(inp["x_sample"], dtype=np.float32)
    xp_ = np.asarray(inp["x_prompt"], dtype=np.float32)
    in_maps = []
    for c in range(n):
        b = c // 2
        m = dict(shared)
        m["xin"] = np.ascontiguousarray(np.concatenate([xs_[b], xp_[2 * c:2 * c + 2].reshape(2 * SP, D)], axis=0))
        cond = np.stack([np.asarray(inp["c"])[b], np.asarray(inp["c_ctx"])], axis=0).astype(np.float32)
        m["condT"] = np.ascontiguousarray(cond.reshape(2, 8, 128).transpose(2, 1, 0))
        m["st_gla"] = np.ascontiguousarray(np.asarray(inp["state_gla"], dtype=np.float32)[b])
        m["c_swa_k"] = np.ascontiguousarray(np.asarray(inp["cache_swa_k"], dtype=np.float32)[b].reshape(NL, 256, 128))
        m["c_swa_v"] = np.ascontiguousarray(np.asarray(inp["cache_swa_v"], dtype=np.float32)[b].reshape(NL, 256, 128))
        m["c_ckv"] = np.ascontiguousarray(np.asarray(inp["cache_mla_ckv"], dtype=np.float32)[b])
        m["c_kr"] = np.ascontiguousarray(np.asarray(inp["cache_mla_krope"], dtype=np.float32)[b])
        in_maps.append(m)
    return in_maps


def kernel(**inp):
    n = 8
    in_maps = _make_in_maps(inp, n)
    nc, fw = build_program()
    res = run_bass_kernel_spmd(nc, in_maps, core_ids=list(range(n)))
    r = res.results
    y_prompt = np.concatenate([r[c]["y_out"][SS:].reshape(2, SP, D) for c in range(n)], axis=0)
    y_sample = np.stack([r[2 * b]["y_out"][:SS] for b in range(4)], axis=0)
    st = np.concatenate([r[c]["o_gla"] for c in range(n)], axis=0)
    ok = np.concatenate([r[c]["o_k"] for c in range(n)], axis=0).reshape(16, NL, SP, 2, 64)
    ov = np.concatenate([r[c]["o_v"] for c in range(n)], axis=0).reshape(16, NL, SP, 2, 64)
    ockv = np.concatenate([r[c]["o_ckv"] for c in range(n)], axis=0)
    okr = np.concatenate([r[c]["o_kr"] for c in range(n)], axis=0)
    return (y_prompt.astype(np.float32), y_sample.astype(np.float32), st.astype(np.float32), ok.astype(np.float32),
            ov.astype(np.float32), ockv.astype(np.float32), okr.astype(np.float32))
```

```python
import os
import numpy as np
import ml_dtypes
DBG = os.environ.get('KDBG', '')
from contextlib import ExitStack
import concourse.bass as bass
import concourse.mybir as mybir
from concourse.bass_utils import run_bass_kernel_spmd

F32 = mybir.dt.float32
BF16 = mybir.dt.bfloat16
AF = mybir.ActivationFunctionType
ALU = mybir.AluOpType
AX = mybir.AxisListType

D = 1024
DFF = 2816
NJ = 22
NL = 4
SS = 4096
SP = 256
NTOK = 4608
EPS = 1e-6
GROUPS = [(0, 1024, 0), (1024, 1024, 0), (2048, 1024, 0), (3072, 1024, 0), (4096, 512, 1)]
FM = [("qg01", 128), ("qg23", 128), ("kg01", 128), ("kg23", 128), ("af", 16), ("ab", 16),
      ("qs0", 64), ("qs1", 64), ("qs2", 64), ("qs3", 64), ("qsp0", 64), ("qsp1", 64), ("qsp2", 64), ("qsp3", 64),
      ("ks0", 64), ("ks1", 64), ("ksp0", 64), ("ksp1", 64), ("zf01", 128), ("zf23", 128), ("kr", 32), ("krp", 32)]
FMI = {n: i for i, (n, _) in enumerate(FM)}
FMOFF = np.cumsum([0] + [s for _, s in FM]).tolist()
NFM = FMOFF[-1]
NTM = 1184
TMCH = [(0, 256), (256, 256), (512, 256), (768, 256), (1024, 160)]
O_QG, O_KG, O_VG, O_RG, O_AF, O_AB, O_QS, O_KS, O_VS, O_ZF, O_QA, O_KVA, O_KR = (
    0, 256, 512, 768, 1024, 1040, 1056, 1312, 1440, 1568, 1824, 2080, 2208)


def _partner(dim):
    half = dim // 2
    q = half // 2
    idx = np.arange(dim)
    blk = idx // half
    w = idx % half
    partner = blk * half + (w + q) % half
    sign = np.where(w < q, -1.0, 1.0)
    return partner, sign


def _rope_tables(dim, S=SS):
    half = dim // 2
    q = half // 2
    t = np.arange(S)
    rows = (t // 64).astype(np.float64)
    cols = (t % 64).astype(np.float64)
    inv = 10000.0 ** (-np.arange(0, half, 2, dtype=np.float64) / half)
    idx = np.arange(dim)
    blk = idx // half
    w = idx % half
    i = w % q
    pos = np.where(blk[:, None] == 0, rows[None, :], cols[None, :])
    ang = pos * inv[i][:, None]
    _, sign = _partner(dim)
    return np.cos(ang).astype(np.float32), (np.sin(ang) * sign[:, None]).astype(np.float32)


class Res:
    __slots__ = ("w", "r")

    def __init__(self):
        self.w = []
        self.r = []


class Q:
    def __init__(self, name, handle, sem):
        self.name = name
        self.h = handle
        self.sem = sem
        self.count = 0
        self.known = {}
        self.dma_pool = []
        self.dma_next = 0


class FW:
    def __init__(self, nc, n_dma_sems=24):
        self.nc = nc
        self.sems = {}
        self._stack = []

        def mk(name):
            cm = nc.semaphore(name)
            self.sems[name] = cm.__enter__()
            self._stack.append(cm)
            return name

        self.pe = Q("pe", nc.tensor, mk("s_pe"))
        self.act = Q("act", nc.scalar, mk("s_act"))
        self.dve = Q("dve", nc.vector, mk("s_dve"))
        self.pool = Q("pool", nc.gpsimd, mk("s_pool"))
        self.sp = Q("sp", nc.sync, mk("s_sp"))
        self.queues = [self.pe, self.act, self.dve, self.pool, self.sp]
        for q in (self.sp, self.pool):
            for i in range(n_dma_sems):
                q.dma_pool.append([mk(f"d_{q.name}{i}"), 0])
        self.n_instr = 0

    def close(self):
        for cm in reversed(self._stack):
            cm.__exit__(None, None, None)

    def _wait(self, q, tok):
        if tok is None:
            return
        key, val = tok
        if q.known.get(key, 0) >= val:
            return
        q.h.wait_ge(self.sems[key], val)
        self.n_instr += 1
        q.known[key] = val

    def deps(self, q, reads, writes, join=False):
        for r in reads:
            for t in r.w:
                self._wait(q, t)
        for w in writes:
            if not join:
                for t in w.w:
                    self._wait(q, t)
            for t in w.r:
                self._wait(q, t)

    def _commit(self, tok, reads, writes, join=False):
        for r in reads:
            r.r.append(tok)
            if len(r.r) > 48:
                best = {}
                for k, v in r.r:
                    if best.get(k, 0) < v:
                        best[k] = v
                r.r = list(best.items())
        for w in writes:
            if join:
                w.w.append(tok)
            else:
                w.w = [tok]
            w.r = []

    def op(self, q, fn, reads=(), writes=()):
        self.deps(q, reads, writes)
        ins = fn()
        q.count += 1
        ins.then_inc(self.sems[q.sem], 1)
        tok = (q.sem, q.count)
        self._commit(tok, reads, writes)
        self.n_instr += 1
        return tok

    def group(self, q, fns, reads=(), writes=()):
        self.deps(q, reads, writes)
        ins = None
        for fn in fns:
            ins = fn()
            self.n_instr += 1
        q.count += 1
        ins.then_inc(self.sems[q.sem], 1)
        if q.name == "pe":
            q.known[q.sem] = q.count
        tok = (q.sem, q.count)
        self._commit(tok, reads, writes)
        return tok

    def dma(self, q, out, in_, reads=(), writes=(), join=False, **kw):
        self.deps(q, reads, writes, join)
        slot = q.dma_pool[q.dma_next]
        q.dma_next = (q.dma_next + 1) % len(q.dma_pool)
        key, val = slot
        if val > 0:
            self._wait(q, (key, val))
        ins = q.h.dma_start(out=out, in_=in_, **kw)
        val += 16
        slot[1] = val
        ins.then_inc(self.sems[key], 16)
        tok = (key, val)
        self._commit(tok, reads, writes, join)
        self.n_instr += 1
        return tok

    def all_tokens(self):
        toks = []
        for q in self.queues:
            if q.count > 0:
                toks.append((q.sem, q.count))
            for key, val in q.dma_pool:
                if val > 0:
                    toks.append((key, val))
        return toks

    def barrier(self):
        toks = self.all_tokens()
        for q in self.queues:
            for t in toks:
                self._wait(q, t)

    def final_wait(self, q):
        for t in self.all_tokens():
            self._wait(q, t)


def build_program(n_layers=NL, do_mix=True, stage=99):
    nc = bass.Bass("TRN2", target_bir_lowering=False)
    fw = FW(nc)
    PE, ACT, DVE, POOL, SPQ = fw.pe, fw.act, fw.dve, fw.pool, fw.sp

    def din(name, shape, dt=F32):
        return nc.dram_tensor(name, list(shape), dt, kind="ExternalInput").ap()

    def dout(name, shape, dt=F32):
        return nc.dram_tensor(name, list(shape), dt, kind="ExternalOutput").ap()

    def dscr(name, shape, dt):
        return nc.dram_tensor(name, list(shape), dt, kind="Internal").ap()

    xin = din("xin", [NTOK, D])
    condT = din("condT", [128, 8, 2])
    st_gla = din("st_gla", [NL, 2, 4, 64, 64])
    c_swa_k = din("c_swa_k", [NL, 256, 128])
    c_swa_v = din("c_swa_v", [NL, 256, 128])
    c_ckv = din("c_ckv", [NL, 256, 128])
    c_kr = din("c_kr", [NL, 256, 32])
    w_mod = din("w_mod", [NL, D, 9 * D])
    b_mod = din("b_mod", [NL, 9 * D])
    bmod_r = din("bmod_r", [128, NL, 9, 8])
    g_norm = din("g_norm", [NL, 6, D])
    gnorm_r = din("gnorm_r", [128, NL, 6, 8])
    w_gate = din("w_gate", [NL, 2, NJ, 128, 8 * 128])
    w_up = din("w_up", [NL, 2, NJ, 128, 8 * 128])
    w_down = din("w_down", [NL, 2, DFF, D])
    w_fm = din("w_fm", [NL, 128, 8 * NFM])
    w_tm = din("w_tm", [NL, D, NTM])
    w_out = din("w_out", [NL, D, D])
    gla_wg = din("gla_wg", [NL, 2, 16, 256])
    gla_bg_r = din("gla_bg_r", [128, NL, 2, 2])
    gla_gout = din("gla_gout", [NL, 64])
    swa_sink = din("swa_sink", [NL, 4])
    mla_gq = din("mla_gq", [NL, 256])
    mla_gkv = din("mla_gkv", [NL, 128])
    mla_wqb = din("mla_wqb", [NL, 256, 4, 96])
    mla_wqbp = din("mla_wqbp", [NL, 256, 4, 96])
    mla_wkvb = din("mla_wkvb", [NL, 128, 512])
    ident_in = din("ident", [128, 128])
    mask_f_in = din("mask_f", [128, 128])
    mask_b_in = din("mask_b", [128, 128])
    cos64 = din("cos64", [64, SS])
    sin64 = din("sin64", [64, SS])
    cos96 = din("cos96", [96, SS])
    sin96 = din("sin96", [96, SS])
    dftc = din("dftc", [32, 128, 32 * 128], BF16)
    dfts = din("dfts", [32, 128, 32 * 128], BF16)
    dftc_p = din("dftc_p", [2, 128, 2 * 128], BF16)
    dfts_p = din("dfts_p", [2, 128, 2 * 128], BF16)
    chc = din("chc", [128, 128])
    chs = din("chs", [128, 128])
    y_out = dout("y_out", [NTOK, D])
    o_gla = dout("o_gla", [2, NL, 2, 4, 64, 64])
    o_k = dout("o_k", [2, NL, 256, 128])
    o_v = dout("o_v", [2, NL, 256, 128])
    o_ckv = dout("o_ckv", [2, NL, 256, 128])
    o_kr = dout("o_kr", [2, NL, 256, 32])
    dbg_mix = dout("dbg_mix", [NTOK, D]) if DBG else None
    xs = dscr("xs", [NTOK, D], F32)
    zT = dscr("zT", [len(FM), 128, NTOK], BF16)
    ztm = dscr("ztm", [NTOK, 1024], BF16)
    mixo = dscr("mixo", [NTOK, D], BF16)
    gates = dscr("gates", [NL, 3, 2, D], F32)
    R_xs = [Res() for _ in GROUPS]
    R_zT = Res()
    R_ztm = Res()
    R_mixo = Res()
    R_gates = Res()

    glob = ExitStack()

    def sb(stack, name, shape, dt):
        return stack.enter_context(nc.sbuf_tensor("t_" + name, list(shape), dt))

    def pm(stack, name, shape, dt):
        return stack.enter_context(nc.psum_tensor("q_" + name, list(shape), dt))

    PT = [pm(glob, f"pt{i}", [128, 1024], BF16) for i in range(2)]
    R_PT = [Res() for _ in range(2)]
    PB = [pm(glob, f"pb{i}", [128, 512], F32) for i in range(6)]
    R_PB = [Res() for _ in range(6)]

    ident = sb(glob, "ident", [128, 128], BF16)
    R_const = Res()
    maskf = sb(glob, "maskf", [128, 128], BF16)
    maskb = sb(glob, "maskb", [128, 128], BF16)
    epsb = sb(glob, "epsb", [128, 1], F32)
    gsT = sb(glob, "gsT", [128, NL, 3, 2, 8], F32)
    shT = sb(glob, "shT", [128, NL, 3, 2, 8], F32)
    R_mod = Res()
    fw.dma(POOL, ident[:], ident_in[:, :], writes=[R_const])
    fw.dma(POOL, maskf[:], mask_f_in[:, :], writes=[R_const], join=True)
    fw.dma(POOL, maskb[:], mask_b_in[:, :], writes=[R_const], join=True)
    R_eps = Res()
    fw.op(DVE, lambda: nc.vector.memset(epsb[:], EPS), writes=[R_eps])

    with ExitStack() as st:
        silc32 = sb(st, "silc32", [128, 8, 2], F32)
        silc = sb(st, "silc", [128, 8, 2], BF16)
        silbc = sb(st, "silbc", [128, 2, 8, 128], BF16)
        bmr = sb(st, "bmr", [128, NL, 9, 8], F32)
        gnr = sb(st, "gnr", [128, NL, 6, 8], F32)
        modraw = sb(st, "modraw", [128, 9, 8, 2], F32)
        wm = [sb(st, f"wm{i}", [128, 8, 1024], BF16) for i in range(2)]
        R_wm = [Res(), Res()]
        brow = [sb(st, f"brow{i}", [128, 1024], F32) for i in range(2)]
        grow = [sb(st, f"grow{i}", [128, 1024], F32) for i in range(2)]
        gt_sb = [sb(st, f"gtsb{i}", [128, 1024], F32) for i in range(2)]
        R_row = [Res(), Res()]
        R_gt = [Res(), Res()]
        R_m = Res()
        R_raw = Res()
        fw.dma(SPQ, silc32[:], condT[:, :, :], writes=[R_m])
        fw.dma(SPQ, bmr[:], bmod_r[:, :, :, :], writes=[R_m], join=True)
        fw.dma(SPQ, gnr[:], gnorm_r[:, :, :, :], writes=[R_m], join=True)
        fw.op(ACT, lambda: nc.scalar.activation(out=silc[:], in_=silc32[:], func=AF.Silu), reads=[R_m], writes=[R_m])
        for c in range(2):
            fw.op(DVE, lambda c=c: nc.vector.tensor_copy(
                silbc[:, c, :, :], silc[:, :, c:c + 1].to_broadcast([128, 8, 128])), reads=[R_m], writes=[R_m])
        cnt = 0
        gcnt = 0
        for l in range(n_layers):
            for v in range(9):
                w_i = cnt % 2
                cnt += 1
                fw.dma(POOL, wm[w_i][:], w_mod[l, :, v * D:(v + 1) * D].rearrange("(k p) c -> p k c", p=128),
                       writes=[R_wm[w_i]])
                i_sub, j_kind = v // 3, v % 3
                if j_kind < 2:
                    ps = PB[0]
                    fns = []
                    for m in range(8):
                        for k in range(8):
                            fns.append(lambda m=m, k=k, w_i=w_i: nc.tensor.matmul(
                                ps[:, m * 2:(m + 1) * 2], lhsT=wm[w_i][:, k, m * 128:(m + 1) * 128],
                                rhs=silc[:, k, :], start=(k == 0), stop=(k == 7)))
                    fw.group(PE, fns, reads=[R_wm[w_i], R_m], writes=[R_PB[0]])
                    dst = shT if j_kind == 0 else gsT
                    psv = ps[:, 0:16].rearrange("p (m c) -> p c m", c=2)
                    fw.op(DVE, lambda dst=dst, psv=psv, l=l, v=v, i_sub=i_sub: nc.vector.tensor_tensor(
                        out=dst[:, l, i_sub, :, :], in0=psv,
                        in1=bmr[:, l, v, :].unsqueeze(1).to_broadcast([128, 2, 8]), op=ALU.add),
                        reads=[R_PB[0], R_m], writes=[R_mod])
                    if j_kind == 1:
                        fw.op(DVE, lambda l=l, i_sub=i_sub: nc.vector.scalar_tensor_tensor(
                            out=gsT[:, l, i_sub, :, :], in0=gsT[:, l, i_sub, :, :], scalar=1.0,
                            in1=gnr[:, l, 2 * i_sub, :].unsqueeze(1).to_broadcast([128, 2, 8]),
                            op0=ALU.add, op1=ALU.mult), reads=[R_mod, R_m], writes=[R_mod])
                else:
                    r_i = gcnt % 2
                    gcnt += 1
                    fw.dma(SPQ, brow[r_i][:], b_mod[l, v * D:(v + 1) * D].partition_broadcast(128), writes=[R_row[r_i]])
                    fw.dma(SPQ, grow[r_i][:], g_norm[l, 2 * i_sub + 1, :].partition_broadcast(128), writes=[R_row[r_i]], join=True)
                    fac = 1.0 if i_sub == 1 else 0.5
                    for c in range(2):
                        for n in range(2):
                            pb = 1 + n
                            fns = [lambda k=k, c=c, n=n, pb=pb, w_i=w_i: nc.tensor.matmul(
                                PB[pb][:, :], lhsT=silbc[:, c, k, :], rhs=wm[w_i][:, k, n * 512:(n + 1) * 512],
                                start=(k == 0), stop=(k == 7)) for k in range(8)]
                            fw.group(PE, fns, reads=[R_wm[w_i], R_m], writes=[R_PB[pb]])
                            fw.op(DVE, lambda c=c, n=n, pb=pb, r_i=r_i: nc.vector.tensor_tensor(
                                out=gt_sb[c][:, n * 512:(n + 1) * 512], in0=PB[pb][:, :],
                                in1=brow[r_i][:, n * 512:(n + 1) * 512], op=ALU.add),
                                reads=[R_PB[pb], R_row[r_i]], writes=[R_gt[c]])
                        fw.op(DVE, lambda c=c, r_i=r_i, fac=fac: nc.vector.scalar_tensor_tensor(
                            out=gt_sb[c][:], in0=gt_sb[c][:], scalar=fac, in1=grow[r_i][:],
                            op0=ALU.mult, op1=ALU.mult), reads=[R_gt[c], R_row[r_i]], writes=[R_gt[c]])
                        fw.dma(SPQ, gates[l, i_sub, c:c + 1, :], gt_sb[c][0:1, :], reads=[R_gt[c]], writes=[R_gates])
        fw.barrier()

    if stage == 0:
        dbg = sb(glob, "dbg", [128, 1024], F32)
        fw.dma(SPQ, dbg[:], gates[0, 0, 0, :].partition_broadcast(128), reads=[R_gates], writes=[R_mod])
        fw.dma(SPQ, y_out[0:128, :], dbg[:], reads=[R_mod])
        fw.dma(SPQ, y_out[128:256, 0:192], gsT[:].rearrange("p a b c d -> p (a b c d)"), reads=[R_mod])
        fw.dma(SPQ, y_out[256:384, 0:192], shT[:].rearrange("p a b c d -> p (a b c d)"), reads=[R_mod])
        fw.final_wait(SPQ)
        glob.close()
        fw.close()
        return nc, fw
    tl = None
    xg = xn = hT = actT = wd = junk = ss = rstd = ss2 = rs2 = None
    wgu = gtg = sg = tmp = zst = zo32 = None
    tlc = [0]

    def alloc_tl():
        nonlocal tl, xg, xn, hT, actT, wd, wgu, gtg, junk, sg, ss, rstd, ss2, rs2, tmp, zst, zo32
        tl = ExitStack()
        tlc[0] += 1
        u = f"_{tlc[0]}"
        xg = sb(tl, "xg" + u, [128, 8, D], F32)
        xn = sb(tl, "xn" + u, [128, 8, D], BF16)
        hT = sb(tl, "hT" + u, [128, 8, 1024], BF16)
        actT = sb(tl, "actT" + u, [128, NJ, 1024], BF16)
        wd = sb(tl, "wd" + u, [128, NJ, D], BF16)
        wgu = [sb(tl, f"wgu{i}" + u, [128, 2, 8 * 128], BF16) for i in range(3)]
        gtg = [sb(tl, f"gtg{i}" + u, [128, D], F32) for i in range(2)]
        junk = sb(tl, "junk" + u, [128, D], BF16)
        sg = [sb(tl, f"sg{i}" + u, [128, 512], F32) for i in range(2)]
        ss = sb(tl, "ss" + u, [128, 8], F32)
        rstd = sb(tl, "rstd" + u, [128, 8], F32)
        ss2 = sb(tl, "ss2" + u, [128, 2, 2], F32)
        rs2 = sb(tl, "rs2" + u, [128, 2], F32)
        tmp = [sb(tl, f"tmp{i}" + u, [128, 512], F32) for i in range(2)]
        zst = [sb(tl, f"zst{i}" + u, [128, 1024], BF16) for i in range(2)]
        zo32 = [sb(tl, f"zo32{i}" + u, [128, 288], F32) for i in range(4)]

    def free_tl():
        fw.barrier()
        tl.close()

    alloc_tl()
    R_xg, R_xn, R_hT, R_actT, R_wd, R_wtm = Res(), Res(), Res(), Res(), Res(), Res()
    R_wgu = [Res() for _ in range(3)]
    R_gtg = [Res(), Res()]
    R_xns = [Res() for _ in range(8)]
    R_hTs = [Res() for _ in range(8)]
    R_sg = [Res(), Res()]
    R_ss, R_rstd = Res(), Res()
    R_ss2 = [Res(), Res()]
    R_tmp = [Res(), Res()]
    R_zst = [Res(), Res()]
    R_zo32 = [Res() for _ in range(4)]
    state = {"wgu": 0, "gtg": 0, "pd": 0, "zst": 0, "ev": 0}

    def evac(out, in_, reads, writes, scale=None):
        state["ev"] += 1
        if False:
            if scale is None:
                fw.op(ACT, lambda: nc.scalar.copy(out, in_), reads=reads, writes=writes)
            else:
                fw.op(ACT, lambda: nc.scalar.mul(out, in_, scale), reads=reads, writes=writes)
        else:
            if scale is None:
                fw.op(DVE, lambda: nc.vector.tensor_copy(out, in_), reads=reads, writes=writes)
            else:
                fw.op(DVE, lambda: nc.vector.tensor_scalar_mul(out, in_, scale), reads=reads, writes=writes)

    def norm_to_hT(T, l, i_sub, c):
        nonlocal transpose_to_hT
        nst = T // 128
        fw.op(DVE, lambda: nc.vector.memset(ss[:], 0.0), writes=[R_ss])
        for s in range(nst):
            fw.op(ACT, lambda s=s: nc.scalar.activation(out=junk[:], in_=xg[:, s, :], func=AF.Square,
                                                        accum_out=ss[:, s:s + 1]), reads=[R_xg], writes=[R_ss])
        fw.op(ACT, lambda: nc.scalar.activation(out=rstd[:, 0:nst], in_=ss[:, 0:nst], func=AF.Sqrt,
                                                scale=1.0 / D, bias=epsb[:, 0:1]), reads=[R_ss, R_eps], writes=[R_rstd])
        fw.op(DVE, lambda: nc.vector.reciprocal(rstd[:, 0:nst], rstd[:, 0:nst]), reads=[R_rstd], writes=[R_rstd])
        for s in range(nst):
            fw.op(DVE, lambda s=s: nc.vector.tensor_scalar_mul(xn[:, s, :], xg[:, s, :], rstd[:, s:s + 1]),
                  reads=[R_xg, R_rstd], writes=[R_xn])
        transpose_to_hT(T, scale_bias=(l, i_sub, c))

    def norm_elem(s, l, i_sub, c):
        fw.op(DVE, lambda: nc.vector.memset(ss[:, s:s + 1], 0.0), writes=[R_ss])
        fw.op(ACT, lambda: nc.scalar.activation(out=junk[:], in_=xg[:, s, :], func=AF.Square, accum_out=ss[:, s:s + 1]),
              reads=[R_xg], writes=[R_ss])
        fw.op(ACT, lambda: nc.scalar.activation(out=rstd[:, s:s + 1], in_=ss[:, s:s + 1], func=AF.Sqrt, scale=1.0 / D,
                                                bias=epsb[:, 0:1]), reads=[R_ss, R_eps], writes=[R_rstd])
        fw.op(DVE, lambda: nc.vector.reciprocal(rstd[:, s:s + 1], rstd[:, s:s + 1]), reads=[R_rstd], writes=[R_rstd])
        fw.op(DVE, lambda: nc.vector.tensor_scalar_mul(xn[:, s, :], xg[:, s, :], rstd[:, s:s + 1]),
              reads=[R_xg, R_rstd], writes=[R_xns[s], R_xn])

    def norm_tr(s, l, i_sub, c):
        p = s % 2
        fns = [lambda k=k: nc.tensor.transpose(PT[p][:, k * 128:(k + 1) * 128], xn[:, s, k * 128:(k + 1) * 128], ident[:])
               for k in range(8)]
        fw.group(PE, fns, reads=[R_xns[s], R_const], writes=[R_PT[p]])
        fw.group(DVE, [lambda k=k: nc.vector.tensor_scalar(
            out=hT[:, k, s * 128:(s + 1) * 128], in0=PT[p][:, k * 128:(k + 1) * 128], scalar1=gsT[:, l, i_sub, c, k:k + 1],
            scalar2=shT[:, l, i_sub, c, k:k + 1], op0=ALU.mult, op1=ALU.add) for k in range(8)],
            reads=[R_PT[p], R_mod], writes=[R_hTs[s]])

    def transpose_to_hT(T, scale_bias=None):
        nst = T // 128
        for k in range(8):
            p = k % 2
            fns = [lambda s=s, k=k, p=p: nc.tensor.transpose(PT[p][:, s * 128:(s + 1) * 128],
                                                           xn[:, s, k * 128:(k + 1) * 128], ident[:])
                   for s in range(nst)]
            fw.group(PE, fns, reads=[R_xn, R_const], writes=[R_PT[p]])
            if scale_bias is not None:
                l, i_sub, c = scale_bias
                fw.op(DVE, lambda k=k, p=p: nc.vector.tensor_scalar(
                    out=hT[:, k, 0:T], in0=PT[p][:, 0:T], scalar1=gsT[:, l, i_sub, c, k:k + 1],
                    scalar2=shT[:, l, i_sub, c, k:k + 1], op0=ALU.mult, op1=ALU.add),
                    reads=[R_PT[p], R_mod], writes=R_hTs[0:nst])
            else:
                fw.op(ACT, lambda k=k, p=p: nc.scalar.copy(hT[:, k, 0:T], PT[p][:, 0:T]),
                      reads=[R_PT[p]], writes=R_hTs[0:nst])

    def load_gtg(l, i_sub, c):
        gi = state["gtg"] % 2
        state["gtg"] += 1
        fw.dma(SPQ, gtg[gi][:], gates[l, i_sub, c, :].partition_broadcast(128), reads=[R_gates], writes=[R_gtg[gi]])
        return gi

    def post_norm_residual(s, pa, pb, gi):
        q = s % 2
        fw.op(DVE, lambda: nc.vector.memset(ss2[:, q, :], 0.0), writes=[R_ss2[q]])
        for n, bank in enumerate((pa, pb)):
            fw.op(ACT, lambda n=n, bank=bank: nc.scalar.activation(
                out=junk[:, 0:512], in_=PB[bank][:, :], func=AF.Square, accum_out=ss2[:, q, n:n + 1]),
                reads=[R_PB[bank]], writes=[R_ss2[q]])
        fw.op(DVE, lambda: nc.vector.tensor_tensor(out=rs2[:, q:q + 1], in0=ss2[:, q, 0:1], in1=ss2[:, q, 1:2],
                                                   op=ALU.add), reads=[R_ss2[q]], writes=[R_ss2[q]])
        fw.op(ACT, lambda: nc.scalar.activation(out=rs2[:, q:q + 1], in_=rs2[:, q:q + 1], func=AF.Sqrt,
                                                scale=1.0 / D, bias=epsb[:, 0:1]), reads=[R_ss2[q], R_eps],
              writes=[R_ss2[q]])
        fw.op(DVE, lambda: nc.vector.reciprocal(rs2[:, q:q + 1], rs2[:, q:q + 1]), reads=[R_ss2[q]], writes=[R_ss2[q]])
        fw.group(DVE, [lambda n=n, bank=bank: nc.vector.scalar_tensor_tensor(
            out=tmp[n][:], in0=PB[bank][:, :], scalar=rs2[:, q:q + 1], in1=gtg[gi][:, n * 512:(n + 1) * 512],
            op0=ALU.mult, op1=ALU.mult) for n, bank in enumerate((pa, pb))],
            reads=[R_PB[pa], R_PB[pb], R_ss2[q], R_gtg[gi]], writes=[R_tmp[0], R_tmp[1]])
        fw.group(DVE, [lambda n=n: nc.vector.tensor_tensor(
            out=xg[:, s, n * 512:(n + 1) * 512], in0=xg[:, s, n * 512:(n + 1) * 512], in1=tmp[n][:], op=ALU.add)
            for n in range(2)], reads=[R_tmp[0], R_tmp[1], R_xg], writes=[R_xg])

    def load_wgu(l, i, j):
        wi = state["wgu"] % 3
        state["wgu"] += 1
        fw.dma(POOL, wgu[wi][:, 0, :], w_gate[l, i, j, :, :], writes=[R_wgu[wi]])
        fw.dma(POOL, wgu[wi][:, 1, :], w_up[l, i, j, :, :], writes=[R_wgu[wi]], join=True)
        return wi

    def ffn(T, l, i, c, pre_normed=False, next_norm=None):
        i_sub = 0 if i == 0 else 2
        nst, nh = T // 128, T // 512
        gi = load_gtg(l, i_sub, c)
        pend = [load_wgu(l, i, 0), load_wgu(l, i, 1)]
        if not pre_normed:
            norm_to_hT(T, l, i_sub, c)
        for j in range(NJ):
            wi = pend.pop(0)
            if j + 2 < NJ:
                pend.append(load_wgu(l, i, j + 2))
            fw.dma(POOL, wd[:, j, :], w_down[l, i, j * 128:(j + 1) * 128, :], writes=[R_wd], join=(j > 0))
            for h in range(nh):
                pg, pu = (0, 1) if (j * nh + h) % 2 == 0 else (2, 3)
                wv = wgu[wi][:].rearrange("p g (k c) -> p g k c", k=8)
                fns = [lambda k=k, wv=wv, h=h, pg=pg: nc.tensor.matmul(
                    PB[pg][:, :], lhsT=wv[:, 0, k, :], rhs=hT[:, k, h * 512:(h + 1) * 512],
                    start=(k == 0), stop=(k == 7)) for k in range(8)]
                fw.group(PE, fns, reads=[R_wgu[wi]] + R_hTs[4 * h:4 * h + 4], writes=[R_PB[pg]])
                fns = [lambda k=k, wv=wv, h=h, pu=pu: nc.tensor.matmul(
                    PB[pu][:, :], lhsT=wv[:, 1, k, :], rhs=hT[:, k, h * 512:(h + 1) * 512],
                    start=(k == 0), stop=(k == 7)) for k in range(8)]
                fw.group(PE, fns, reads=[R_wgu[wi]] + R_hTs[4 * h:4 * h + 4], writes=[R_PB[pu]])
                si = (j * nh + h) % 2
                fw.op(ACT, lambda si=si, pg=pg: nc.scalar.activation(out=sg[si][:], in_=PB[pg][:, :], func=AF.Silu),
                      reads=[R_PB[pg]], writes=[R_sg[si]])
                fw.op(DVE, lambda si=si, pu=pu, j=j, h=h: nc.vector.tensor_tensor(
                    out=actT[:, j, h * 512:(h + 1) * 512], in0=sg[si][:], in1=PB[pu][:, :], op=ALU.mult),
                    reads=[R_sg[si], R_PB[pu]], writes=[R_actT])
        for s in range(nst):
            pa, pb = (0, 1) if s % 2 == 0 else (2, 3)
            for n, bank in enumerate((pa, pb)):
                fns = [lambda j=j, s=s, n=n, bank=bank: nc.tensor.matmul(
                    PB[bank][:, :], lhsT=actT[:, j, s * 128:(s + 1) * 128], rhs=wd[:, j, n * 512:(n + 1) * 512],
                    start=(j == 0), stop=(j == NJ - 1)) for j in range(NJ)]
                fw.group(PE, fns, reads=[R_actT, R_wd], writes=[R_PB[bank]])
            post_norm_residual(s, pa, pb, gi)
            if next_norm is not None:
                norm_elem(s, *next_norm)
                if s >= 1:
                    norm_tr(s - 1, *next_norm)
        if next_norm is not None:
            norm_tr(nst - 1, *next_norm)

    def win_phase(T, l, c, t0, gidx, pre_normed=False):
        nst, nh = T // 128, T // 512
        if not pre_normed:
            norm_to_hT(T, l, 1, c)
        for ci, (nm, m) in enumerate(FM):
            if stage == 51 or (stage == 53 and m < 64):
                continue
            wi = state["wgu"] % 3
            state["wgu"] += 1
            off = FMOFF[ci] * 8
            fw.dma(POOL, wgu[wi][:, 0, 0:8 * m], w_fm[l, :, off:off + 8 * m], writes=[R_wgu[wi]])
            wv = wgu[wi][:, 0, 0:8 * m].rearrange("p (k c) -> p k c", k=8)
            zi = state["zst"] % 2
            state["zst"] += 1
            for h in range(nh):
                bank = 4 + (h % 2)
                fns = [lambda k=k, wv=wv, h=h, bank=bank, m=m: nc.tensor.matmul(
                    PB[bank][0:m, :], lhsT=wv[:, k, :], rhs=hT[:, k, h * 512:(h + 1) * 512],
                    start=(k == 0), stop=(k == 7)) for k in range(8)]
                fw.group(PE, fns, reads=[R_wgu[wi]] + R_hTs[4 * h:4 * h + 4], writes=[R_PB[bank]])
                evac(zst[zi][0:m, h * 512:(h + 1) * 512], PB[bank][0:m, :], [R_PB[bank]], [R_zst[zi]],
                     scale=(0.125 if ci < 2 else None))
            fw.dma(SPQ, zT[ci, 0:m, t0:t0 + T], zst[zi][0:m, 0:T], reads=[R_zst[zi]], writes=[R_zT])
        if stage in (52, 53):
            return
        for ni, (o, w) in enumerate(TMCH):
            wi = state["wgu"] % 3
            state["wgu"] += 1
            wflat = wgu[wi][:].rearrange("p g c -> p (g c)")
            wv = wflat.rearrange("p (k c) -> p k c", k=8)
            fw.dma(POOL, wv[:, :, 0:w], w_tm[l, :, o:o + w].rearrange("(k p) c -> p k c", p=128), writes=[R_wgu[wi]])
            for s in range(nst):
                bank = 4 + ((ni * nst + s) % 2)
                fns = [lambda k=k, s=s, w=w, bank=bank, wv=wv: nc.tensor.matmul(
                    PB[bank][:, 0:w], lhsT=hT[:, k, s * 128:(s + 1) * 128], rhs=wv[:, k, 0:w],
                    start=(k == 0), stop=(k == 7)) for k in range(8)]
                fw.group(PE, fns, reads=[R_wgu[wi], R_hTs[s]], writes=[R_PB[bank]])
                if o < 1024:
                    evac(xn[:, s, o:o + w], PB[bank][:, 0:w], [R_PB[bank]], [R_xn])
                if c == 1 and o == 512:
                    fw.op(DVE, lambda bank=bank, s=s: nc.vector.tensor_copy(zo32[s][:, 0:128], PB[bank][:, 0:128]),
                          reads=[R_PB[bank]], writes=[R_zo32[s]])
                if c == 1 and o == 1024:
                    fw.op(DVE, lambda bank=bank, s=s: nc.vector.tensor_copy(zo32[s][:, 128:288], PB[bank][:, 0:160]),
                          reads=[R_PB[bank]], writes=[R_zo32[s]])
        for s in range(nst):
            r0 = t0 + s * 128
            fw.dma(SPQ, ztm[r0:r0 + 128, :], xn[:, s, :], reads=[R_xn], writes=[R_ztm], join=(s > 0))
            if c == 1:
                pi, pr = (s * 128) // SP, (s * 128) % SP
                fw.dma(SPQ, o_v[pi, l, pr:pr + 128, :], zo32[s][:, 0:128], reads=[R_zo32[s]])
                fw.dma(SPQ, o_k[pi, l, pr:pr + 128, :], zo32[s][:, 128:256], reads=[R_zo32[s]])
                fw.dma(SPQ, o_kr[pi, l, pr:pr + 128, :], zo32[s][:, 256:288], reads=[R_zo32[s]])

    def wout_phase(T, l, c, t0, next_norm=None):
        nst = T // 128
        gi = load_gtg(l, 1, c)
        for s in range(nst):
            pa, pb = (0, 1) if s % 2 == 0 else (2, 3)
            for n, bank in enumerate((pa, pb)):
                fns = [lambda k=k, s=s, n=n, bank=bank: nc.tensor.matmul(
                    PB[bank][:, :], lhsT=hT[:, k, s * 128:(s + 1) * 128], rhs=wd[:, k, n * 512:(n + 1) * 512],
                    start=(k == 0), stop=(k == 7)) for k in range(8)]
                fw.group(PE, fns, reads=[R_hTs[s], R_wd], writes=[R_PB[bank]])
            post_norm_residual(s, pa, pb, gi)
            if next_norm is not None:
                norm_elem(s, *next_norm)
                if s >= 1:
                    norm_tr(s - 1, *next_norm)
        if next_norm is not None:
            norm_tr(nst - 1, *next_norm)

    def wout_prefetch(T, l, t0):
        nst = T // 128
        fw.dma(POOL, wd[:, 0:8, :], w_out[l, :, :].rearrange("(k p) c -> p k c", p=128), writes=[R_wd])
        fw.dma(SPQ, xn[:, 0:nst, :], mixo[t0:t0 + T, :].rearrange("(s p) c -> p s c", p=128),
               reads=[R_mixo], writes=[R_xn] + R_xns[0:nst])
        transpose_to_hT(T, None)

    def load_x(src, t0, T, res=None):
        fw.dma(SPQ, xg[:, 0:T // 128, :], src[t0:t0 + T, :].rearrange("(s p) c -> p s c", p=128),
               reads=[res] if res is not None else [], writes=[R_xg])

    def store_x(dst, t0, T, res=None):
        fw.dma(SPQ, dst[t0:t0 + T, :].rearrange("(s p) c -> p s c", p=128), xg[:, 0:T // 128, :],
               reads=[R_xg], writes=[res] if res is not None else [])

    from_mix = {}

    def mix_phase(l):
        if not do_mix:
            return
        _mixers(nc, fw, l, dict(
            zT=zT, ztm=ztm, mixo=mixo, R_zT=R_zT, R_ztm=R_ztm, R_mixo=R_mixo, PB=PB, R_PB=R_PB, PT=PT, R_PT=R_PT,
            ident=ident, maskf=maskf, maskb=maskb, epsb=epsb, R_const=R_const, st_gla=st_gla, c_swa_k=c_swa_k,
            c_swa_v=c_swa_v, c_ckv=c_ckv, c_kr=c_kr, gla_wg=gla_wg, gla_bg_r=gla_bg_r, gla_gout=gla_gout,
            swa_sink=swa_sink, mla_gq=mla_gq, mla_gkv=mla_gkv, mla_wqb=mla_wqb, mla_wqbp=mla_wqbp, mla_wkvb=mla_wkvb,
            cos64=cos64, sin64=sin64, cos96=cos96, sin96=sin96, dftc=dftc, dfts=dfts, dftc_p=dftc_p, dfts_p=dfts_p,
            chc=chc, chs=chs, o_gla=o_gla, o_ckv=o_ckv, sb=sb, dbg_mix=dbg_mix))

    if stage in (1, 2, 3, 4, 5, 6, 20, 21, 51, 52, 53):
        t0, T, c = GROUPS[4] if stage != 4 else GROUPS[0]
        load_x(xin, t0, T)
        if stage == 2:
            norm_to_hT(T, 0, 0, c)
        if stage == 20:
            _saved = transpose_to_hT
            transpose_to_hT = lambda *a, **k: None
            norm_to_hT(T, 0, 0, c)
        if stage == 21:
            fw.op(DVE, lambda: nc.vector.tensor_copy(xn[:, 0:4, :], xg[:, 0:4, :]), reads=[R_xg], writes=[R_xn])
            transpose_to_hT(T, None)
        if stage in (3, 4):
            ffn(T, 0, 0, c)
        if stage in (5, 6, 51, 52, 53):
            win_phase(T, 0, c, t0, 4)
        if stage == 6:
            store_x(xs, t0, T, R_xs[4])
            fw.barrier()
            load_x(xs, t0, T, R_xs[4])
        store_x(y_out, t0, T)
        fw.final_wait(SPQ)
        tl.close()
        glob.close()
        fw.close()
        return nc, fw
    for gidx, (t0, T, c) in enumerate(GROUPS):
        load_x(xin, t0, T)
        ffn(T, 0, 0, c, next_norm=(0, 1, c))
        win_phase(T, 0, c, t0, gidx, pre_normed=True)
        store_x(xs, t0, T, R_xs[gidx])
    for l in range(n_layers):
        free_tl()
        mix_phase(l)
        fw.barrier()
        alloc_tl()
        for gidx, (t0, T, c) in enumerate(GROUPS):
            if do_mix:
                wout_prefetch(T, l, t0)
            load_x(xs, t0, T, R_xs[gidx])
            if do_mix:
                wout_phase(T, l, c, t0)
            if l + 1 < n_layers:
                ffn(T, l, 1, c, next_norm=(l + 1, 0, c))
                ffn(T, l + 1, 0, c, pre_normed=True, next_norm=(l + 1, 1, c))
                win_phase(T, l + 1, c, t0, gidx, pre_normed=True)
                store_x(xs, t0, T, R_xs[gidx])
            else:
                ffn(T, l, 1, c)
                store_x(y_out, t0, T)
    fw.final_wait(SPQ)
    fw.final_wait(ACT)
    tl.close()
    glob.close()
    fw.close()
    return nc, fw


def _mixers(nc, fw, l, E):
    PE, ACT, DVE, POOL, SPQ = fw.pe, fw.act, fw.dve, fw.pool, fw.sp
    sb = E["sb"]
    PB, R_PB, PT, R_PT = E["PB"], E["R_PB"], E["PT"], E["R_PT"]
    zT, ztm, mixo = E["zT"], E["ztm"], E["mixo"]
    R_zT, R_ztm, R_mixo = E["R_zT"], E["R_ztm"], E["R_mixo"]
    ident, maskf, maskb, epsb, R_const = E["ident"], E["maskf"], E["maskb"], E["epsb"], E["R_const"]
    dbg_mix = E["dbg_mix"]
    which = [ch for ch in "fmsg" if ch in DBG] or list("fmsg")
    seqs = [(0, SS, True, 1), (SS, 2 * SP, False, 2)]
    uid = [0]
    cnt = {"s": 0, "o": 0, "pt": 0, "tp": 0}

    def U(n):
        uid[0] += 1
        return f"{n}_L{l}_{uid[0]}"

    def emit_out(ap, R_ap, row0, col0, rows=128, w=256):
        fw.dma(SPQ, mixo[row0:row0 + rows, col0:col0 + w], ap, reads=[R_ap], writes=[R_mixo], join=True)
        if dbg_mix is not None and l == 0:
            fw.dma(POOL, dbg_mix[row0:row0 + rows, col0:col0 + w], ap, reads=[R_ap])

    def load_rows(dst3, src2, R_src, R_dst, nt):
        for i, s0 in enumerate(range(0, nt, 8)):
            n_ = min(8, nt - s0)
            fw.dma(SPQ, dst3[:, s0:s0 + n_, :], src2[s0 * 128:(s0 + n_) * 128, :].rearrange("(s p) c -> p s c", p=128),
                   reads=[R_src], writes=[R_dst], join=(i > 0))

    def transpose_many(dst_fn, src_fn, n, rows_out, reads, R_dst):
        for i0 in range(0, n, 8):
            c_ = min(8, n - i0)
            p = cnt["tp"] % 2
            cnt["tp"] += 1
            fns = [lambda i=i, p=p, i0=i0: nc.tensor.transpose(PT[p][0:rows_out, (i - i0) * 128:(i - i0 + 1) * 128],
                                                                src_fn(i), ident[:]) for i in range(i0, i0 + c_)]
            fw.group(PE, fns, reads=list(reads) + [R_const], writes=[R_PT[p]])
            fw.op(ACT, lambda p=p, i0=i0, c_=c_: nc.scalar.copy(dst_fn(i0, c_), PT[p][0:rows_out, 0:c_ * 128]),
                  reads=[R_PT[p]], writes=[R_dst])

    class AttStream:
        def __init__(self, st, LA=3):
            self.LA = LA
            self.pt = [sb(st, U("pt"), [128, 512], BF16) for _ in range(4)]
            self.R_pt = [Res() for _ in range(4)]
            self.den = sb(st, U("den"), [128, 4], F32)
            self.R_den = [Res() for _ in range(4)]
            self.sbank = [0, 1, 4, 5]
            self.items = []
            self.nblk = 0

        def add_block(self, qT, keys, scale, den_extra, out_ap, R_out, reads, done_cb=None):
            blk = dict(ob=2 + self.nblk % 2, di=self.nblk % 4, den_extra=den_extra, out_ap=out_ap, R_out=R_out,
                       reads=reads, done_cb=done_cb, nb=len(keys), scale=scale, qT=qT)
            self.nblk += 1
            for b0 in range(0, len(keys), 4):
                self.items.append((blk, b0, keys[b0:b0 + 4]))

        def _qk(self, i):
            blk, b0, batch = self.items[i]
            slot = i % 4
            sbank = self.sbank[slot]
            qT, scale = blk["qT"], blk["scale"]
            fns = [lambda j=j, kT=kT: nc.tensor.matmul(PB[sbank][:, j * 128:(j + 1) * 128], lhsT=kT, rhs=qT,
                                                       start=True, stop=True) for j, (kT, _, _) in enumerate(batch)]
            fw.group(PE, fns, reads=blk["reads"], writes=[R_PB[sbank]])
            wv_ = len(batch) * 128
            fw.op(ACT, lambda: nc.scalar.activation(out=self.pt[slot][:, 0:wv_], in_=PB[sbank][:, 0:wv_], func=AF.Exp,
                                                    scale=scale), reads=[R_PB[sbank]], writes=[self.R_pt[slot]])
            for j, (_, _, mk) in enumerate(batch):
                if mk is not None:
                    fw.op(POOL, lambda j=j, mk=mk: nc.gpsimd.tensor_tensor(
                        out=self.pt[slot][:, j * 128:(j + 1) * 128], in0=self.pt[slot][:, j * 128:(j + 1) * 128], in1=mk[:],
                        op=ALU.mult), reads=[self.R_pt[slot], R_const], writes=[self.R_pt[slot]])

        def _pv(self, i):
            blk, b0, batch = self.items[i]
            slot = i % 4
            ob, nb, di = blk["ob"], blk["nb"], blk["di"]
            fns = [lambda j=j, vx=vx: nc.tensor.matmul(PB[ob][:, 0:65], lhsT=self.pt[slot][:, j * 128:(j + 1) * 128], rhs=vx,
                                                       start=(b0 + j == 0), stop=(b0 + j == nb - 1))
                   for j, (_, vx, _) in enumerate(batch)]
            fw.group(PE, fns, reads=list(blk["reads"]) + [self.R_pt[slot]], writes=[R_PB[ob]])
            if b0 + len(batch) == nb:
                den, R_den = self.den, self.R_den
                if blk["den_extra"] is not None:
                    fw.op(DVE, lambda: nc.vector.tensor_tensor(out=den[:, di:di + 1], in0=PB[ob][:, 64:65], in1=blk["den_extra"],
                                                               op=ALU.add), reads=list(blk["reads"]) + [R_PB[ob]], writes=[R_den[di]])
                else:
                    fw.op(DVE, lambda: nc.vector.tensor_copy(den[:, di:di + 1], PB[ob][:, 64:65]), reads=[R_PB[ob]],
                          writes=[R_den[di]])
                fw.op(DVE, lambda: nc.vector.reciprocal(den[:, di:di + 1], den[:, di:di + 1]), reads=[R_den[di]], writes=[R_den[di]])
                fw.op(DVE, lambda: nc.vector.tensor_scalar_mul(blk["out_ap"], PB[ob][:, 0:64], den[:, di:di + 1]),
                      reads=[R_PB[ob], R_den[di]], writes=[blk["R_out"]])
                if blk["done_cb"] is not None:
                    blk["done_cb"]()

        def run(self):
            n = len(self.items)
            for i in range(n + self.LA):
                if i < n:
                    self._qk(i)
                if i - self.LA >= 0:
                    self._pv(i - self.LA)
            self.items = []

    def rope_rows(st, dst_fn, R_dst, rows, na, npm, cosT, sinT, row_off, t0, S):
        CH = 1024
        ra = [sb(st, U("ra"), [rows, CH], BF16) for _ in range(2)]
        rp = [sb(st, U("rp"), [rows, CH], BF16) for _ in range(2)]
        tc_ = [sb(st, U("tc"), [rows, CH], F32) for _ in range(2)]
        ts_ = [sb(st, U("ts"), [rows, CH], F32) for _ in range(2)]
        R_in = [Res(), Res()]
        R_t = [Res(), Res()]
        for i, c0 in enumerate(range(0, S, CH)):
            b = i % 2
            fw.dma(SPQ, ra[b][:], zT[FMI[na], 0:rows, t0 + c0:t0 + c0 + CH], reads=[R_zT], writes=[R_in[b]])
            fw.dma(SPQ, rp[b][:], zT[FMI[npm], 0:rows, t0 + c0:t0 + c0 + CH], reads=[R_zT], writes=[R_in[b]], join=True)
            fw.dma(SPQ, tc_[b][:], cosT[row_off:row_off + rows, c0:c0 + CH], writes=[R_t[b]])
            fw.dma(SPQ, ts_[b][:], sinT[row_off:row_off + rows, c0:c0 + CH], writes=[R_t[b]], join=True)
            fw.op(DVE, lambda b=b: nc.vector.tensor_tensor(out=tc_[b][:], in0=tc_[b][:], in1=ra[b][:], op=ALU.mult),
                  reads=[R_in[b], R_t[b]], writes=[R_t[b]])
            fw.op(POOL, lambda b=b: nc.gpsimd.tensor_tensor(out=ts_[b][:], in0=ts_[b][:], in1=rp[b][:], op=ALU.mult),
                  reads=[R_in[b], R_t[b]], writes=[R_t[b]])
            fw.op(DVE, lambda b=b, c0=c0: nc.vector.tensor_tensor(out=dst_fn(c0, CH), in0=tc_[b][:], in1=ts_[b][:], op=ALU.add),
                  reads=[R_t[b]], writes=[R_dst])

    def rope_multi(st, items, rows, cosT, sinT, row_off, t0, S):
        CH = 1024
        NB = 3
        tc_ = [sb(st, U("mtc"), [rows, CH], F32) for _ in range(2)]
        ts_ = [sb(st, U("mts"), [rows, CH], F32) for _ in range(2)]
        ra = [sb(st, U("mra"), [rows, CH], BF16) for _ in range(NB)]
        rp = [sb(st, U("mrp"), [rows, CH], BF16) for _ in range(NB)]
        t1 = [sb(st, U("mt1"), [rows, CH], F32) for _ in range(NB)]
        t2 = [sb(st, U("mt2"), [rows, CH], F32) for _ in range(NB)]
        R_tab = [Res(), Res()]
        R_in = [Res() for _ in range(NB)]
        R_t1 = [Res() for _ in range(NB)]
        R_t2 = [Res() for _ in range(NB)]
        n = 0
        for ci, c0 in enumerate(range(0, S, CH)):
            tb = ci % 2
            fw.dma(SPQ, tc_[tb][:], cosT[row_off:row_off + rows, c0:c0 + CH], writes=[R_tab[tb]])
            fw.dma(SPQ, ts_[tb][:], sinT[row_off:row_off + rows, c0:c0 + CH], writes=[R_tab[tb]], join=True)
            for (na, npm, dst_fn, R_d) in items:
                b = n % NB
                n += 1
                fw.dma(SPQ, ra[b][:], zT[FMI[na], 0:rows, t0 + c0:t0 + c0 + CH], reads=[R_zT], writes=[R_in[b]])
                fw.dma(SPQ, rp[b][:], zT[FMI[npm], 0:rows, t0 + c0:t0 + c0 + CH], reads=[R_zT], writes=[R_in[b]], join=True)
                fw.op(DVE, lambda b=b, tb=tb: nc.vector.tensor_tensor(out=t1[b][:], in0=tc_[tb][:], in1=ra[b][:], op=ALU.mult),
                      reads=[R_in[b], R_tab[tb]], writes=[R_t1[b]])
                fw.op(POOL, lambda b=b, tb=tb: nc.gpsimd.tensor_tensor(out=t2[b][:], in0=ts_[tb][:], in1=rp[b][:], op=ALU.mult),
                      reads=[R_in[b], R_tab[tb]], writes=[R_t2[b]])
                fw.op(DVE, lambda b=b, c0=c0, dst_fn=dst_fn: nc.vector.tensor_tensor(
                    out=dst_fn(c0, CH), in0=t1[b][:], in1=t2[b][:], op=ALU.add), reads=[R_t1[b], R_t2[b]], writes=[R_d])

    def fnet(t0, S, sample, nseq):
        nt = S // 128
        ntq = nt // nseq
        with ExitStack() as st:
            zf = sb(st, U("zf"), [128, 2, S], BF16)
            R_zf = Res()
            for q, nm in enumerate(("zf01", "zf23")):
                fw.dma(SPQ, zf[:, q, :], zT[FMI[nm], :, t0:t0 + S], reads=[R_zT], writes=[R_zf], join=(q > 0))
            ch = sb(st, U("ch"), [128, 2, 128], BF16)
            R_ch = Res()
            fw.dma(POOL, ch[:, 0, :], E["chc"][:, :], writes=[R_ch])
            fw.dma(POOL, ch[:, 1, :], E["chs"][:, :], writes=[R_ch], join=True)
            zcs = sb(st, U("zcs"), [128, nt, 512], BF16)
            R_zcs = Res()
            for t in range(nt):
                bank = 4 + t % 2
                fns = []
                for cs in range(2):
                    for q in range(2):
                        fns.append(lambda cs=cs, q=q, t=t, bank=bank: nc.tensor.matmul(
                            PB[bank][:, cs * 256 + q * 128: cs * 256 + (q + 1) * 128], lhsT=zf[:, q, t * 128:(t + 1) * 128],
                            rhs=ch[:, cs, :], start=True, stop=True))
                fw.group(PE, fns, reads=[R_zf, R_ch], writes=[R_PB[bank]])
                fw.op(DVE, lambda t=t, bank=bank: nc.vector.tensor_copy(zcs[:, t, :], PB[bank][:, :]),
                      reads=[R_PB[bank]], writes=[R_zcs])
            tabc, tabs = (E["dftc"], E["dfts"]) if sample else (E["dftc_p"], E["dfts_p"])
            NTB = 5 if sample else 2
            tb = [sb(st, U("tb"), [128, 2, ntq * 128], BF16) for _ in range(NTB)]
            R_tb = [Res() for _ in range(NTB)]
            ost = [sb(st, U("fo"), [128, 256], BF16) for _ in range(2)]
            R_ost = [Res(), Res()]
            def ld_tab(m):
                tbi = m % NTB
                fw.dma(SPQ, tb[tbi][:, 0, :], tabc[m % ntq, :, :], writes=[R_tb[tbi]])
                fw.dma(SPQ, tb[tbi][:, 1, :], tabs[m % ntq, :, :], writes=[R_tb[tbi]], join=True)
            for m in range(min(NTB - 1, nt)):
                ld_tab(m)
            for m in range(nt):
                if m + NTB - 1 < nt:
                    ld_tab(m + NTB - 1)
                b = m % 2
                tbi = m % NTB
                bank = 4 + m % 2
                fns = []
                n_mm = 2 * ntq
                kb = (m // ntq) * ntq
                for cs in range(2):
                    for k in range(ntq):
                        idx = cs * ntq + k
                        fns.append(lambda cs=cs, k=k, tbi=tbi, bank=bank, idx=idx, kb=kb: nc.tensor.matmul(
                            PB[bank][:, 0:256], lhsT=tb[tbi][:, cs, k * 128:(k + 1) * 128],
                            rhs=zcs[:, kb + k, cs * 256:(cs + 1) * 256], start=(idx == 0), stop=(idx == n_mm - 1)))
                fw.group(PE, fns, reads=[R_tb[tbi], R_zcs], writes=[R_PB[bank]])
                fw.op(DVE, lambda b=b, bank=bank: nc.vector.tensor_copy(ost[b][:], PB[bank][:, 0:256]),
                      reads=[R_PB[bank]], writes=[R_ost[b]])
                emit_out(ost[b][:], R_ost[b], t0 + m * 128, 512)
            fw.barrier()

    def swa(t0, S, sample, nseq):
        nt = S // 128
        ntq = nt // nseq
        with ExitStack() as st:
            q_sb = sb(st, U("sq"), [64, 4, S], BF16)
            k_sb = sb(st, U("sk"), [64, 2, S], BF16)
            R_q, R_k = Res(), Res()
            if sample:
                with ExitStack() as st2:
                    items = [(f"qs{h}", f"qsp{h}", (lambda c0, n_, h=h: q_sb[:, h, c0:c0 + n_]), R_q) for h in range(4)]
                    items += [(f"ks{h}", f"ksp{h}", (lambda c0, n_, h=h: k_sb[:, h, c0:c0 + n_]), R_k) for h in range(2)]
                    rope_multi(st2, items, 64, E["cos64"], E["sin64"], 0, t0, S)
                    fw.barrier()
            else:
                for h in range(4):
                    fw.dma(SPQ, q_sb[:, h, :], zT[FMI[f"qs{h}"], 0:64, t0:t0 + S], reads=[R_zT], writes=[R_q], join=(h > 0))
                for h in range(2):
                    fw.dma(SPQ, k_sb[:, h, :], zT[FMI[f"ks{h}"], 0:64, t0:t0 + S], reads=[R_zT], writes=[R_k], join=(h > 0))
            vraw = sb(st, U("vr"), [128, nt, 128], BF16)
            vx = sb(st, U("vx"), [128, nt, 2, 65], BF16)
            R_vr, R_vx = Res(), Res()
            load_rows(vraw, ztm[t0:t0 + S, 512:640], R_ztm, R_vr, nt)
            fw.op(DVE, lambda: nc.vector.memset(vx[:], 1.0), writes=[R_vx])
            fw.op(DVE, lambda: nc.vector.tensor_copy(vx[:, :, :, 0:64], vraw[:].rearrange("p s (h d) -> p s h d", h=2)),
                  reads=[R_vr], writes=[R_vx])
            rd = [R_q, R_k, R_vx]
            if sample:
                kc_tm = sb(st, U("kctm"), [128, 2, 128], BF16)
                vc_tm = sb(st, U("vctm"), [128, 2, 128], BF16)
                kcT = sb(st, U("kcT"), [64, 2, 256], BF16)
                vcx = sb(st, U("vcx"), [128, 2, 2, 65], BF16)
                R_kc, R_vc, R_kcT, R_vcx = Res(), Res(), Res(), Res()
                fw.dma(POOL, kc_tm[:], E["c_swa_k"][l, :, :].rearrange("(s p) c -> p s c", p=128), writes=[R_kc])
                fw.dma(POOL, vc_tm[:], E["c_swa_v"][l, :, :].rearrange("(s p) c -> p s c", p=128), writes=[R_vc])
                for kvh in range(2):
                    transpose_many(lambda i0, c_, kvh=kvh: kcT[:, kvh, i0 * 128:(i0 + c_) * 128],
                                   lambda i, kvh=kvh: kc_tm[:, i, kvh * 64:(kvh + 1) * 64], 2, 64, [R_kc], R_kcT)
                fw.op(DVE, lambda: nc.vector.memset(vcx[:], 1.0), writes=[R_vcx])
                fw.op(DVE, lambda: nc.vector.tensor_copy(vcx[:, :, :, 0:64], vc_tm[:].rearrange("p s (h d) -> p s h d", h=2)),
                      reads=[R_vc], writes=[R_vcx])
                rd += [R_kcT, R_vcx]
            snk = sb(st, U("snk"), [128, 4], F32)
            R_snk = Res()
            fw.dma(SPQ, snk[:], E["swa_sink"][l, :].partition_broadcast(128), writes=[R_snk])
            fw.op(ACT, lambda: nc.scalar.activation(out=snk[:], in_=snk[:], func=AF.Exp), reads=[R_snk], writes=[R_snk])
            rd.append(R_snk)
            stream = AttStream(st)
            ost = [sb(st, U("so"), [128, 256], BF16) for _ in range(2)]
            R_ost = [Res(), Res()]
            for n in range(nt):
                b = n % 2
                for h in range(4):
                    kvh = h // 2
                    keys = []
                    if sample:
                        keys += [(kcT[:, kvh, 0:128], vcx[:, 0, kvh, :], None), (kcT[:, kvh, 128:256], vcx[:, 1, kvh, :], None)]
                        for dn, mk in ((-1, maskb), (0, None), (1, maskf)):
                            nb_ = n + dn
                            if 0 <= nb_ < nt:
                                keys.append((k_sb[:, kvh, nb_ * 128:(nb_ + 1) * 128], vx[:, nb_, kvh, :], mk))
                    else:
                        keys = [(k_sb[:, kvh, j * 128:(j + 1) * 128], vx[:, j, kvh, :], None)
                                for j in range((n // ntq) * ntq, (n // ntq + 1) * ntq)]
                    cb = (lambda b=b, n=n: emit_out(ost[b][:], R_ost[b], t0 + n * 128, 256)) if h == 3 else None
                    stream.add_block(q_sb[:, h, n * 128:(n + 1) * 128], keys, 0.125, snk[:, h:h + 1],
                                     ost[b][:, h * 64:(h + 1) * 64], R_ost[b], rd, cb)
            stream.run()
            fw.barrier()

    def mla(t0, S, sample, nseq):
        nt = S // 128
        ntq = nt // nseq
        nctx = 256 if sample else 0
        NK = nctx + S
        nkt = NK // 128
        nct = nctx // 128
        with ExitStack() as st:
            KT = sb(st, U("KT"), [96, 4, NK], BF16)
            QT = sb(st, U("QT"), [96, 4, S], BF16)
            vmx = sb(st, U("vmx"), [128, nkt, 4, 65], BF16)
            R_KT, R_QT, R_vmx = Res(), Res(), Res()
            with ExitStack() as st2:
                gkv = sb(st2, U("gkv"), [128, 128], F32)
                R_g = Res()
                fw.dma(SPQ, gkv[:], E["mla_gkv"][l, :].partition_broadcast(128), writes=[R_g])
                kva = sb(st2, U("kva"), [128, nt, 128], BF16)
                R_kva = Res()
                load_rows(kva, ztm[t0:t0 + S, 896:1024], R_ztm, R_kva, nt)
                sq = sb(st2, U("sqk"), [128, nt, 128], F32)
                ssk = sb(st2, U("ssk"), [128, nt], F32)
                R_sq, R_ss = Res(), Res()
                fw.op(POOL, lambda: nc.gpsimd.tensor_tensor(out=sq[:], in0=kva[:], in1=kva[:], op=ALU.mult), reads=[R_kva], writes=[R_sq])
                fw.op(DVE, lambda: nc.vector.reduce_sum(out=ssk[:], in_=sq[:], axis=AX.X), reads=[R_sq], writes=[R_ss])
                fw.op(ACT, lambda: nc.scalar.activation(out=ssk[:], in_=ssk[:], func=AF.Sqrt, scale=1.0 / 128, bias=epsb[:, 0:1]),
                      reads=[R_ss], writes=[R_ss])
                fw.op(DVE, lambda: nc.vector.reciprocal(ssk[:], ssk[:]), reads=[R_ss], writes=[R_ss])
                fw.op(POOL, lambda: nc.gpsimd.tensor_tensor(out=sq[:], in0=kva[:], in1=ssk[:].unsqueeze(2).to_broadcast([128, nt, 128]),
                                                            op=ALU.mult), reads=[R_kva, R_ss, R_sq], writes=[R_sq])
                fw.op(DVE, lambda: nc.vector.tensor_tensor(out=sq[:], in0=sq[:], in1=gkv[:].unsqueeze(1).to_broadcast([128, nt, 128]),
                                                           op=ALU.mult), reads=[R_sq, R_g], writes=[R_sq])
                if not sample:
                    for pi in range(nseq):
                        fw.dma(SPQ, E["o_ckv"][pi, l, :, :].rearrange("(s p) c -> p s c", p=128), sq[:, pi * ntq:(pi + 1) * ntq, :],
                               reads=[R_sq])
                ckvb = sb(st2, U("ckvb"), [128, nkt, 128], BF16)
                R_cb = Res()
                if sample:
                    fw.dma(POOL, ckvb[:, 0:2, :], E["c_ckv"][l, :, :].rearrange("(s p) c -> p s c", p=128), writes=[R_cb])
                fw.op(DVE, lambda: nc.vector.tensor_copy(ckvb[:, nct:nkt, :], sq[:]), reads=[R_sq], writes=[R_cb])
                ckvT = sb(st2, U("ckvT"), [128, NK], BF16)
                R_cT = Res()
                transpose_many(lambda i0, c_: ckvT[:, i0 * 128:(i0 + c_) * 128], lambda i: ckvb[:, i, :], nkt, 128, [R_cb], R_cT)
                wkv = sb(st2, U("wkv"), [128, 512], BF16)
                R_wkv = Res()
                fw.dma(POOL, wkv[:], E["mla_wkvb"][l, :, :], writes=[R_wkv])
                ci = 0
                for h in range(4):
                    for c0 in range(0, NK, 512):
                        n_ = min(512, NK - c0)
                        bank = 4 + ci % 2
                        ci += 1
                        fw.group(PE, [lambda h=h, c0=c0, n_=n_, bank=bank: nc.tensor.matmul(
                            PB[bank][0:64, 0:n_], lhsT=wkv[:, h * 128:h * 128 + 64], rhs=ckvT[:, c0:c0 + n_], start=True, stop=True)],
                            reads=[R_wkv, R_cT], writes=[R_PB[bank]])
                        fw.op(DVE, lambda h=h, c0=c0, n_=n_, bank=bank: nc.vector.tensor_copy(KT[0:64, h, c0:c0 + n_], PB[bank][0:64, 0:n_]),
                              reads=[R_PB[bank]], writes=[R_KT])
                fw.op(DVE, lambda: nc.vector.memset(vmx[:], 1.0), writes=[R_vmx])
                for kt in range(nkt):
                    bank = 4 + ci % 2
                    ci += 1
                    fw.group(PE, [lambda kt=kt, bank=bank: nc.tensor.matmul(
                        PB[bank][:, :], lhsT=ckvT[:, kt * 128:(kt + 1) * 128], rhs=wkv[:, :], start=True, stop=True)],
                        reads=[R_wkv, R_cT], writes=[R_PB[bank]])
                    fw.op(DVE, lambda kt=kt, bank=bank: nc.vector.tensor_copy(
                        vmx[:, kt, :, 0:64], PB[bank][:, :].rearrange("p (h c) -> p h c", h=4)[:, :, 64:128]),
                        reads=[R_PB[bank]], writes=[R_vmx])
                krs = sb(st2, U("krs"), [32, NK], BF16)
                R_krs = Res()
                if sample:
                    kr_tm = sb(st2, U("krtm"), [128, 2, 32], BF16)
                    R_krt = Res()
                    fw.dma(POOL, kr_tm[:], E["c_kr"][l, :, :].rearrange("(s p) c -> p s c", p=128), writes=[R_krt])
                    transpose_many(lambda i0, c_: krs[:, i0 * 128:(i0 + c_) * 128], lambda i: kr_tm[:, i, :], 2, 32, [R_krt], R_krs)
                    rope_rows(st2, lambda c0, n_: krs[:, nctx + c0:nctx + c0 + n_], R_krs, 32, "kr", "krp",
                              E["cos96"], E["sin96"], 64, t0, S)
                else:
                    fw.dma(SPQ, krs[:], zT[FMI["kr"], 0:32, t0:t0 + S], reads=[R_zT], writes=[R_krs])
                for h in range(4):
                    fw.dma(SPQ, KT[64:96, h, :], krs[:], reads=[R_krs], writes=[R_KT], join=True)
                fw.barrier()
            with ExitStack() as st2:
                gq = sb(st2, U("gq"), [128, 256], F32)
                R_g = Res()
                fw.dma(SPQ, gq[:], E["mla_gq"][l, :].partition_broadcast(128), writes=[R_g])
                qa = sb(st2, U("qa"), [128, nt, 256], BF16)
                R_qa = Res()
                load_rows(qa, ztm[t0:t0 + S, 640:896], R_ztm, R_qa, nt)
                sq = sb(st2, U("sqq"), [128, nt, 256], F32)
                ssq = sb(st2, U("ssq"), [128, nt], F32)
                R_sq, R_ss = Res(), Res()
                fw.op(POOL, lambda: nc.gpsimd.tensor_tensor(out=sq[:], in0=qa[:], in1=qa[:], op=ALU.mult), reads=[R_qa], writes=[R_sq])
                fw.op(DVE, lambda: nc.vector.reduce_sum(out=ssq[:], in_=sq[:], axis=AX.X), reads=[R_sq], writes=[R_ss])
                fw.op(ACT, lambda: nc.scalar.activation(out=ssq[:], in_=ssq[:], func=AF.Sqrt, scale=1.0 / 256, bias=epsb[:, 0:1]),
                      reads=[R_ss], writes=[R_ss])
                fw.op(DVE, lambda: nc.vector.reciprocal(ssq[:], ssq[:]), reads=[R_ss], writes=[R_ss])
                fw.op(POOL, lambda: nc.gpsimd.tensor_tensor(out=sq[:], in0=qa[:], in1=ssq[:].unsqueeze(2).to_broadcast([128, nt, 256]),
                                                            op=ALU.mult), reads=[R_qa, R_ss, R_sq], writes=[R_sq])
                fw.op(DVE, lambda: nc.vector.tensor_tensor(out=qa[:], in0=sq[:], in1=gq[:].unsqueeze(1).to_broadcast([128, nt, 256]),
                                                           op=ALU.mult), reads=[R_sq, R_g], writes=[R_qa])
                qnT = sb(st2, U("qnT"), [128, 2, S], BF16)
                R_qnT = Res()
                for k in range(2):
                    transpose_many(lambda i0, c_, k=k: qnT[:, k, i0 * 128:(i0 + c_) * 128],
                                   lambda i, k=k: qa[:, i, k * 128:(k + 1) * 128], nt, 128, [R_qa], R_qnT)
                wq = sb(st2, U("wq"), [128, 2, 4, 96], BF16)
                wqp = sb(st2, U("wqp"), [128, 2, 4, 96], BF16)
                R_wq = Res()
                fw.dma(POOL, wq[:], E["mla_wqb"][l, :, :, :].rearrange("(k p) h c -> p k h c", p=128), writes=[R_wq])
                fw.dma(POOL, wqp[:], E["mla_wqbp"][l, :, :, :].rearrange("(k p) h c -> p k h c", p=128), writes=[R_wq], join=True)
                tcs = [sb(st2, U("tcq"), [96, 512], F32) for _ in range(2)]
                tsn = [sb(st2, U("tsq"), [96, 512], F32) for _ in range(2)]
                t1q = [sb(st2, U("t1q"), [96, 512], F32) for _ in range(2)]
                t2q = [sb(st2, U("t2q"), [96, 512], F32) for _ in range(2)]
                R_t = [Res(), Res()]
                R_t1 = [Res(), Res()]
                R_t2 = [Res(), Res()]
                ci = 0
                for cix, c0 in enumerate(range(0, S, 512)):
                    n_ = min(512, S - c0)
                    tb = cix % 2
                    if sample:
                        fw.dma(SPQ, tcs[tb][:, 0:n_], E["cos96"][:, c0:c0 + n_], writes=[R_t[tb]])
                        fw.dma(SPQ, tsn[tb][:, 0:n_], E["sin96"][:, c0:c0 + n_], writes=[R_t[tb]], join=True)
                    for h in range(4):
                        b = ci % 2
                        ci += 1
                        fw.group(PE, [lambda k=k, h=h, c0=c0, n_=n_: nc.tensor.matmul(
                            PB[4][0:96, 0:n_], lhsT=wq[:, k, h, :], rhs=qnT[:, k, c0:c0 + n_], start=(k == 0), stop=(k == 1))
                            for k in range(2)], reads=[R_wq, R_qnT], writes=[R_PB[4]])
                        if sample:
                            fw.group(PE, [lambda k=k, h=h, c0=c0, n_=n_: nc.tensor.matmul(
                                PB[5][0:96, 0:n_], lhsT=wqp[:, k, h, :], rhs=qnT[:, k, c0:c0 + n_], start=(k == 0), stop=(k == 1))
                                for k in range(2)], reads=[R_wq, R_qnT], writes=[R_PB[5]])
                            fw.op(DVE, lambda b=b, n_=n_, tb=tb: nc.vector.tensor_tensor(
                                out=t1q[b][:, 0:n_], in0=tcs[tb][:, 0:n_], in1=PB[4][0:96, 0:n_], op=ALU.mult),
                                reads=[R_t[tb], R_PB[4]], writes=[R_t1[b]])
                            fw.op(DVE, lambda b=b, n_=n_, tb=tb: nc.vector.tensor_tensor(
                                out=t2q[b][:, 0:n_], in0=tsn[tb][:, 0:n_], in1=PB[5][0:96, 0:n_], op=ALU.mult),
                                reads=[R_t[tb], R_PB[5]], writes=[R_t2[b]])
                            fw.op(POOL, lambda b=b, n_=n_, h=h, c0=c0: nc.gpsimd.tensor_tensor(
                                out=QT[:, h, c0:c0 + n_], in0=t1q[b][:, 0:n_], in1=t2q[b][:, 0:n_], op=ALU.add),
                                reads=[R_t1[b], R_t2[b]], writes=[R_QT])
                        else:
                            fw.op(DVE, lambda h=h, c0=c0, n_=n_: nc.vector.tensor_copy(QT[:, h, c0:c0 + n_], PB[4][0:96, 0:n_]),
                                  reads=[R_PB[4]], writes=[R_QT])
                fw.barrier()
            stream = AttStream(st)
            ost = [sb(st, U("mo"), [128, 256], BF16) for _ in range(2)]
            R_ost = [Res(), Res()]
            rd = [R_KT, R_QT, R_vmx]
            sc = 96.0 ** -0.5
            for n in range(nt):
                b = n % 2
                for h in range(4):
                    krange = range(nkt) if sample else range((n // ntq) * ntq, (n // ntq + 1) * ntq)
                    keys = [(KT[:, h, kt * 128:(kt + 1) * 128], vmx[:, kt, h, :], None) for kt in krange]
                    cb = (lambda b=b, n=n: emit_out(ost[b][:], R_ost[b], t0 + n * 128, 768)) if h == 3 else None
                    stream.add_block(QT[:, h, n * 128:(n + 1) * 128], keys, sc, None, ost[b][:, h * 64:(h + 1) * 64],
                                     R_ost[b], rd, cb)
            stream.run()
            fw.barrier()

    def gla(t0, S, sample, nseq):
        nt = S // 128
        ntq = nt // nseq
        with ExitStack() as st:
            oacc = sb(st, U("oacc"), [128, nt, 256], F32)
            R_oacc = Res()
            with ExitStack() as st2:
                rmask = sb(st2, U("rmask"), [128, S], F32)
                R_rm = Res()
                fw.op(DVE, lambda: nc.vector.memset(rmask[:], 1.0), writes=[R_rm])
                fw.op(DVE, lambda: nc.vector.memset(rmask[:].rearrange("p (c t) -> p c t", t=128)[:, :, 0:1], 0.0), writes=[R_rm])
                bg = sb(st2, U("bg"), [128, 2, 2], F32)
                R_bg = Res()
                fw.dma(SPQ, bg[:], E["gla_bg_r"][:, l, :, :], writes=[R_bg])
                qT = sb(st2, U("gq"), [128, S], BF16)
                kT = sb(st2, U("gk"), [128, S], BF16)
                v = sb(st2, U("gv"), [128, nt, 256], BF16)
                aT = sb(st2, U("ga"), [16, S], BF16)
                wg = sb(st2, U("gwg"), [16, 256], BF16)
                L = sb(st2, U("gL"), [128, S], F32)
                cum = sb(st2, U("gcum"), [128, S], F32)
                Ee = sb(st2, U("gE"), [128, S], F32)
                qd = [sb(st2, U("gqd"), [128, S], BF16) for _ in range(2)]
                ki = [sb(st2, U("gki"), [128, S], BF16) for _ in range(2)]
                kie_tm = [sb(st2, U("gkt"), [128, S], BF16) for _ in range(2)]
                elast = sb(st2, U("gel"), [128, 2, nt], F32)
                S32 = sb(st2, U("gS"), [128, 2, 128], F32)
                Sbf = [sb(st2, U("gSb"), [128, 2, 128], BF16) for _ in range(2)]
                att_sb = [sb(st2, U("gat"), [128, 2, 128], BF16) for _ in range(4)]
                R_q, R_k, R_v, R_a, R_wg, R_L, R_cum, R_E, R_el, R_S = [Res() for _ in range(10)]
                R_qd, R_ki, R_kt = [Res(), Res()], [Res(), Res()], [Res(), Res()]
                R_Sb = [Res(), Res()]
                R_att = [Res() for _ in range(4)]
                R_ab = [Res() for _ in range(4)]
                R_o = [Res(), Res()]
                R_p = [Res(), Res()]
                abank = [0, 1, 4, 5]
                load_rows(v, ztm[t0:t0 + S, 0:256], R_ztm, R_v, nt)
                for dr in range(2):
                    fw.dma(SPQ, aT[:], zT[FMI["af" if dr == 0 else "ab"], 0:16, t0:t0 + S], reads=[R_zT], writes=[R_a])
                    fw.dma(POOL, wg[:], E["gla_wg"][l, dr, :, :], writes=[R_wg])
                    for hp in range(2):
                        fw.dma(SPQ, qT[:], zT[FMI["qg01" if hp == 0 else "qg23"], :, t0:t0 + S], reads=[R_zT], writes=[R_q])
                        fw.dma(SPQ, kT[:], zT[FMI["kg01" if hp == 0 else "kg23"], :, t0:t0 + S], reads=[R_zT], writes=[R_k])
                        BW = 1024 if S >= 1024 else S
                        blocks = [(c0, BW) for c0 in range(0, S, BW)]
                        R_Lb = [Res() for _ in blocks]
                        R_cmb = [Res() for _ in blocks]
                        R_Eb = [Res() for _ in blocks]
                        cum3 = cum[:].rearrange("p (c t) -> p c t", t=128)
                        L3 = L[:].rearrange("p (c t) -> p c t", t=128)
                        mi = 0
                        for bi, (c0, bw) in enumerate(blocks):
                            for cc in range(c0, c0 + bw, 512):
                                n_ = min(512, c0 + bw - cc)
                                bank = 4 + mi % 2
                                mi += 1
                                fw.group(PE, [lambda cc=cc, n_=n_, bank=bank, hp=hp: nc.tensor.matmul(
                                    PB[bank][:, 0:n_], lhsT=wg[:, hp * 128:(hp + 1) * 128], rhs=aT[:, cc:cc + n_], start=True, stop=True)],
                                    reads=[R_wg, R_a], writes=[R_PB[bank]])
                                fw.op(DVE, lambda cc=cc, n_=n_, bank=bank, dr=dr, hp=hp: nc.vector.tensor_scalar(
                                    out=L[:, cc:cc + n_], in0=PB[bank][:, 0:n_], scalar1=bg[:, dr, hp:hp + 1], scalar2=-1.0,
                                    op0=ALU.add, op1=ALU.mult), reads=[R_PB[bank], R_bg], writes=[R_Lb[bi]])
                        for bi, (c0, bw) in enumerate(blocks):
                            fw.op(ACT, lambda c0=c0, bw=bw: nc.scalar.activation(out=L[:, c0:c0 + bw], in_=L[:, c0:c0 + bw], func=AF.Exp),
                                  reads=[R_Lb[bi]], writes=[R_Lb[bi]])
                        for bi, (c0, bw) in enumerate(blocks):
                            fw.op(DVE, lambda c0=c0, bw=bw: nc.vector.tensor_scalar_add(L[:, c0:c0 + bw], L[:, c0:c0 + bw], 1.0),
                                  reads=[R_Lb[bi]], writes=[R_Lb[bi]])
                        for bi, (c0, bw) in enumerate(blocks):
                            fw.op(ACT, lambda c0=c0, bw=bw: nc.scalar.activation(out=L[:, c0:c0 + bw], in_=L[:, c0:c0 + bw], func=AF.Ln),
                                  reads=[R_Lb[bi]], writes=[R_Lb[bi]])
                        for bi, (c0, bw) in enumerate(blocks):
                            fw.op(DVE, lambda c0=c0, bw=bw: nc.vector.tensor_tensor_scan(
                                out=cum[:, c0:c0 + bw], data0=rmask[:, c0:c0 + bw], data1=L[:, c0:c0 + bw], initial=0.0,
                                op0=ALU.mult, op1=ALU.add), reads=[R_Lb[bi], R_rm], writes=[R_cmb[bi]])
                        for bi, (c0, bw) in enumerate(blocks):
                            k0, k1 = c0 // 128, (c0 + bw) // 128
                            fw.op(ACT, lambda k0=k0, k1=k1, hp=hp: nc.scalar.activation(
                                out=elast[:, hp, k0:k1], in_=cum3[:, k0:k1, 127], func=AF.Exp, scale=-1.0 / 16),
                                reads=[R_cmb[bi]], writes=[R_el])
                        if dr == 0:
                            csrc, R_cs = cum, R_cmb
                        else:
                            for bi, (c0, bw) in enumerate(blocks):
                                k0, k1 = c0 // 128, (c0 + bw) // 128
                                fw.op(DVE, lambda c0=c0, bw=bw: nc.vector.tensor_tensor(
                                    out=L[:, c0:c0 + bw], in0=L[:, c0:c0 + bw], in1=cum[:, c0:c0 + bw], op=ALU.subtract),
                                    reads=[R_Lb[bi], R_cmb[bi]], writes=[R_Lb[bi]])
                                fw.op(DVE, lambda k0=k0, k1=k1: nc.vector.tensor_tensor(
                                    out=L3[:, k0:k1, :], in0=L3[:, k0:k1, :],
                                    in1=cum3[:, k0:k1, 127:128].to_broadcast([128, k1 - k0, 128]), op=ALU.add),
                                    reads=[R_Lb[bi], R_cmb[bi]], writes=[R_Lb[bi]])
                            csrc, R_cs = L, R_Lb
                        for bi, (c0, bw) in enumerate(blocks):
                            fw.op(ACT, lambda c0=c0, bw=bw, csrc=csrc: nc.scalar.activation(
                                out=Ee[:, c0:c0 + bw], in_=csrc[:, c0:c0 + bw], func=AF.Exp, scale=-1.0 / 16),
                                reads=[R_cs[bi]], writes=[R_Eb[bi]])
                        for bi, (c0, bw) in enumerate(blocks):
                            fw.op(POOL, lambda c0=c0, bw=bw, hp=hp: nc.gpsimd.tensor_tensor(
                                out=qd[hp][:, c0:c0 + bw], in0=qT[:, c0:c0 + bw], in1=Ee[:, c0:c0 + bw], op=ALU.mult),
                                reads=[R_q, R_Eb[bi]], writes=[R_qd[hp]])
                        for bi, (c0, bw) in enumerate(blocks):
                            fw.op(ACT, lambda c0=c0, bw=bw, csrc=csrc: nc.scalar.activation(
                                out=Ee[:, c0:c0 + bw], in_=csrc[:, c0:c0 + bw], func=AF.Exp, scale=1.0 / 16),
                                reads=[R_cs[bi]], writes=[R_Eb[bi]])
                        for bi, (c0, bw) in enumerate(blocks):
                            fw.op(POOL, lambda c0=c0, bw=bw, hp=hp: nc.gpsimd.tensor_tensor(
                                out=ki[hp][:, c0:c0 + bw], in0=kT[:, c0:c0 + bw], in1=Ee[:, c0:c0 + bw], op=ALU.mult),
                                reads=[R_k, R_Eb[bi]], writes=[R_ki[hp]])
                        kT3 = kT[:].rearrange("p (c t) -> p c t", t=128)
                        ki3 = ki[hp][:].rearrange("p (c t) -> p c t", t=128)
                        for bi, (c0, bw) in enumerate(blocks):
                            k0, k1 = c0 // 128, (c0 + bw) // 128
                            fw.op(POOL, lambda k0=k0, k1=k1, hp=hp, kT3=kT3, ki3=ki3: nc.gpsimd.tensor_tensor(
                                out=kT3[:, k0:k1, :], in0=ki3[:, k0:k1, :],
                                in1=elast[:, hp, k0:k1].unsqueeze(2).to_broadcast([128, k1 - k0, 128]), op=ALU.mult),
                                reads=[R_ki[hp], R_el], writes=[R_k])
                        transpose_many(lambda i0, c_, hp=hp: kie_tm[hp][:, i0 * 128:(i0 + c_) * 128],
                                       lambda i: kT[:, i * 128:(i + 1) * 128], nt, 128, [R_k], R_kt[hp])
                    fw.op(DVE, lambda: nc.vector.memset(S32[:], 0.0), writes=[R_S])
                    if sample:
                        for hp in range(2):
                            for hh in range(2):
                                fw.dma(SPQ, S32[hh * 64:(hh + 1) * 64, hp, hh * 64:(hh + 1) * 64],
                                       E["st_gla"][l, dr, 2 * hp + hh, :, :], writes=[R_S], join=(hp + hh > 0))
                    fw.op(DVE, lambda: nc.vector.tensor_copy(Sbf[0][:], S32[:]), reads=[R_S], writes=[R_Sb[0]])
                    mk = maskf if dr == 0 else maskb
                    order = list(range(nt)) if dr == 0 else list(range(nt - 1, -1, -1))

                    obank = [2, 4]
                    pbank = [3, 5]

                    def front(step):
                        c = order[step]
                        cols = slice(c * 128, (c + 1) * 128)
                        ph = step % 2
                        for hh in range(2):
                            fns = [lambda hh=hh, hp=hp: nc.tensor.matmul(
                                PB[hh][:, hp * 128:(hp + 1) * 128], lhsT=ki[hp][hh * 64:(hh + 1) * 64, cols],
                                rhs=qd[hp][hh * 64:(hh + 1) * 64, cols], start=True, stop=True) for hp in range(2)]
                            fw.group(PE, fns, reads=[R_ki[0], R_ki[1], R_qd[0], R_qd[1]], writes=[R_PB[hh]])
                        for hh in range(2):
                            fw.op(DVE, lambda hh=hh: nc.vector.tensor_tensor(
                                out=att_sb[ph * 2 + hh][:], in0=PB[hh][:, 0:256].rearrange("p (h t) -> p h t", h=2),
                                in1=mk[:].unsqueeze(1).to_broadcast([128, 2, 128]), op=ALU.mult),
                                reads=[R_PB[hh], R_const], writes=[R_att[ph * 2 + hh]])
                        pb_ = pbank[ph]
                        fns = [lambda hp=hp: nc.tensor.matmul(
                            PB[pb_][:, hp * 128:(hp + 1) * 128], lhsT=kie_tm[hp][:, cols],
                            rhs=v[:, c, hp * 128:(hp + 1) * 128], start=True, stop=True) for hp in range(2)]
                        fw.group(PE, fns, reads=[R_kt[0], R_kt[1], R_v], writes=[R_PB[pb_]])

                    def back(step):
                        c = order[step]
                        cols = slice(c * 128, (c + 1) * 128)
                        ph = step % 2
                        ob_, pb_ = obank[ph], pbank[ph]
                        fns = []
                        for hp in range(2):
                            for hh in range(2):
                                hb = hh * 64
                                oc = hp * 128 + hh * 64
                                fns.append(lambda hp=hp, hh=hh, oc=oc: nc.tensor.matmul(
                                    PB[ob_][:, oc:oc + 64], lhsT=att_sb[ph * 2 + hh][:, hp, :],
                                    rhs=v[:, c, hp * 128 + hh * 64: hp * 128 + (hh + 1) * 64], start=True, stop=False))
                                fns.append(lambda hp=hp, hh=hh, hb=hb, oc=oc: nc.tensor.matmul(
                                    PB[ob_][:, oc:oc + 64], lhsT=qd[hp][hb:hb + 64, cols], rhs=Sbf[ph][hb:hb + 64, hp, hh * 64:(hh + 1) * 64],
                                    start=False, stop=True))
                        fw.group(PE, fns, reads=[R_att[ph * 2], R_att[ph * 2 + 1], R_v, R_qd[0], R_qd[1], R_Sb[ph]], writes=[R_PB[ob_]])
                        for hp in range(2):
                            fw.op(DVE, lambda hp=hp: nc.vector.scalar_tensor_tensor(
                                out=S32[:, hp, :], in0=S32[:, hp, :], scalar=elast[:, hp, c:c + 1],
                                in1=PB[pb_][:, hp * 128:(hp + 1) * 128], op0=ALU.mult, op1=ALU.add),
                                reads=[R_S, R_el, R_PB[pb_]], writes=[R_S])
                        fw.op(POOL, lambda: nc.gpsimd.tensor_copy(Sbf[1 - ph][:], S32[:]), reads=[R_S], writes=[R_Sb[1 - ph]])
                        if dr == 0:
                            fw.op(DVE, lambda: nc.vector.tensor_copy(oacc[:, c, :], PB[ob_][:, 0:256]),
                                  reads=[R_PB[ob_]], writes=[R_oacc])
                        else:
                            fw.op(DVE, lambda: nc.vector.tensor_tensor(out=oacc[:, c, :], in0=oacc[:, c, :],
                                                                       in1=PB[ob_][:, 0:256], op=ALU.add),
                                  reads=[R_PB[ob_], R_oacc], writes=[R_oacc])

                    front(0)
                    for step in range(nt):
                        if step + 1 < nt:
                            front(step + 1)
                        back(step)
                        c = order[step]
                        boundary = (step == nt - 1) or (order[step + 1] // ntq != c // ntq)
                        if boundary and not sample:
                            pi = c // ntq
                            for hp in range(2):
                                for hh in range(2):
                                    fw.dma(SPQ, E["o_gla"][pi, l, dr, 2 * hp + hh, :, :],
                                           S32[hh * 64:(hh + 1) * 64, hp, hh * 64:(hh + 1) * 64], reads=[R_S])
                            if step < nt - 1:
                                fw.op(DVE, lambda: nc.vector.memset(S32[:], 0.0), writes=[R_S])
                                nph = (step + 1) % 2
                                fw.op(DVE, lambda nph=nph: nc.vector.memset(Sbf[nph][:], 0.0), writes=[R_Sb[nph]])
                fw.barrier()
            with ExitStack() as st2:
                r = sb(st2, U("gr"), [128, nt, 256], BF16)
                sqo = sb(st2, U("gsq"), [128, nt, 256], F32)
                ssg = sb(st2, U("gss"), [128, nt * 4], F32)
                gout = sb(st2, U("ggo"), [128, 64], F32)
                y = sb(st2, U("gy"), [128, nt, 256], BF16)
                R_r, R_sq, R_ss, R_go, R_y = Res(), Res(), Res(), Res(), Res()
                load_rows(r, ztm[t0:t0 + S, 256:512], R_ztm, R_r, nt)
                fw.dma(SPQ, gout[:], E["gla_gout"][l, :].partition_broadcast(128), writes=[R_go])
                fw.op(DVE, lambda: nc.vector.tensor_tensor(out=sqo[:], in0=oacc[:], in1=oacc[:], op=ALU.mult), reads=[R_oacc], writes=[R_sq])
                fw.op(DVE, lambda: nc.vector.reduce_sum(out=ssg[:], in_=sqo[:].rearrange("p s (h d) -> p (s h) d", h=4), axis=AX.X),
                      reads=[R_sq], writes=[R_ss])
                fw.op(ACT, lambda: nc.scalar.activation(out=ssg[:], in_=ssg[:], func=AF.Sqrt, scale=1.0 / 64, bias=epsb[:, 0:1]),
                      reads=[R_ss], writes=[R_ss])
                fw.op(DVE, lambda: nc.vector.reciprocal(ssg[:], ssg[:]), reads=[R_ss], writes=[R_ss])
                o4 = oacc[:].rearrange("p s (h d) -> p (s h) d", h=4)
                fw.op(DVE, lambda: nc.vector.tensor_tensor(out=o4, in0=o4, in1=ssg[:].unsqueeze(2).to_broadcast([128, nt * 4, 64]),
                                                           op=ALU.mult), reads=[R_oacc, R_ss], writes=[R_oacc])
                fw.op(DVE, lambda: nc.vector.tensor_tensor(out=o4, in0=o4, in1=gout[:].unsqueeze(1).to_broadcast([128, nt * 4, 64]),
                                                           op=ALU.mult), reads=[R_oacc, R_go], writes=[R_oacc])
                fw.op(ACT, lambda: nc.scalar.activation(out=sqo[:], in_=r[:], func=AF.Silu), reads=[R_r, R_ss], writes=[R_sq])
                fw.op(DVE, lambda: nc.vector.tensor_tensor(out=y[:], in0=oacc[:], in1=sqo[:], op=ALU.mult),
                      reads=[R_oacc, R_sq], writes=[R_y])
                for s0 in range(0, nt, 8):
                    n_ = min(8, nt - s0)
                    r0 = t0 + s0 * 128
                    fw.dma(SPQ, mixo[r0:r0 + n_ * 128, 0:256].rearrange("(s p) c -> p s c", p=128), y[:, s0:s0 + n_, :],
                           reads=[R_y], writes=[R_mixo], join=True)
                    if dbg_mix is not None and l == 0:
                        fw.dma(POOL, dbg_mix[r0:r0 + n_ * 128, 0:256].rearrange("(s p) c -> p s c", p=128), y[:, s0:s0 + n_, :],
                               reads=[R_y])
                fw.barrier()

    for (t0, S, sample, nseq) in seqs:
        if "f" in which:
            fnet(t0, S, sample, nseq)
        if "s" in which:
            swa(t0, S, sample, nseq)
        if "m" in which:
            mla(t0, S, sample, nseq)
        if "g" in which:
            gla(t0, S, sample, nseq)


_CACHE = {}


def _consts():
    if "c" in _CACHE:
        return _CACHE["c"]
    c = {}
    c["ident"] = np.eye(128, dtype=np.float32)
    j = np.arange(128)[:, None]
    i = np.arange(128)[None, :]
    c["mask_f"] = (j <= i).astype(np.float32)
    c["mask_b"] = (j >= i).astype(np.float32)
    c64, s64 = _rope_tables(64)
    c["cos64"], c["sin64"] = c64, s64
    c32, s32 = _rope_tables(32)
    c["cos96"] = np.concatenate([np.ones((64, SS), np.float32), c32], 0)
    c["sin96"] = np.concatenate([np.zeros((64, SS), np.float32), s32], 0)

    def dft(S, scale):
        s = np.arange(S, dtype=np.int64)
        prod = (s[:, None] * s[None, :]) % S
        ang = 2.0 * np.pi * prod.astype(np.float64) / S
        nb = S // 128
        C = (np.cos(ang) * scale).astype(np.float32)
        Sn = (np.sin(ang) * scale).astype(np.float32)

        def lay(M):
            return np.ascontiguousarray(M.reshape(nb, 128, nb, 128).transpose(2, 1, 0, 3)).reshape(
                nb, 128, nb * 128).astype(ml_dtypes.bfloat16)
        return lay(C), lay(Sn)
    c["dftc"], c["dfts"] = dft(SS, 1.0 / 64)
    c["dftc_p"], c["dfts_p"] = dft(SP, 1.0 / 16)
    cc = np.arange(64)
    angc = 2.0 * np.pi * ((cc[:, None] * cc[None, :]) % 64) / 64
    Cc = (np.cos(angc) / 8).astype(np.float32)
    Sc = (-np.sin(angc) / 8).astype(np.float32)
    z = np.zeros((64, 64), np.float32)
    c["chc"] = np.block([[Cc, z], [z, Cc]])
    c["chs"] = np.block([[Sc, z], [z, Sc]])
    _CACHE["c"] = c
    return c


def _prep_shared(inp):
    f = lambda a: np.ascontiguousarray(np.asarray(a, dtype=np.float32))
    sh = {}
    sh["w_mod"] = f(inp["w_mod"])
    sh["b_mod"] = f(inp["b_mod"])
    sh["bmod_r"] = f(np.asarray(inp["b_mod"]).reshape(NL, 9, 8, 128).transpose(3, 0, 1, 2))
    sh["g_norm"] = f(inp["g_norm"])
    sh["gnorm_r"] = f(np.asarray(inp["g_norm"]).reshape(NL, 6, 8, 128).transpose(3, 0, 1, 2))
    for nm, key in (("w_gate", "w_ffn_gate"), ("w_up", "w_ffn_up")):
        w = np.asarray(inp[key]).reshape(NL, 2, 8, 128, NJ, 128)
        sh[nm] = f(w.transpose(0, 1, 4, 3, 2, 5).reshape(NL, 2, NJ, 128, 8 * 128))
    sh["w_down"] = f(inp["w_ffn_down"])
    w_in = np.asarray(inp["w_in"])
    p64, _ = _partner(64)
    p32, _ = _partner(32)
    cols = {}
    cols["qg01"] = np.arange(O_QG, O_QG + 128); cols["qg23"] = np.arange(O_QG + 128, O_QG + 256)
    cols["kg01"] = np.arange(O_KG, O_KG + 128); cols["kg23"] = np.arange(O_KG + 128, O_KG + 256)
    cols["af"] = np.arange(O_AF, O_AF + 16); cols["ab"] = np.arange(O_AB, O_AB + 16)
    for h in range(4):
        cols[f"qs{h}"] = O_QS + h * 64 + np.arange(64)
        cols[f"qsp{h}"] = O_QS + h * 64 + p64
    for h in range(2):
        cols[f"ks{h}"] = O_KS + h * 64 + np.arange(64)
        cols[f"ksp{h}"] = O_KS + h * 64 + p64
    cols["zf01"] = np.arange(O_ZF, O_ZF + 128); cols["zf23"] = np.arange(O_ZF + 128, O_ZF + 256)
    cols["kr"] = O_KR + np.arange(32); cols["krp"] = O_KR + p32
    blocks = []
    for nm, m in FM:
        blk = w_in[:, :, cols[nm]].reshape(NL, 8, 128, m).transpose(0, 2, 1, 3).reshape(NL, 128, 8 * m)
        blocks.append(blk)
    sh["w_fm"] = f(np.concatenate(blocks, axis=2))
    tmcols = np.concatenate([np.arange(O_VG, O_VG + 256), np.arange(O_RG, O_RG + 256), np.arange(O_VS, O_VS + 128),
                             np.arange(O_QA, O_QA + 256), np.arange(O_KVA, O_KVA + 128),
                             np.arange(O_KS, O_KS + 128), np.arange(O_KR, O_KR + 32)])
    sh["w_tm"] = f(w_in[:, :, tmcols])
    sh["w_out"] = f(inp["w_out"])
    sh["gla_wg"] = f(inp["gla_w_gate"])
    sh["gla_bg_r"] = f(np.asarray(inp["gla_b_gate"]).reshape(NL, 2, 2, 128).transpose(3, 0, 1, 2))
    sh["gla_gout"] = f(inp["gla_g_out"])
    sh["swa_sink"] = f(inp["swa_sink"])
    sh["mla_gq"] = f(inp["mla_g_q"])
    sh["mla_gkv"] = f(inp["mla_g_kv"])
    wq = np.asarray(inp["mla_w_q_b"]).reshape(NL, 256, 4, 96)
    sh["mla_wqb"] = f(wq)
    permq = np.concatenate([np.arange(64), 64 + p32])
    sh["mla_wqbp"] = f(wq[:, :, :, permq])
    sh["mla_wkvb"] = f(inp["mla_w_kv_b"])
    sh.update(_consts())
    return sh


def _make_in_maps(inp, n=8):
    shared = _prep_shared(inp)
    xs_ = np.asarray(inp["x_sample"], dtype=np.float32)
    xp_ = np.asarray(inp["x_prompt"], dtype=np.float32)
    in_maps = []
    for c in range(n):
        b = c // 2
        m = dict(shared)
        m["xin"] = np.ascontiguousarray(np.concatenate([xs_[b], xp_[2 * c:2 * c + 2].reshape(2 * SP, D)], axis=0))
        cond = np.stack([np.asarray(inp["c"])[b], np.asarray(inp["c_ctx"])], axis=0).astype(np.float32)
        m["condT"] = np.ascontiguousarray(cond.reshape(2, 8, 128).transpose(2, 1, 0))
        m["st_gla"] = np.ascontiguousarray(np.asarray(inp["state_gla"], dtype=np.float32)[b])
        m["c_swa_k"] = np.ascontiguousarray(np.asarray(inp["cache_swa_k"], dtype=np.float32)[b].reshape(NL, 256, 128))
        m["c_swa_v"] = np.ascontiguousarray(np.asarray(inp["cache_swa_v"], dtype=np.float32)[b].reshape(NL, 256, 128))
        m["c_ckv"] = np.ascontiguousarray(np.asarray(inp["cache_mla_ckv"], dtype=np.float32)[b])
        m["c_kr"] = np.ascontiguousarray(np.asarray(inp["cache_mla_krope"], dtype=np.float32)[b])
        in_maps.append(m)
    return in_maps


def kernel(**inp):
    n = 8
    in_maps = _make_in_maps(inp, n)
    nc, fw = build_program()
    res = run_bass_kernel_spmd(nc, in_maps, core_ids=list(range(n)))
    r = res.results
    y_prompt = np.concatenate([r[c]["y_out"][SS:].reshape(2, SP, D) for c in range(n)], axis=0)
    y_sample = np.stack([r[2 * b]["y_out"][:SS] for b in range(4)], axis=0)
    st = np.concatenate([r[c]["o_gla"] for c in range(n)], axis=0)
    ok = np.concatenate([r[c]["o_k"] for c in range(n)], axis=0).reshape(16, NL, SP, 2, 64)
    ov = np.concatenate([r[c]["o_v"] for c in range(n)], axis=0).reshape(16, NL, SP, 2, 64)
    ockv = np.concatenate([r[c]["o_ckv"] for c in range(n)], axis=0)
    okr = np.concatenate([r[c]["o_kr"] for c in range(n)], axis=0)
    return (y_prompt.astype(np.float32), y_sample.astype(np.float32), st.astype(np.float32), ok.astype(np.float32),
            ov.astype(np.float32), ockv.astype(np.float32), okr.astype(np.float32))
```
